# Optimizing a Trainium2 kernel written in Bass

```python
import math
import jax, jax.numpy as jnp
from jax import lax
import numpy as np

D_MODEL = 1024
BATCH = 4
SEQ = 8192
DEPTH = 2

N_MIXERS = 2
N_MLSTM_LAYERS = (DEPTH + 1) // 2
N_CONV_LAYERS = DEPTH // 2

M_HEADS = 4
M_DV = D_MODEL // M_HEADS
M_DQK = M_DV // 2
M_QK_WIDTH = 2 * M_HEADS * M_DQK
M_V_WIDTH = M_HEADS * M_DV
M_IN_WIDTH = M_QK_WIDTH + M_V_WIDTH + M_V_WIDTH + 2 * M_HEADS
M_CONV_K = 4
M_CHUNK = 64
FGATE_BIAS = 3.0

C_KERNEL = 31

D_FF = 2816
F_CONV_K = 3

EPS = 1e-6

kernel_name = "hybrid_mlstm_conformer_convffn"


def rmsnorm(x, g):
    xf = x.astype(jnp.float32)
    y = xf * lax.rsqrt(jnp.mean(xf * xf, axis=-1, keepdims=True) + EPS)
    return (y * g.astype(jnp.float32)).astype(x.dtype)


def layernorm(x, g, b):
    xf = x.astype(jnp.float32)
    mu = jnp.mean(xf, axis=-1, keepdims=True)
    var = jnp.mean(jnp.square(xf - mu), axis=-1, keepdims=True)
    y = (xf - mu) * lax.rsqrt(var + EPS)
    return (y * g.astype(jnp.float32) + b.astype(jnp.float32)).astype(x.dtype)


def causal_dwconv(x, w, b):
    k, c = w.shape
    y = lax.conv_general_dilated(
        x, w[:, None, :].astype(x.dtype), window_strides=(1,), padding=[(k - 1, 0)],
        dimension_numbers=("NWC", "WIO", "NWC"), feature_group_count=c)
    return y + b.astype(x.dtype)


def mlstm_chunkwise(q, k, v, logi, logf):
    bsz, nh, t, dk = q.shape
    dv = v.shape[-1]
    nc = t // M_CHUNK

    def chunks(a):
        a = a.reshape(a.shape[:2] + (nc, M_CHUNK) + a.shape[3:])
        return jnp.moveaxis(a, 2, 0)

    qc, kc, vc = chunks(q), chunks(k), chunks(v)
    lic = chunks(logi)
    bc = jnp.cumsum(chunks(logf), axis=-1)
    causal = jnp.tril(jnp.ones((M_CHUNK, M_CHUNK), dtype=bool))

    def step(carry, inp):
        c_st, n_st, m_st = carry
        qb, kb, vb, li, bb = inp
        d = bb[..., :, None] - bb[..., None, :] + li[..., None, :]
        d = jnp.where(causal, d, -jnp.inf)
        inter = bb + m_st[..., None]
        m_t = jnp.maximum(inter, jnp.max(d, axis=-1))
        s = jnp.einsum("bhtd,bhsd->bhts", qb, kb) * jnp.exp(d - m_t[..., None])
        sc = jnp.exp(inter - m_t)
        num = jnp.einsum("bhts,bhsv->bhtv", s, vb) + sc[..., None] * jnp.einsum("bhvd,bhtd->bhtv", c_st, qb)
        den = jnp.sum(s, axis=-1) + sc * jnp.einsum("bhd,bhtd->bht", n_st, qb)
        h = num / jnp.maximum(jnp.abs(den), jnp.exp(-m_t))[..., None]
        b_last = bb[..., -1]
        w_log = b_last[..., None] - bb + li
        m_new = jnp.maximum(b_last + m_st, jnp.max(w_log, axis=-1))
        decay = jnp.exp(b_last + m_st - m_new)
        ws = jnp.exp(w_log - m_new[..., None])
        c_new = decay[..., None, None] * c_st + jnp.einsum("bhs,bhsv,bhsd->bhvd", ws, vb, kb)
        n_new = decay[..., None] * n_st + jnp.einsum("bhs,bhsd->bhd", ws, kb)
        return (c_new, n_new, m_new), h

    init = (jnp.zeros((bsz, nh, dv, dk), jnp.float32),
            jnp.zeros((bsz, nh, dk), jnp.float32),
            jnp.zeros((bsz, nh), jnp.float32))
    _, hs = lax.scan(step, init, (qc, kc, vc, lic, bc))
    return jnp.moveaxis(hs, 0, 2).reshape(bsz, nh, t, dv)


def mlstm_mixer(x, w_in, conv_w, conv_b, b_gates, head_norm, w_out):
    bsz, t, _ = x.shape
    proj = x @ w_in
    qk = jax.nn.silu(causal_dwconv(proj[..., :M_QK_WIDTH], conv_w, conv_b))
    v = proj[..., M_QK_WIDTH:M_QK_WIDTH + M_V_WIDTH]
    o = proj[..., M_QK_WIDTH + M_V_WIDTH:M_QK_WIDTH + 2 * M_V_WIDTH]
    gates = (proj[..., M_QK_WIDTH + 2 * M_V_WIDTH:] + b_gates).astype(jnp.float32)
    logi = jnp.transpose(gates[..., :M_HEADS], (0, 2, 1))
    logf = jnp.transpose(jax.nn.log_sigmoid(gates[..., M_HEADS:]), (0, 2, 1))

    def heads(a, d):
        return jnp.transpose(a.reshape(bsz, t, M_HEADS, d), (0, 2, 1, 3)).astype(jnp.float32)

    q = heads(qk[..., :M_HEADS * M_DQK], M_DQK) * (M_DQK ** -0.5)
    k = heads(qk[..., M_HEADS * M_DQK:], M_DQK)
    h = mlstm_chunkwise(q, k, heads(v, M_DV), logi, logf)
    h = jnp.transpose(h, (0, 2, 1, 3))
    h = h * lax.rsqrt(jnp.mean(h * h, axis=-1, keepdims=True) + EPS)
    h = h.reshape(bsz, t, M_V_WIDTH) * head_norm.astype(jnp.float32)
    h = h.astype(x.dtype) * jax.nn.sigmoid(o)
    return h @ w_out


def conformer_conv_mixer(x, w_in, b_in, dw_w, dw_b, ln_g, ln_b, w_out, b_out):
    a = x @ w_in + b_in
    u = a[..., :D_MODEL] * jax.nn.sigmoid(a[..., D_MODEL:])
    u = causal_dwconv(u, dw_w, dw_b)
    u = jax.nn.silu(layernorm(u, ln_g, ln_b))
    return u @ w_out + b_out


def conv_ffn(x, w_up, conv_w, conv_b, w_down):
    hdn = causal_dwconv(x @ w_up, conv_w, conv_b)
    return (jax.nn.silu(hdn[..., :D_FF]) * hdn[..., D_FF:]) @ w_down


def setup_inputs(seed: int = 0) -> dict:
    key = jax.random.key(seed)
    ks = iter(jax.random.split(key, 32))
    f32 = jnp.float32
    nrm = lambda shape, s: jax.random.normal(next(ks), shape, f32) * s
    out_scale = 1.0 / math.sqrt(2 * DEPTH)
    nm, ncv = N_MLSTM_LAYERS, N_CONV_LAYERS
    gate_bias = jnp.concatenate([jnp.zeros((nm, M_HEADS), f32), jnp.full((nm, M_HEADS), FGATE_BIAS, f32)], axis=-1)
    return {
        "x": nrm((BATCH, SEQ, D_MODEL), 1.0),
        "norm_mix": 1.0 + nrm((DEPTH, D_MODEL), 0.02),
        "norm_ffn": 1.0 + nrm((DEPTH, D_MODEL), 0.02),
        "norm_final": 1.0 + nrm((D_MODEL,), 0.02),
        "m_w_in": nrm((nm, D_MODEL, M_IN_WIDTH), D_MODEL ** -0.5),
        "m_conv_w": nrm((nm, M_CONV_K, M_QK_WIDTH), M_CONV_K ** -0.5),
        "m_conv_b": nrm((nm, M_QK_WIDTH), 0.02),
        "m_b_gates": gate_bias + nrm((nm, 2 * M_HEADS), 0.1),
        "m_head_norm": 1.0 + nrm((nm, M_V_WIDTH), 0.02),
        "m_w_out": nrm((nm, M_V_WIDTH, D_MODEL), M_V_WIDTH ** -0.5 * out_scale),
        "c_w_in": nrm((ncv, D_MODEL, 2 * D_MODEL), D_MODEL ** -0.5),
        "c_b_in": nrm((ncv, 2 * D_MODEL), 0.02),
        "c_dw_w": nrm((ncv, C_KERNEL, D_MODEL), C_KERNEL ** -0.5),
        "c_dw_b": nrm((ncv, D_MODEL), 0.02),
        "c_ln_g": 1.0 + nrm((ncv, D_MODEL), 0.02),
        "c_ln_b": nrm((ncv, D_MODEL), 0.02),
        "c_w_out": nrm((ncv, D_MODEL, D_MODEL), D_MODEL ** -0.5 * out_scale),
        "c_b_out": nrm((ncv, D_MODEL), 0.02),
        "f_w_up": nrm((DEPTH, D_MODEL, 2 * D_FF), D_MODEL ** -0.5),
        "f_conv_w": nrm((DEPTH, F_CONV_K, 2 * D_FF), F_CONV_K ** -0.5),
        "f_conv_b": nrm((DEPTH, 2 * D_FF), 0.02),
        "f_w_down": nrm((DEPTH, D_FF, D_MODEL), D_FF ** -0.5 * out_scale),
    }


def reference(x, norm_mix, norm_ffn, norm_final,
              m_w_in, m_conv_w, m_conv_b, m_b_gates, m_head_norm, m_w_out,
              c_w_in, c_b_in, c_dw_w, c_dw_b, c_ln_g, c_ln_b, c_w_out, c_b_out,
              f_w_up, f_conv_w, f_conv_b, f_w_down):
    for i in range(DEPTH):
        h = rmsnorm(x, norm_mix[i])
        j = i // N_MIXERS
        if i % N_MIXERS == 0:
            x = x + mlstm_mixer(h, m_w_in[j], m_conv_w[j], m_conv_b[j], m_b_gates[j],
                                m_head_norm[j], m_w_out[j])
        else:
            x = x + conformer_conv_mixer(h, c_w_in[j], c_b_in[j], c_dw_w[j], c_dw_b[j],
                                         c_ln_g[j], c_ln_b[j], c_w_out[j], c_b_out[j])
        x = x + conv_ffn(rmsnorm(x, norm_ffn[i]), f_w_up[i], f_conv_w[i], f_conv_b[i], f_w_down[i])
    return rmsnorm(x, norm_final)
```

```python
import numpy as np
from contextlib import ExitStack
import concourse.bass as bass
import concourse.mybir as mybir
from concourse.bass_utils import run_bass_kernel_spmd

F32 = mybir.dt.float32
BF16 = mybir.dt.bfloat16
ALU = mybir.AluOpType
AF = mybir.ActivationFunctionType

D = 1024
SEQ = 8192
BATCH = 4
NCORES = 8
T = 4096
HALO = 132
HV = 128
DFF = 2816
NJ = DFF // 128
EPS = 1e-6
CK = 31
KD = 8


class Buf:
    __slots__ = ("name", "last_w", "readers", "dma_readers", "excl")

    def __init__(self, name, excl=False):
        self.name = name
        self.last_w = None
        self.readers = []
        self.dma_readers = []
        self.excl = excl


class Op:
    __slots__ = ("eng", "fn", "deps", "odeps", "signal", "sig_idx", "is_dma", "dma_sem", "dma_val",
                 "idx", "batch", "nun", "succ", "ready", "finish")

    def __init__(self, eng, fn, is_dma=False):
        self.eng = eng
        self.fn = fn
        self.deps = []
        self.odeps = []
        self.signal = False
        self.sig_idx = None
        self.is_dma = is_dma
        self.dma_sem = None
        self.dma_val = None
        self.succ = []
        self.ready = 0.0
        self.finish = 0.0


ACT_SETS = {AF.Silu: "silu", AF.Sigmoid: "sigmoid", AF.Exp: "lnexp", AF.Ln: "lnexp", AF.Sqrt: "sqrt",
            AF.Tanh: "silu"}


def _free_elems(ap):
    n = 1
    for d in ap.shape[1:]:
        n *= d
    return n


def est_cost(op):
    if op.fn is None:
        return 0.0, 0.0
    meth, kw = op.fn
    if op.is_dma:
        nb = 1
        for d in kw["out"].shape:
            nb *= d
        return 0.08, 2.0 + nb * 4 / 180e3
    if op.eng == "pe":
        if meth == "transpose":
            return 0.07, 0.3
        n = _free_elems(kw["rhs"])
        f32 = kw["rhs"].dtype == F32
        c = max(0.035, n / 2400.0) * (4 if f32 else 1)
        return c, c + 0.12
    n = _free_elems(kw["out"]) if "out" in kw else _free_elems(kw["ap"])
    if op.eng == "act":
        c = 0.2 + n / 1200.0
    elif op.eng == "dve":
        k = 6.3 if meth == "reciprocal" else 1.0
        c = 0.08 + k * n / 960.0
    else:
        c = 0.3 + n / 400.0
    return c, c + 0.05


class Prog:
    ENGS = ("pe", "act", "dve", "pool", "sp")

    def __init__(self, nc, st):
        self.nc = nc
        self.st = st
        self.ops = {e: [] for e in self.ENGS}
        self.esem = {e: st.enter_context(nc.semaphore("s_" + e)) for e in self.ENGS}
        self.stream_sem = {}
        self.stream_cnt = {}
        self.all_dmas = []
        self.barrier_for = {}
        self.nsem = 0
        self.nops = 0
        self.batch = 0
        self.sched_on = False

    def new_sem(self, name):
        self.nsem += 1
        return self.st.enter_context(self.nc.semaphore(name))

    def _add_dep(self, op, p, kind):
        if p is None or p is op:
            return
        if p.is_dma or op.is_dma or p.eng != op.eng:
            if not p.is_dma:
                p.signal = True
            op.deps.append(p)
        elif kind == "raw" and op.eng != "pe":
            p.signal = True
            op.deps.append(p)
        else:
            op.odeps.append(p)

    def _emit(self, op, reads, writes):
        eng = op.eng
        op.idx = self.nops
        self.nops += 1
        op.batch = self.batch
        if any(b.excl for b in reads):
            writes = list(writes) + [b for b in reads if b.excl]
            reads = [b for b in reads if not b.excl]
        b = self.barrier_for.pop(eng, None)
        if b:
            for p in b:
                self._add_dep(op, p, "raw")
        for bf in reads:
            self._add_dep(op, bf.last_w, "raw")
        for bf in writes:
            lw = bf.last_w
            if lw is not None and not (op.is_dma and lw.is_dma and not bf.readers and not bf.dma_readers):
                self._add_dep(op, lw, "waw")
            for r in bf.readers:
                self._add_dep(op, r, "war")
            for r in bf.dma_readers:
                self._add_dep(op, r, "war")
        for bf in reads:
            if op.is_dma:
                bf.dma_readers.append(op)
            else:
                rd = bf.readers
                if not self.sched_on:
                    rd[:] = [r for r in rd if r.eng != eng]
                rd.append(op)
        for bf in writes:
            bf.last_w = op
            bf.readers = []
            bf.dma_readers = []
        self.ops[eng].append(op)
        return op

    def op(self, eng, meth, kw, reads=(), writes=()):
        return self._emit(Op(eng, (meth, kw)), reads, writes)

    def dma(self, eng, kw, reads=(), writes=(), stream=None):
        op = Op(eng, ("dma_start", kw), is_dma=True)
        if stream is None:
            stream = "_d%d" % len(self.stream_sem)
        if stream not in self.stream_sem:
            self.stream_sem[stream] = self.new_sem("q_" + stream)
            self.stream_cnt[stream] = 0
        self.stream_cnt[stream] += 16
        op.dma_sem = self.stream_sem[stream]
        op.dma_val = self.stream_cnt[stream]
        self.all_dmas.append(op)
        return self._emit(op, reads, writes)

    def barrier(self):
        deps = []
        for e in self.ENGS:
            for p in reversed(self.ops[e]):
                if not p.is_dma:
                    deps.append(p)
                    break
        deps += self.all_dmas
        self.all_dmas = []
        for e in self.ENGS:
            self.barrier_for[e] = list(deps)

    def finish(self, final_eng="sp"):
        self.barrier()
        self._emit(Op(final_eng, None), (), ())
        self.replay()

    def _schedule(self, done):
        import heapq
        batch = []
        for e in self.ENGS:
            batch += self.ops[e][done[e]:]
        cur = self.batch
        last_stream = {}
        for op in sorted(batch, key=lambda o: o.idx):
            if op.is_dma:
                k = id(op.dma_sem)
                if k in last_stream:
                    op.odeps.append(last_stream[k])
                last_stream[k] = op
        for op in batch:
            op.succ = []
            op.nun = 0
            op.ready = 0.0
        for op in batch:
            for p in op.deps + op.odeps:
                if p.batch == cur:
                    p.succ.append(op)
                    op.nun += 1
        cost = {}
        bl = {}
        for op in sorted(batch, key=lambda o: -o.idx):
            c = est_cost(op)
            cost[id(op)] = c
            m = 0.0
            for s_ in op.succ:
                v = bl[id(s_)]
                if v > m:
                    m = v
            bl[id(op)] = c[1] + 0.15 + m
        free = {e: 0.0 for e in self.ENGS}
        order = {e: [] for e in self.ENGS}
        rdy = {e: [] for e in self.ENGS}
        fut = {e: [] for e in self.ENGS}
        for op in batch:
            if op.nun == 0:
                heapq.heappush(fut[op.eng], (0.0, op.idx, op))
        nsched = 0
        cur_set = None
        TBL_PEN = 3.0

        def act_set(op):
            if op.eng != "act" or op.fn is None or op.is_dma:
                return None
            return ACT_SETS.get(op.fn[1].get("func"))

        total = len(batch)
        while nsched < total:
            best = None
            for e in self.ENGS:
                f_, r_ = fut[e], rdy[e]
                while f_ and f_[0][0] <= free[e] + 1e-9:
                    _, _, o = heapq.heappop(f_)
                    pri = bl[id(o)]
                    heapq.heappush(r_, (-pri, o.idx, o))
                if r_:
                    st = free[e]
                elif f_:
                    st = f_[0][0]
                else:
                    continue
                if best is None or st < best[0]:
                    best = (st, e)
            st, e = best
            if rdy[e]:
                if e == "act" and cur_set is not None:
                    cands = heapq.nsmallest(6, rdy[e])
                    pick = cands[0]
                    for c_ in cands:
                        a = act_set(c_[2])
                        if a is None or a == cur_set:
                            if -c_[0] >= -cands[0][0] - TBL_PEN:
                                pick = c_
                            break
                    rdy[e].remove(pick)
                    heapq.heapify(rdy[e])
                    op = pick[2]
                else:
                    op = heapq.heappop(rdy[e])[2]
            else:
                op = heapq.heappop(fut[e])[2]
            a = act_set(op)
            if a is not None:
                if cur_set is not None and a != cur_set:
                    st += 1.4
                cur_set = a
            busy, lat = cost[id(op)]
            free[e] = st + busy
            op.finish = st + lat
            order[e].append(op)
            nsched += 1
            for s_ in op.succ:
                s_.nun -= 1
                if op.finish + 0.15 > s_.ready:
                    s_.ready = op.finish + 0.15
                if s_.nun == 0:
                    heapq.heappush(fut[s_.eng], (s_.ready, s_.idx, s_))
        assert nsched == len(batch), (nsched, len(batch))
        for e in self.ENGS:
            self.ops[e][done[e]:] = order[e]
        return max(free.values())

    def replay(self):
        nc = self.nc
        if not hasattr(self, "sigcnt"):
            self.sigcnt = {e: 0 for e in self.ENGS}
            self.known = {e: {} for e in self.ENGS}
            self.done = {e: 0 for e in self.ENGS}
        if self.sched_on:
            self.est_us = self._schedule(self.done)
        self.batch += 1
        for e in self.ENGS:
            for op in self.ops[e][self.done[e]:]:
                if op.signal and not op.is_dma:
                    self.sigcnt[e] += 1
                    op.sig_idx = self.sigcnt[e]
        esem = self.esem

        def run(ename, eng):
            known = self.known[ename]
            for op in self.ops[ename][self.done[ename]:]:
                need = {}
                for p in op.deps:
                    if p.is_dma:
                        s, v = p.dma_sem, p.dma_val
                    else:
                        if p.sig_idx is None:
                            continue
                        s, v = esem[p.eng], p.sig_idx
                    key = id(s)
                    if known.get(key, 0) >= v:
                        continue
                    if key not in need or need[key][1] < v:
                        need[key] = (s, v)
                for key, (s, v) in need.items():
                    eng.wait_ge(s, v)
                    known[key] = v
                if op.fn is None:
                    continue
                ins = getattr(eng, op.fn[0])(**op.fn[1])
                if op.is_dma:
                    ins.then_inc(op.dma_sem, 16)
                elif op.signal:
                    ins.then_inc(esem[ename], 1)
            self.done[ename] = len(self.ops[ename])

        with nc.Block() as block:
            @block.tensor
            def _(eng):
                run("pe", eng)

            @block.scalar
            def _(eng):
                run("act", eng)

            @block.vector
            def _(eng):
                run("dve", eng)

            @block.gpsimd
            def _(eng):
                run("pool", eng)

            @block.sync
            def _(eng):
                run("sp", eng)


class Tl:
    def __init__(self, t, name, nb=1):
        self.t = t
        self.b = Buf(name)
        self.bs = [Buf("%s_%d" % (name, i)) for i in range(nb)] if nb > 1 else [self.b]


class Env:
    _n = 0

    def __init__(self, nc, P, st):
        self.nc, self.P, self.st = nc, P, st
        Env._n += 1
        self.pfx = "e%d_" % Env._n

    def sb(self, name, shape, dtype, nb=1):
        t = self.st.enter_context(self.nc.sbuf_tensor(self.pfx + name, list(shape), dtype))
        return Tl(t, name, nb)

    def ps(self, name, shape=(128, 512), dtype=F32):
        t = self.st.enter_context(self.nc.psum_tensor(name, list(shape), dtype))
        tl = Tl(t, name)
        tl.b.excl = True
        return tl


def chan_cols(v):
    v = np.asarray(v, np.float32)
    return np.ascontiguousarray(v.reshape(-1, 128).T)


class ConstPack:
    def __init__(self):
        self.cols = []
        self.off = {}
        self.n = 0

    def add(self, name, arr):
        arr = np.asarray(arr, np.float32)
        assert arr.shape[0] == 128
        arr = arr.reshape(128, -1)
        self.off[name] = self.n
        self.cols.append(arr)
        self.n += arr.shape[1]

    def build(self):
        return np.ascontiguousarray(np.concatenate(self.cols, axis=1))


def glu_interleave(w, half):
    K = w.shape[0]
    nj = half // 128
    a = w[:, :half].reshape(K, nj, 128)
    b = w[:, half:].reshape(K, nj, 128)
    return np.ascontiguousarray(np.concatenate([a, b], axis=2).reshape(K, 2 * half))


def wlayout(w):
    K, N = w.shape
    return np.ascontiguousarray(w.reshape(K // 128, 128, N).transpose(1, 0, 2))


def ffn_tiles(start):
    tiles = []
    s = start
    while s < T:
        wv = min(510, T - s)
        tiles.append((s, wv))
        s += wv
    return tiles


def wq(dram):
    if isinstance(dram, tuple):
        return dram[0], "sp", [dram[1]]
    return dram, "pool", []


def load_weight(P, nc, tl, dram, nsplit, eng="pool"):
    dram, eng, rd = wq(dram)
    KC = dram.shape[1]
    per = (KC + nsplit - 1) // nsplit
    groups = []
    k = 0
    gi = 0
    while k < KC:
        k1 = min(KC, k + per)
        bf = tl.bs[gi] if len(tl.bs) > 1 else tl.b
        P.dma(eng, dict(out=tl.t[:, k:k1, :], in_=dram[:, k:k1, :]), reads=rd, writes=[bf])
        groups.append((k, k1, bf))
        k = k1
        gi += 1
    return groups


def load_weight_cols(P, nc, tl, dram, bounds, eng="pool"):
    dram, eng, rd = wq(dram)
    groups = []
    for gi, (b0, b1) in enumerate(bounds):
        bf = Buf("%s_cb%d" % (tl.b.name, gi))
        P.dma(eng, dict(out=tl.t[:, :, b0:b1], in_=dram[:, :, b0:b1]), reads=rd, writes=[bf])
        groups.append((b0, b1, bf))
    return groups


def grp_buf(groups, k):
    for k0, k1, bf in groups:
        if k0 <= k < k1:
            return bf
    raise KeyError(k)


def emit_norm_prep(P, xt, xn, rstd, ps_stat, ones, cst, goff, epso, w, x_src, lnexp=False):
    if x_src is not None:
        P.dma("sp", dict(out=xt.t[:, :, :w], in_=x_src), writes=[xt.b], stream="xt")
    P.op("act", "activation", dict(out=xn.t[:, :, :w], in_=xt.t[:, :, :w], func=AF.Square),
         reads=[xt.b], writes=xn.bs)
    for c in range(8):
        P.op("pe", "matmul", dict(out=ps_stat.t[:, :w], lhsT=ones.t[:, :], rhs=xn.t[:, c, :w],
                                  start=(c == 0), stop=(c == 7)),
             reads=[xn.bs[c], ones.b], writes=[ps_stat.b])
    if lnexp:
        P.op("act", "activation", dict(out=rstd.t[:, :w], in_=ps_stat.t[:, :w], func=AF.Ln,
                                       bias=cst.t[:, epso:epso + 1], scale=1.0),
             reads=[ps_stat.b, cst.b], writes=[rstd.b])
        P.op("act", "activation", dict(out=rstd.t[:, :w], in_=rstd.t[:, :w], func=AF.Exp, scale=-0.5),
             reads=[rstd.b], writes=[rstd.b])
    else:
        P.op("act", "activation", dict(out=rstd.t[:, :w], in_=ps_stat.t[:, :w], func=AF.Sqrt,
                                       bias=cst.t[:, epso:epso + 1], scale=1.0),
             reads=[ps_stat.b, cst.b], writes=[rstd.b])
        P.op("dve", "reciprocal", dict(out=rstd.t[:, :w], in_=rstd.t[:, :w]), reads=[rstd.b], writes=[rstd.b])
    for c in range(8):
        eng = "dve"
        P.op(eng, "scalar_tensor_tensor", dict(out=xn.t[:, c, :w], in0=xt.t[:, c, :w],
                                               scalar=cst.t[:, goff + c:goff + c + 1],
                                               in1=rstd.t[:, :w], op0=ALU.mult, op1=ALU.mult),
             reads=[xt.b, rstd.b, cst.b], writes=[xn.bs[c]])


def stage_ffn(P, nc, x_in, x_out, w_up_d, w_dn_d, cst, co, lname, psb):
    P.sched_on = SCHED_F
    with ExitStack() as st:
        E = Env(nc, P, st)
        wup = E.sb("wup", (128, 8, 2 * DFF), BF16, nb=8)
        wdn = E.sb("wdn", (128, NJ, D), BF16, nb=4)
        xt = E.sb("xt", (128, 8, 512), F32)
        xn = E.sb("xn", (128, 8, 512), BF16, nb=8)
        rstd = E.sb("rstd", (128, 512), F32)
        ones = E.sb("ones", (128, 128), BF16)
        ya = [E.sb("ya%d" % i, (128, 512), F32) for i in range(2)]
        yb = [E.sb("yb%d" % i, (128, 512), F32) for i in range(2)]
        sa = [E.sb("sa%d" % i, (128, 512), F32) for i in range(2)]
        g = E.sb("g", (128, NJ, 512), BF16, nb=NJ)
        xr = [E.sb("xr%d" % i, (128, 512), F32) for i in range(2)]
        xo = [E.sb("xo%d" % i, (128, 512), F32) for i in range(2)]
        psA = [psb[0], psb[1]]
        psB = [psb[2], psb[3]]
        ps_stat = psb[4]
        psD = [psb[5], psb[6], psb[7]]

        P.op("pool", "memset", dict(ap=ones.t[:, :], constant=1.0 / D), writes=[ones.b])
        gup = load_weight_cols(P, nc, wup, w_up_d, [(j * 256, min(NJ, j + 2) * 256) for j in range(0, NJ, 2)])
        gdn = load_weight(P, nc, wdn, w_dn_d, 4)

        goff = co[lname + "_norm"]
        cwo = co[lname + "_cw"]
        cbo = co[lname + "_cb"]
        mko = co["mask"]
        tiles = ffn_tiles(-34 if lname == "f0" else 0)
        xin_v = x_in.ap().rearrange("(c p) t -> p c t", p=128)

        def prep(i):
            s, wv = tiles[i]
            c0 = s + HALO - 2
            w = wv + 2
            emit_norm_prep(P, xt, xn, rstd, ps_stat, ones, cst, goff, co["eps"], w, xin_v[:, :, c0:c0 + w])

        dcount = 0
        prep(0)
        for i, (s, wv) in enumerate(tiles):
            c0 = s + HALO - 2
            w = wv + 2
            for j in range(NJ):
                pa, pb = psA[j % 2], psB[j % 2]
                for (pp, colbase) in ((pa, j * 256), (pb, j * 256 + 128)):
                    for c in range(8):
                        P.op("pe", "matmul", dict(out=pp.t[:, :w], lhsT=wup.t[:, c, colbase:colbase + 128],
                                                  rhs=xn.t[:, c, :w], start=(c == 0), stop=(c == 7)),
                             reads=[grp_buf(gup, colbase), xn.bs[c]], writes=[pp.b])
                yy = (ya[j % 2], yb[j % 2])
                for h, (pp, y) in enumerate(((pa, yy[0]), (pb, yy[1]))):
                    jj = j + h * NJ
                    P.op("act", "activation", dict(out=y.t[:, :wv], in_=pp.t[:, 2:w], func=AF.Identity,
                                                   bias=cst.t[:, cbo + jj:cbo + jj + 1],
                                                   scale=cst.t[:, cwo + jj * 3 + 2:cwo + jj * 3 + 3]),
                         reads=[pp.b, cst.b], writes=[y.b])
                    for k in (1, 0):
                        P.op("dve", "scalar_tensor_tensor", dict(
                            out=y.t[:, :wv], in0=pp.t[:, k:k + wv],
                            scalar=cst.t[:, cwo + jj * 3 + k:cwo + jj * 3 + k + 1],
                            in1=y.t[:, :wv], op0=ALU.mult, op1=ALU.add),
                            reads=[pp.b, y.b, cst.b], writes=[y.b])
                s_ = sa[j % 2]
                P.op("act", "activation", dict(out=s_.t[:, :wv], in_=yy[0].t[:, :wv], func=AF.Silu),
                     reads=[yy[0].b], writes=[s_.b])
                P.op("pool", "tensor_tensor", dict(out=g.t[:, j, :wv], in0=s_.t[:, :wv], in1=yy[1].t[:, :wv],
                                                   op=ALU.mult),
                     reads=[s_.b, yy[1].b], writes=[g.bs[j]])
            if i + 1 < len(tiles):
                prep(i + 1)
            nneg = max(0, min(wv, -s))
            for o in range(8):
                slot = dcount % 2
                pd = psD[dcount % 3]
                dcount += 1
                xr_, xo_ = xr[slot], xo[slot]
                P.dma("sp", dict(out=xr_.t[:, :wv], in_=x_in[o * 128:(o + 1) * 128, c0 + 2:c0 + 2 + wv]),
                      writes=[xr_.b], stream="xr%d" % slot)
                for k in range(NJ):
                    P.op("pe", "matmul", dict(out=pd.t[:, :wv], lhsT=wdn.t[:, k, o * 128:(o + 1) * 128],
                                              rhs=g.t[:, k, :wv], start=(k == 0), stop=(k == NJ - 1)),
                         reads=[grp_buf(gdn, k), g.bs[k]], writes=[pd.b])
                if nneg > 0:
                    P.op("dve", "scalar_tensor_tensor", dict(
                        out=xo_.t[:, :nneg], in0=pd.t[:, :nneg], scalar=cst.t[:, mko:mko + 1],
                        in1=xr_.t[:, :nneg], op0=ALU.mult, op1=ALU.add),
                        reads=[pd.b, xr_.b, cst.b], writes=[xo_.b])
                if nneg < wv:
                    P.op("dve", "tensor_tensor", dict(out=xo_.t[:, nneg:wv], in0=pd.t[:, nneg:wv],
                                                      in1=xr_.t[:, nneg:wv], op=ALU.add),
                         reads=[pd.b, xr_.b], writes=[xo_.b])
                P.dma("sp", dict(out=x_out[o * 128:(o + 1) * 128, c0 + 2:c0 + 2 + wv], in_=xo_.t[:, :wv]),
                      reads=[xo_.b], stream="xo%d" % slot)
        P.barrier()
        P.replay()


def std_tiles(halo=HV):
    tiles = [(-halo, halo)]
    s = 0
    while s < T:
        tiles.append((s, 512))
        s += 512
    return tiles


def stage_conf(P, nc, x_in, x_out, w_in_d, w_out_d, cst, co, psb):
    P.sched_on = SCHED_C
    with ExitStack() as st:
        E = Env(nc, P, st)
        win = E.sb("win", (128, 8, 2 * D), BF16, nb=8)
        wout = E.sb("wout", (128, 8, D), BF16, nb=4)
        dg = E.sb("dg", (128, 8, CK, 128), BF16, nb=8)
        xt = E.sb("xt", (128, 8, 512), F32)
        xn = E.sb("xn", (128, 8, 512), BF16, nb=8)
        rstd = E.sb("rstd", (128, 512), F32)
        ones = E.sb("ones", (128, 128), BF16)
        ub = E.sb("ub", (128, 8, 30 + 512), BF16, nb=8)
        sg = [E.sb("sg%d" % i, (128, 512), F32) for i in range(2)]
        y = E.sb("y", (128, 8, 512), F32, nb=8)
        ybf = E.sb("ybf", (128, 8, 512), BF16, nb=8)
        ysq = E.sb("ysq", (128, 8, 512), BF16, nb=8)
        zs = ybf
        mu = E.sb("mu", (128, 512), F32)
        var = E.sb("var", (128, 512), F32)
        tmp = [E.sb("tmp%d" % i, (128, 512), F32) for i in range(2)]
        xr = [E.sb("xr%d" % i, (128, 512), F32) for i in range(2)]
        xo = [E.sb("xo%d" % i, (128, 512), F32) for i in range(2)]
        psA = [psb[0], psb[1]]
        psB = [psb[2], psb[3]]
        psC = [psb[4], psb[7]]
        ps_stat, ps_mu, ps_m2 = psb[4], psb[5], psb[6]

        P.op("pool", "memset", dict(ap=ones.t[:, :], constant=1.0 / D), writes=[ones.b])
        P.op("pool", "memset", dict(ap=ub.t[:, :, 0:30], constant=0.0), writes=ub.bs)
        gin = load_weight_cols(P, nc, win, w_in_d, [(j * 512, (j + 1) * 512) for j in range(4)])
        gout = load_weight(P, nc, wout, w_out_d, 4)
        ido = co["ident"]
        dwo = co["c_dw"]
        goff = co["c_norm"]
        bio, dbo, lgo, lbo, boo, mko = co["c_bin"], co["c_db"], co["c_lg"], co["c_lb"], co["c_bout"], co["mask"]
        xin_v = x_in.ap().rearrange("(c p) t -> p c t", p=128)
        tiles = std_tiles(32)
        cnt = {"o": 0}

        def load_x(i):
            s, w = tiles[i]
            c0 = s + HALO
            P.dma("sp", dict(out=xt.t[:, :, :w], in_=xin_v[:, :, c0:c0 + w]), writes=[xt.b], stream="xt")

        def prep(i):
            s, w = tiles[i]
            emit_norm_prep(P, xt, xn, rstd, ps_stat, ones, cst, goff, co["eps"], w, None)

        def phase_a(i):
            s, w = tiles[i]
            for j in range(8):
                pa, pb = psA[j % 2], psB[j % 2]
                for (pp, colbase) in ((pa, j * 256), (pb, j * 256 + 128)):
                    for c in range(8):
                        P.op("pe", "matmul", dict(out=pp.t[:, :w], lhsT=win.t[:, c, colbase:colbase + 128],
                                                  rhs=xn.t[:, c, :w], start=(c == 0), stop=(c == 7)),
                             reads=[grp_buf(gin, colbase), xn.bs[c]], writes=[pp.b])
                sg_ = sg[j % 2]
                P.op("act", "activation", dict(out=sg_.t[:, :w], in_=pb.t[:, :w], func=AF.Sigmoid,
                                               bias=cst.t[:, bio + 8 + j:bio + 9 + j], scale=1.0),
                     reads=[pb.b, cst.b], writes=[sg_.b])
                P.op("dve", "scalar_tensor_tensor", dict(out=ub.t[:, j, 30:30 + w], in0=pa.t[:, :w],
                                                         scalar=cst.t[:, bio + j:bio + j + 1], in1=sg_.t[:, :w],
                                                         op0=ALU.add, op1=ALU.mult),
                     reads=[pa.b, sg_.b, cst.b], writes=[ub.bs[j]])
                if s < 0:
                    P.op("act", "activation", dict(out=ub.t[:, j, 30:30 + w], in_=ub.t[:, j, 30:30 + w],
                                                   func=AF.Identity, scale=cst.t[:, mko:mko + 1]),
                         reads=[ub.bs[j], cst.b], writes=[ub.bs[j]])

        def phase_b(i):
            s, w = tiles[i]
            for c in range(8):
                pc = psC[c % 2]
                for k in range(KD, CK):
                    P.op("pe", "matmul", dict(out=pc.t[:, :w], lhsT=dg.t[:, c, k, :], rhs=ub.t[:, c, k:k + w],
                                              start=(k == KD), stop=(k == CK - 1)),
                         reads=[dg.bs[c], ub.bs[c]], writes=[pc.b])
                P.op("act", "activation", dict(out=y.t[:, c, :w], in_=pc.t[:, :w], func=AF.Identity,
                                               bias=cst.t[:, dbo + c:dbo + c + 1], scale=1.0),
                     reads=[pc.b, cst.b], writes=[y.bs[c]])
                for k in range(KD):
                    P.op("dve", "scalar_tensor_tensor", dict(
                        out=y.t[:, c, :w], in0=ub.t[:, c, k:k + w],
                        scalar=cst.t[:, dwo + c * CK + k:dwo + c * CK + k + 1], in1=y.t[:, c, :w],
                        op0=ALU.mult, op1=ALU.add),
                        reads=[ub.bs[c], y.bs[c], cst.b], writes=[y.bs[c]])
                P.op("act", "activation", dict(out=ysq.t[:, c, :w], in_=y.t[:, c, :w], func=AF.Square),
                     reads=[y.bs[c]], writes=[ysq.bs[c]])
                P.op("dve", "tensor_copy", dict(out=ybf.t[:, c, :w], in_=y.t[:, c, :w]),
                     reads=[y.bs[c]], writes=[ybf.bs[c]])
            P.op("pool", "tensor_copy", dict(out=ub.t[:, :, 0:30], in_=ub.t[:, :, w:w + 30]),
                 reads=ub.bs, writes=ub.bs)
            for c in range(8):
                P.op("pe", "matmul", dict(out=ps_mu.t[:, :w], lhsT=ones.t[:, :], rhs=ybf.t[:, c, :w],
                                          start=(c == 0), stop=(c == 7)),
                     reads=[ybf.bs[c], ones.b], writes=[ps_mu.b])
            for c in range(8):
                P.op("pe", "matmul", dict(out=ps_m2.t[:, :w], lhsT=ones.t[:, :], rhs=ysq.t[:, c, :w],
                                          start=(c == 0), stop=(c == 7)),
                     reads=[ysq.bs[c], ones.b], writes=[ps_m2.b])

        def phase_c_elem(i):
            s, w = tiles[i]
            P.op("act", "activation", dict(out=mu.t[:, :w], in_=ps_mu.t[:, :w], func=AF.Identity),
                 reads=[ps_mu.b], writes=[mu.b])
            P.op("dve", "scalar_tensor_tensor", dict(out=var.t[:, :w], in0=mu.t[:, :w], scalar=-1.0,
                                                     in1=mu.t[:, :w], op0=ALU.mult, op1=ALU.mult),
                 reads=[mu.b], writes=[var.b])
            P.op("dve", "tensor_tensor", dict(out=var.t[:, :w], in0=ps_m2.t[:, :w], in1=var.t[:, :w], op=ALU.add),
                 reads=[ps_m2.b, var.b], writes=[var.b])
            P.op("act", "activation", dict(out=var.t[:, :w], in_=var.t[:, :w], func=AF.Sqrt,
                                           bias=cst.t[:, co["eps"]:co["eps"] + 1], scale=1.0),
                 reads=[var.b, cst.b], writes=[var.b])
            P.op("dve", "reciprocal", dict(out=var.t[:, :w], in_=var.t[:, :w]), reads=[var.b], writes=[var.b])
            for c in range(8):
                t_ = tmp[c % 2]
                P.op("dve", "tensor_tensor", dict(out=t_.t[:, :w], in0=y.t[:, c, :w], in1=mu.t[:, :w],
                                                  op=ALU.subtract),
                     reads=[y.bs[c], mu.b], writes=[t_.b])
                P.op("dve", "scalar_tensor_tensor", dict(out=t_.t[:, :w], in0=t_.t[:, :w],
                                                         scalar=cst.t[:, lgo + c:lgo + c + 1], in1=var.t[:, :w],
                                                         op0=ALU.mult, op1=ALU.mult),
                     reads=[t_.b, var.b, cst.b], writes=[t_.b])
                P.op("act", "activation", dict(out=zs.t[:, c, :w], in_=t_.t[:, :w], func=AF.Silu,
                                               bias=cst.t[:, lbo + c:lbo + c + 1], scale=1.0),
                     reads=[t_.b, cst.b], writes=[zs.bs[c]])

        def phase_c_pe(i):
            s, w = tiles[i]
            c0 = s + HALO
            for o in range(8):
                po = psB[o % 2]
                slot = cnt["o"] % 2
                cnt["o"] += 1
                xo_, xr_ = xo[slot], xr[slot]
                P.dma("sp", dict(out=xr_.t[:, :w], in_=x_in[o * 128:(o + 1) * 128, c0:c0 + w]),
                      writes=[xr_.b], stream="xr%d" % slot)
                for c in range(8):
                    P.op("pe", "matmul", dict(out=po.t[:, :w], lhsT=wout.t[:, c, o * 128:(o + 1) * 128],
                                              rhs=zs.t[:, c, :w], start=(c == 0), stop=(c == 7)),
                         reads=[grp_buf(gout, c), zs.bs[c]], writes=[po.b])
                if s < 0:
                    P.op("dve", "tensor_scalar", dict(out=xo_.t[:, :w], in0=po.t[:, :w],
                                                      scalar1=cst.t[:, boo + o:boo + o + 1],
                                                      scalar2=cst.t[:, mko:mko + 1], op0=ALU.add, op1=ALU.mult),
                         reads=[po.b, cst.b], writes=[xo_.b])
                    P.op("dve", "tensor_tensor", dict(out=xo_.t[:, :w], in0=xo_.t[:, :w], in1=xr_.t[:, :w],
                                                      op=ALU.add),
                         reads=[xo_.b, xr_.b], writes=[xo_.b])
                else:
                    P.op("dve", "scalar_tensor_tensor", dict(out=xo_.t[:, :w], in0=po.t[:, :w],
                                                             scalar=cst.t[:, boo + o:boo + o + 1],
                                                             in1=xr_.t[:, :w], op0=ALU.add, op1=ALU.add),
                         reads=[po.b, xr_.b, cst.b], writes=[xo_.b])
                P.dma("sp", dict(out=x_out[o * 128:(o + 1) * 128, c0:c0 + w], in_=xo_.t[:, :w]),
                      reads=[xo_.b], stream="xo%d" % slot)

        n = len(tiles)
        load_x(0)
        prep(0)
        if n > 1:
            load_x(1)
        ident_b = cst.t[:, ido:ido + 128].rearrange("p (a b) -> p a b", a=1).broadcast_to([128, CK, 128])
        for c in range(8):
            dwc = cst.t[:, dwo + c * CK:dwo + (c + 1) * CK].rearrange("p (a b) -> p a b", b=1).broadcast_to(
                [128, CK, 128])
            P.op("dve", "tensor_tensor", dict(out=dg.t[:, c, :, :], in0=ident_b, in1=dwc, op=ALU.mult),
                 reads=[cst.b], writes=[dg.bs[c]])
        phase_a(0)
        for i in range(n):
            phase_b(i)
            if i + 1 < n:
                prep(i + 1)
                if i + 2 < n:
                    load_x(i + 2)
            phase_c_elem(i)
            if i + 1 < n:
                phase_a(i + 1)
            phase_c_pe(i)
        P.barrier()
        P.replay()


SCHED_M, SCHED_C, SCHED_F = True, True, False
PREFETCH_CASTS = []
NPRE = T - HV
MW = 3080


def stage_mlstm(P, nc, x_in, xp, x_out, w_in_d, w_out_d, cst, co, psb):
    P.sched_on = SCHED_M
    with ExitStack() as st:
        E = Env(nc, P, st)
        win = E.sb("win", (128, 8, MW), BF16, nb=8)
        wout = E.sb("wout", (128, 8, D), BF16, nb=8)
        xt = E.sb("xt", (128, 8, 512), F32)
        xn2 = [E.sb("xn%d" % i, (128, 8, 512), BF16, nb=8) for i in range(2)]
        rstd = E.sb("rstd", (128, 512), F32)
        ones = E.sb("ones", (128, 128), BF16)
        onesf = E.sb("onesf", (128, 128), F32)
        identb = E.sb("identb", (128, 128), BF16)
        pq = E.sb("pq", (128, 8, 3 + 512), F32, nb=8)
        cv = [E.sb("cv%d" % i, (128, 512), F32) for i in range(2)]
        qk2 = [E.sb("qk%d" % i, (128, 8, 512), BF16, nb=8) for i in range(2)]
        vh2 = [E.sb("vh%d" % i, (128, 4, 4, 260), BF16, nb=4) for i in range(2)]
        sgo2 = [E.sb("sgo%d" % i, (128, 4, D), BF16, nb=4) for i in range(2)]
        gsb = E.sb("gsb", (128, 4, 8), F32)
        sp_ = E.sb("sp", (128, 4, 4), F32)
        tmpg = E.sb("tmpg", (128, 4, 4), F32)
        ek3 = [E.sb("ek%d" % i, (128, 16), F32) for i in range(3)]
        ebq_2 = [E.sb("ebq%d" % i, (128, 16), F32) for i in range(3)]
        ebq2_2 = [E.sb("ebqq%d" % i, (128, 16), F32) for i in range(3)]
        ebl = [E.sb("ebl%d" % i, (128, 16), F32) for i in range(3)]
        elast = E.sb("elast", (128, 4), F32)
        stb = [E.sb("stb0", (128, 4, 128), BF16)] * 2
        ktok = [E.sb("ktok0", (128, 4, 128), BF16)] * 2
        G = E.sb("G", (128, 4, 260), F32)
        Cb = E.sb("Cb", (128, 4, 260), BF16)
        ssq = E.sb("ssq", (128, 4), F32)
        junk = E.sb("junk", (128, 256), BF16)
        sm = [E.sb("sm%d" % i, (128, 4), F32) for i in range(4)]
        hn = [E.sb("hn0", (128, D), BF16)] * 2
        numraw = [E.sb("numraw%d" % i, (128, D), F32) for i in range(2)]
        denraw = [E.sb("denraw%d" % i, (128, 4), F32) for i in range(2)]
        hnT = E.sb("hnT", (128, 8, 512), BF16, nb=4)
        xr = [E.sb("xr%d" % i, (128, 512), F32) for i in range(2)]
        xo = [E.sb("xo%d" % i, (128, 512), F32) for i in range(2)]
        psA = [psb[0], psb[1], psb[2]]
        ps_stat = psb[7]
        misc = psb[3]
        ps_s = psb[4]
        ps_num = [psb[5], psb[6]]
        ps_U = [psb[7], psb[7]]
        g_ps = misc.t[:, 0:32]
        bbn_ps = misc.t[:, 32:48]
        tot_ps = misc.t[:, 48:64]
        den_ps = misc.t[:, 64:68]
        nu_ps = misc.t[:, 72:76]
        psT4 = psb[4].t.bitcast(BF16)

        ido, mko, epo = co["ident"], co["mask"], co["eps"]
        tro = co["maskT"]
        P.op("pool", "memset", dict(ap=ones.t[:, :], constant=1.0 / D), writes=[ones.b])
        P.op("pool", "memset", dict(ap=onesf.t[:, :], constant=1.0), writes=[onesf.b])
        P.op("pool", "memset", dict(ap=pq.t[:, :, 0:3], constant=0.0), writes=pq.bs)
        P.op("pool", "memset", dict(ap=G.t[:, :, :], constant=0.0), writes=[G.b])
        P.op("pool", "memset", dict(ap=elast.t[:, :], constant=1.0), writes=[elast.b])
        for i in range(2):
            P.op("pool", "memset", dict(ap=vh2[i].t[:, :, :, :], constant=0.0), writes=vh2[i].bs)
        P.op("dve", "tensor_copy", dict(out=identb.t[:, :], in_=cst.t[:, ido:ido + 128]),
             reads=[cst.b], writes=[identb.b])
        gin = load_weight_cols(P, nc, win, w_in_d, [(0, 512), (512, 1024), (3072, 3080), (1024, 2048), (2048, 3072)])
        gout = load_weight(P, nc, wout, w_out_d, 8)
        for (dst, src, rb) in PREFETCH_CASTS:
            KC = src.shape[1]
            step = max(1, KC // 4)
            for k in range(0, KC, step):
                k1 = min(KC, k + step)
                P.dma("pool", dict(out=dst[:, k:k1, :], in_=src[:, k:k1, :]), writes=[rb])
        hno = co["m_hn"]
        for j in range(8):
            P.op("act", "activation", dict(out=wout.t[:, j, :], in_=wout.t[:, j, :], func=AF.Identity,
                                           scale=cst.t[:, hno + j:hno + j + 1]),
                 reads=[grp_buf(gout, j), cst.b], writes=[grp_buf(gout, j)])

        goff, cwo, cbo, bgo, lqo = co["m_norm"], co["m_cw"], co["m_cb"], co["m_bg"], co["lnqs"]
        xin_v = x_in.ap().rearrange("(c p) t -> p c t", p=128)
        xp_v = xp.ap().rearrange("(c p) t -> p c t", p=128)

        tiles = []
        s = 0
        while s < NPRE:
            w = min(512, NPRE - s)
            tiles.append(("pre", xp_v[:, :, s:s + w], w, s - T))
            s += w
        for (s, w) in std_tiles():
            tiles.append(("main", xin_v[:, :, s + HALO:s + HALO + w], w, s))
        cnt = {"t": 0, "o": 0, "k": 0, "e": 0}

        def gen_g(ti):
            kind, xsrc, w, s0 = tiles[ti]
            main = kind == "main"
            nch = w // 128
            xn = xn2[ti % 2]
            ek, ebq, ebq2, ebl_c = ek3[ti % 3], ebq_2[ti % 3], ebq2_2[ti % 3], ebl[ti % 3]
            if ti == 0:
                P.dma("sp", dict(out=xt.t[:, :, :w], in_=xsrc), writes=[xt.b], stream="xt")
            emit_norm_prep(P, xt, xn, rstd, ps_stat, ones, cst, goff, epo, w, None, lnexp=True)
            if ti + 1 < len(tiles):
                w2 = tiles[ti + 1][2]
                P.dma("sp", dict(out=xt.t[:, :, :w2], in_=tiles[ti + 1][1]), writes=[xt.b], stream="xt")
            yield
            for ch in range(nch):
                for c in range(8):
                    P.op("pe", "matmul", dict(out=g_ps[:, ch * 8:(ch + 1) * 8],
                                              lhsT=xn.t[:, c, ch * 128:(ch + 1) * 128], rhs=win.t[:, c, 3072:3080],
                                              start=(c == 0), stop=(c == 7)),
                         reads=[xn.bs[c], grp_buf(gin, 3072)], writes=[misc.b])
            gs3 = gsb.t[:, 0:nch, :]
            P.op("dve", "tensor_tensor", dict(out=gsb.t[:, 0:nch, :].rearrange("p a b -> p (a b)"),
                                              in0=g_ps[:, 0:nch * 8], in1=cst.t[:, bgo:bgo + nch * 8], op=ALU.add),
                 reads=[misc.b, cst.b], writes=[gsb.b])
            P.op("act", "activation", dict(out=sp_.t[:, 0:nch, :], in_=gs3[:, :, 4:8], func=AF.Exp, scale=-1.0),
                 reads=[gsb.b], writes=[sp_.b])
            P.op("act", "activation", dict(out=sp_.t[:, 0:nch, :], in_=sp_.t[:, 0:nch, :], func=AF.Ln,
                                           bias=1.0, scale=1.0),
                 reads=[sp_.b], writes=[sp_.b])
            spf = sp_.t[:, 0:nch, :].rearrange("p a b -> p (a b)")
            P.op("pe", "matmul", dict(out=bbn_ps[:, 0:nch * 4], lhsT=cst.t[:, tro:tro + 128], rhs=spf,
                                      start=True, stop=True),
                 reads=[sp_.b, cst.b], writes=[misc.b])
            P.op("pe", "matmul", dict(out=tot_ps[:, 0:nch * 4], lhsT=onesf.t[:, :], rhs=spf, start=True, stop=True),
                 reads=[sp_.b, onesf.b], writes=[misc.b])
            P.op("dve", "tensor_tensor", dict(out=tmpg.t[:, 0:nch, :], in0=gs3[:, :, 0:4],
                                              in1=bbn_ps[:, 0:nch * 4].rearrange("p (a b) -> p a b", b=4),
                                              op=ALU.add),
                 reads=[gsb.b, misc.b], writes=[tmpg.b])
            P.op("act", "activation", dict(out=ek.t[:, 0:nch * 4],
                                           in_=tmpg.t[:, 0:nch, :].rearrange("p a b -> p (a b)"), func=AF.Exp),
                 reads=[tmpg.b], writes=[ek.b])
            if s0 < 0:
                P.op("dve", "tensor_scalar", dict(out=ek.t[:, 0:nch * 4], in0=ek.t[:, 0:nch * 4],
                                                  scalar1=cst.t[:, mko:mko + 1], scalar2=None, op0=ALU.mult),
                     reads=[ek.b, cst.b], writes=[ek.b])
            P.op("act", "activation", dict(out=ebl_c.t[:, 0:nch * 4], in_=tot_ps[:, 0:nch * 4], func=AF.Exp,
                                           scale=-1.0),
                 reads=[misc.b], writes=[ebl_c.b])
            if main:
                P.op("act", "activation", dict(out=ebq.t[:, 0:nch * 4], in_=bbn_ps[:, 0:nch * 4], func=AF.Exp,
                                               scale=-1.0, bias=cst.t[:, lqo:lqo + 1]),
                     reads=[misc.b, cst.b], writes=[ebq.b])
                P.op("dve", "tensor_tensor", dict(out=ebq2.t[:, 0:nch * 4], in0=ebq.t[:, 0:nch * 4],
                                                  in1=ebq.t[:, 0:nch * 4], op=ALU.mult),
                     reads=[ebq.b], writes=[ebq2.b])
            yield

        def gen_p(ti):
            kind, xsrc, w, s0 = tiles[ti]
            main = kind == "main"
            last_pre = (not main) and tiles[ti + 1][0] == "main"
            nch = w // 128
            par = ti % 2
            xn = xn2[ti % 2]
            qk, vh, sgo = qk2[par], vh2[par], sgo2[par]
            ek = ek3[ti % 3]
            jlist = list(range(8)) if (main or last_pre) else [4, 5, 6, 7]
            for j in jlist:
                pa = psA[cnt["t"] % 3]
                cv_ = cv[cnt["t"] % 2]
                cnt["t"] += 1
                for c in range(8):
                    P.op("pe", "matmul", dict(out=pa.t[:, :w], lhsT=win.t[:, c, j * 128:(j + 1) * 128],
                                              rhs=xn.t[:, c, :w], start=(c == 0), stop=(c == 7)),
                         reads=[grp_buf(gin, j * 128), xn.bs[c]], writes=[pa.b])
                P.op("act", "activation", dict(out=pq.t[:, j, 3:3 + w], in_=pa.t[:, :w], func=AF.Identity),
                     reads=[pa.b], writes=[pq.bs[j]])
                P.op("act", "activation", dict(out=cv_.t[:, :w], in_=pa.t[:, :w], func=AF.Identity,
                                               bias=cst.t[:, cbo + j:cbo + j + 1],
                                               scale=cst.t[:, cwo + j * 4 + 3:cwo + j * 4 + 4]),
                     reads=[pa.b, cst.b], writes=[cv_.b])
                for k in (2, 1, 0):
                    P.op("dve", "scalar_tensor_tensor", dict(
                        out=cv_.t[:, :w], in0=pq.t[:, j, k:k + w],
                        scalar=cst.t[:, cwo + j * 4 + k:cwo + j * 4 + k + 1], in1=cv_.t[:, :w],
                        op0=ALU.mult, op1=ALU.add),
                        reads=[pq.bs[j], cv_.b, cst.b], writes=[cv_.b])
                P.op("act", "activation", dict(out=qk.t[:, j, :w], in_=cv_.t[:, :w], func=AF.Silu),
                     reads=[cv_.b], writes=[qk.bs[j]])
                yield
            P.op("pool", "tensor_copy", dict(out=pq.t[:, :, 0:3], in_=pq.t[:, :, w:w + 3]),
                 reads=pq.bs, writes=pq.bs)
            for ch in range(nch):
                tok = slice(ch * 128, (ch + 1) * 128)
                for half in range(2):
                    pa = psA[cnt["t"] % 3]
                    cnt["t"] += 1
                    for c in range(8):
                        P.op("pe", "matmul", dict(out=pa.t[:, :], lhsT=xn.t[:, c, tok],
                                                  rhs=win.t[:, c, 1024 + half * 512:1024 + (half + 1) * 512],
                                                  start=(c == 0), stop=(c == 7)),
                             reads=[xn.bs[c], grp_buf(gin, 1024)], writes=[pa.b])
                    for hh in range(2):
                        h = half * 2 + hh
                        if hh == 0:
                            P.op("act", "activation", dict(out=vh.t[:, ch, h, 0:256],
                                                           in_=pa.t[:, hh * 256:(hh + 1) * 256], func=AF.Identity,
                                                           scale=ek.t[:, ch * 4 + h:ch * 4 + h + 1]),
                                 reads=[pa.b, ek.b], writes=[vh.bs[ch]])
                        else:
                            P.op("dve", "tensor_scalar", dict(out=vh.t[:, ch, h, 0:256],
                                                              in0=pa.t[:, hh * 256:(hh + 1) * 256],
                                                              scalar1=ek.t[:, ch * 4 + h:ch * 4 + h + 1],
                                                              scalar2=None, op0=ALU.mult),
                                 reads=[pa.b, ek.b], writes=[vh.bs[ch]])
                    yield
                P.op("dve", "tensor_copy", dict(out=vh.t[:, ch, :, 256:257],
                                                in_=ek.t[:, ch * 4:ch * 4 + 4].rearrange("p (a b) -> p a b", b=1)),
                     reads=[ek.b], writes=[vh.bs[ch]])
                if main:
                    for half in range(2):
                        pa = psA[cnt["t"] % 3]
                        cnt["t"] += 1
                        for c in range(8):
                            P.op("pe", "matmul", dict(out=pa.t[:, :], lhsT=xn.t[:, c, tok],
                                                      rhs=win.t[:, c, 2048 + half * 512:2048 + (half + 1) * 512],
                                                      start=(c == 0), stop=(c == 7)),
                                 reads=[xn.bs[c], grp_buf(gin, 2048)], writes=[pa.b])
                        P.op("act", "activation", dict(out=sgo.t[:, ch, half * 512:(half + 1) * 512], in_=pa.t[:, :],
                                                       func=AF.Sigmoid),
                             reads=[pa.b], writes=[sgo.bs[ch]])
                        yield

        def gen_chunks(ti):
            kind, xsrc, w, s0 = tiles[ti]
            main = kind == "main"
            nch = w // 128
            par = ti % 2
            qk, vh, sgo = qk2[par], vh2[par], sgo2[par]
            ebq, ebq2, ebl_c = ebq_2[ti % 3], ebq2_2[ti % 3], ebl[ti % 3]

            def state_part(ch):
                tok = slice(ch * 128, (ch + 1) * 128)
                kt = ktok[cnt["k"] % 2]
                sb_ = stb[cnt["k"] % 2]
                nr = numraw[cnt["k"] % 2]
                dr = denraw[cnt["k"] % 2]
                cnt["k"] += 1
                if ch > 0:
                    eprev = ebl_c.t[:, (ch - 1) * 4:(ch - 1) * 4 + 4]
                    eprev_b = ebl_c.b
                else:
                    eprev = elast.t[:, 0:4]
                    eprev_b = elast.b
                for h in range(4):
                    P.op("pe", "transpose", dict(out=psT4[:, h * 128:(h + 1) * 128], in_=qk.t[:, 4 + h, tok],
                                                 identity=identb.t[:, :]),
                         reads=[qk.bs[4 + h], identb.b], writes=[ps_s.b])
                P.op("dve", "tensor_copy", dict(out=kt.t[:, :, :].rearrange("p a b -> p (a b)"), in_=psT4[:, 0:512]),
                     reads=[ps_s.b], writes=[kt.b])
                yield
                if main:
                    for h in range(4):
                        P.op("act", "activation", dict(out=Cb.t[:, h, 0:257], in_=G.t[:, h, 0:257], func=AF.Identity,
                                                       scale=eprev[:, h:h + 1]),
                             reads=[G.b, eprev_b], writes=[Cb.b])
                    for h in range(4):
                        P.op("pe", "matmul", dict(out=ps_s.t[:, h * 128:(h + 1) * 128], lhsT=qk.t[:, 4 + h, tok],
                                                  rhs=qk.t[:, h, tok], start=True, stop=True),
                             reads=[qk.bs[4 + h], qk.bs[h]], writes=[ps_s.b])
                    P.op("dve", "tensor_tensor", dict(
                        out=sb_.t[:, :, :], in0=ps_s.t[:, :].rearrange("p (a b) -> p a b", b=128),
                        in1=cst.t[:, tro:tro + 128].rearrange("p (a b) -> p a b", a=1).broadcast_to([128, 4, 128]),
                        op=ALU.mult),
                        reads=[ps_s.b, cst.b], writes=[sb_.b])
                    yield
                def u_pair(hp):
                    for h in (2 * hp, 2 * hp + 1):
                        pu = ps_U[h // 2]
                        P.op("pe", "matmul", dict(out=pu.t[:, (h % 2) * 256:(h % 2 + 1) * 256], lhsT=kt.t[:, h, :],
                                                  rhs=vh.t[:, ch, h, 0:256], start=True, stop=True),
                             reads=[kt.b, vh.bs[ch]], writes=[pu.b])
                        P.op("pe", "matmul", dict(out=nu_ps[:, h:h + 1], lhsT=kt.t[:, h, :],
                                                  rhs=vh.t[:, ch, h, 256:257], start=True, stop=True),
                             reads=[kt.b, vh.bs[ch]], writes=[misc.b])

                def g_pair(hp):
                    for h in (2 * hp, 2 * hp + 1):
                        pu = ps_U[h // 2]
                        P.op("dve", "scalar_tensor_tensor", dict(out=G.t[:, h, 0:256], in0=G.t[:, h, 0:256],
                                                                 scalar=eprev[:, h:h + 1],
                                                                 in1=pu.t[:, (h % 2) * 256:(h % 2 + 1) * 256],
                                                                 op0=ALU.mult, op1=ALU.add),
                             reads=[G.b, eprev_b, pu.b], writes=[G.b])
                u_pair(0)
                if main:
                    for h in range(4):
                        pn = ps_num[h // 2]
                        P.op("pe", "matmul", dict(out=pn.t[:, (h % 2) * 256:(h % 2 + 1) * 256], lhsT=sb_.t[:, h, :],
                                                  rhs=vh.t[:, ch, h, 0:256], start=True, stop=False),
                             reads=[sb_.b, vh.bs[ch]], writes=[pn.b])
                        P.op("pe", "matmul", dict(out=pn.t[:, (h % 2) * 256:(h % 2 + 1) * 256], lhsT=qk.t[:, h, tok],
                                                  rhs=Cb.t[:, h, 0:256], start=False, stop=True),
                             reads=[qk.bs[h], Cb.b], writes=[pn.b])
                        P.op("pe", "matmul", dict(out=den_ps[:, h:h + 1], lhsT=sb_.t[:, h, :],
                                                  rhs=vh.t[:, ch, h, 256:257], start=True, stop=False),
                             reads=[sb_.b, vh.bs[ch]], writes=[misc.b])
                        P.op("pe", "matmul", dict(out=den_ps[:, h:h + 1], lhsT=qk.t[:, h, tok],
                                                  rhs=Cb.t[:, h, 256:257], start=False, stop=True),
                             reads=[qk.bs[h], Cb.b], writes=[misc.b])
                g_pair(0)
                u_pair(1)
                g_pair(1)
                gn = G.t[:, :, 256:257].rearrange("p a b -> p (a b)")
                P.op("dve", "tensor_tensor", dict(out=gn, in0=gn, in1=eprev, op=ALU.mult),
                     reads=[G.b, eprev_b], writes=[G.b])
                P.op("dve", "tensor_tensor", dict(out=gn, in0=gn, in1=nu_ps[:, 0:4], op=ALU.add),
                     reads=[G.b, misc.b], writes=[G.b])
                if main:
                    for q_ in range(2):
                        P.op("dve", "tensor_copy", dict(out=nr.t[:, q_ * 512:(q_ + 1) * 512], in_=ps_num[q_].t[:, :]),
                             reads=[ps_num[q_].b], writes=[nr.b])
                    P.op("dve", "tensor_copy", dict(out=dr.t[:, :], in_=den_ps[:, 0:4]),
                         reads=[misc.b], writes=[dr.b])
                yield
                return (nr, dr)

            def epilogue(ch, nr, dr):
                tok = slice(ch * 128, (ch + 1) * 128)
                hn_ = hn[cnt["e"] % 2]
                cnt["e"] += 1
                P.op("pool", "memset", dict(ap=ssq.t[:, :], constant=0.0), writes=[ssq.b])
                for h in range(4):
                    P.op("act", "activation", dict(out=junk.t[:, :], in_=nr.t[:, h * 256:(h + 1) * 256],
                                                   func=AF.Square, accum_out=ssq.t[:, h:h + 1]),
                         reads=[nr.b, ssq.b], writes=[junk.b, ssq.b])
                e_ = ebq.t[:, ch * 4:ch * 4 + 4]
                e2_ = ebq2.t[:, ch * 4:ch * 4 + 4]
                a0, a1, a2, a3 = sm
                P.op("dve", "tensor_tensor", dict(out=a0.t[:, :], in0=dr.t[:, :], in1=e_, op=ALU.mult),
                     reads=[dr.b, ebq.b], writes=[a0.b])
                P.op("dve", "tensor_tensor", dict(out=a1.t[:, :], in0=a0.t[:, :], in1=a0.t[:, :], op=ALU.mult),
                     reads=[a0.b], writes=[a1.b])
                P.op("dve", "tensor_single_scalar", dict(out=a1.t[:, :], in_=a1.t[:, :], scalar=1.0, op=ALU.max),
                     reads=[a1.b], writes=[a1.b])
                P.op("dve", "scalar_tensor_tensor", dict(out=a2.t[:, :], in0=ssq.t[:, :], scalar=1.0 / 256,
                                                         in1=e2_, op0=ALU.mult, op1=ALU.mult),
                     reads=[ssq.b, ebq2.b], writes=[a2.b])
                P.op("dve", "scalar_tensor_tensor", dict(out=a2.t[:, :], in0=a1.t[:, :], scalar=EPS,
                                                         in1=a2.t[:, :], op0=ALU.mult, op1=ALU.add),
                     reads=[a1.b, a2.b], writes=[a2.b])
                P.op("act", "activation", dict(out=a2.t[:, :], in_=a2.t[:, :], func=AF.Ln),
                     reads=[a2.b], writes=[a2.b])
                P.op("act", "activation", dict(out=a2.t[:, :], in_=a2.t[:, :], func=AF.Exp, scale=-0.5),
                     reads=[a2.b], writes=[a2.b])
                P.op("dve", "tensor_tensor", dict(out=a3.t[:, :], in0=a2.t[:, :], in1=e_, op=ALU.mult),
                     reads=[a2.b, ebq.b], writes=[a3.b])
                for h in range(4):
                    P.op("dve", "scalar_tensor_tensor", dict(out=hn_.t[:, h * 256:(h + 1) * 256],
                                                             in0=nr.t[:, h * 256:(h + 1) * 256],
                                                             scalar=a3.t[:, h:h + 1],
                                                             in1=sgo.t[:, ch, h * 256:(h + 1) * 256],
                                                             op0=ALU.mult, op1=ALU.mult),
                         reads=[nr.b, a3.b, sgo.bs[ch]], writes=[hn_.b])
                yield
                return (ch, hn_)

            def htrans(ch, hn_):
                tok = slice(ch * 128, (ch + 1) * 128)
                for j in range(8):
                    P.op("pe", "transpose", dict(out=psT4[:, j * 128:(j + 1) * 128],
                                                 in_=hn_.t[:, j * 128:(j + 1) * 128], identity=identb.t[:, :]),
                         reads=[hn_.b, identb.b], writes=[ps_s.b])
                P.op("act", "activation", dict(out=hnT.t[:, :, tok],
                                               in_=psT4[:, :].rearrange("p (a b) -> p a b", b=128),
                                               func=AF.Identity),
                     reads=[ps_s.b], writes=[hnT.bs[ch]])
                yield

            pend = None
            pend_h = None
            for ch in range(nch):
                raw = yield from state_part(ch)
                if main:
                    if pend_h is not None:
                        yield from htrans(*pend_h)
                        pend_h = None
                    if pend is not None:
                        pend_h = yield from epilogue(*pend)
                    pend = (ch, raw[0], raw[1])
            if pend_h is not None:
                yield from htrans(*pend_h)
            if pend is not None:
                pend_h = yield from epilogue(*pend)
                yield
                yield from htrans(*pend_h)
            P.op("dve", "tensor_copy", dict(out=elast.t[:, :], in_=ebl_c.t[:, (nch - 1) * 4:nch * 4]),
                 reads=[ebl_c.b], writes=[elast.b])
            if main:
                c0 = s0 + HALO
                for o in range(8):
                    po = psA[cnt["t"] % 3]
                    cnt["t"] += 1
                    slot = cnt["o"] % 2
                    cnt["o"] += 1
                    xo_, xr_ = xo[slot], xr[slot]
                    P.dma("sp", dict(out=xr_.t[:, :w], in_=x_in[o * 128:(o + 1) * 128, c0:c0 + w]),
                          writes=[xr_.b], stream="xr%d" % slot)
                    for j in range(8):
                        P.op("pe", "matmul", dict(out=po.t[:, :w], lhsT=wout.t[:, j, o * 128:(o + 1) * 128],
                                                  rhs=hnT.t[:, j, :w], start=(j == 0), stop=(j == 7)),
                             reads=[grp_buf(gout, j)] + hnT.bs[:nch], writes=[po.b])
                    if s0 < 0:
                        P.op("dve", "scalar_tensor_tensor", dict(out=xo_.t[:, :w], in0=po.t[:, :w],
                                                                 scalar=cst.t[:, mko:mko + 1], in1=xr_.t[:, :w],
                                                                 op0=ALU.mult, op1=ALU.add),
                             reads=[po.b, xr_.b, cst.b], writes=[xo_.b])
                    else:
                        P.op("dve", "tensor_tensor", dict(out=xo_.t[:, :w], in0=po.t[:, :w], in1=xr_.t[:, :w],
                                                          op=ALU.add),
                             reads=[po.b, xr_.b], writes=[xo_.b])
                    P.dma("sp", dict(out=x_out[o * 128:(o + 1) * 128, c0:c0 + w], in_=xo_.t[:, :w]),
                          reads=[xo_.b], stream="xo%d" % slot)
                    yield

        n = len(tiles)

        def run_all(g):
            for _ in g:
                pass

        def interleave(gens):
            alive = [True] * len(gens)
            while any(alive):
                for i, g in enumerate(gens):
                    if alive[i]:
                        try:
                            next(g)
                        except StopIteration:
                            alive[i] = False

        run_all(gen_g(0))
        if n > 1:
            run_all(gen_g(1))
        run_all(gen_p(0))
        for ti in range(n):
            gens = [gen_chunks(ti)]
            if ti + 1 < n:
                gens.append(gen_p(ti + 1))
            if ti + 2 < n:
                gens.append(gen_g(ti + 2))
            interleave(gens)
        P.barrier()
        P.replay()


def stage_final(P, nc, x_in, y_out, cst, co, psb):
    P.sched_on = False
    with ExitStack() as st:
        E = Env(nc, P, st)
        xts = [E.sb("xt%d" % i, (128, 8, 512), F32) for i in range(2)]
        xn = E.sb("xn", (128, 8, 512), BF16, nb=8)
        rstd = E.sb("rstd", (128, 512), F32)
        ones = E.sb("ones", (128, 128), BF16)
        yo = [E.sb("yo%d" % i, (128, 8, 512), F32) for i in range(2)]
        ps_stat = [psb[0], psb[1]]
        P.op("pool", "memset", dict(ap=ones.t[:, :], constant=1.0 / D), writes=[ones.b])
        goff, epso = co["fin_norm"], co["eps"]
        xin_v = x_in.ap().rearrange("(c p) t -> p c t", p=128)
        yout_v = y_out.ap().rearrange("(c p) t -> p c t", p=128)
        NT_ = T // 512
        w = 512

        def fload(i):
            c0 = HALO + i * 512
            P.dma("sp", dict(out=xts[i % 2].t[:, :, :w], in_=xin_v[:, :, c0:c0 + w]), writes=[xts[i % 2].b],
                  stream="fx%d" % (i % 2))

        fload(0)
        fload(1)
        for i in range(NT_):
            xt = xts[i % 2]
            y_ = yo[i % 2]
            pst = ps_stat[i % 2]
            P.op("act", "activation", dict(out=xn.t[:, :, :w], in_=xt.t[:, :, :w], func=AF.Square),
                 reads=[xt.b], writes=xn.bs)
            for c in range(8):
                P.op("pe", "matmul", dict(out=pst.t[:, :w], lhsT=ones.t[:, :], rhs=xn.t[:, c, :w],
                                          start=(c == 0), stop=(c == 7)),
                     reads=[xn.bs[c], ones.b], writes=[pst.b])
            P.op("act", "activation", dict(out=rstd.t[:, :w], in_=pst.t[:, :w], func=AF.Sqrt,
                                           bias=cst.t[:, epso:epso + 1], scale=1.0),
                 reads=[pst.b, cst.b], writes=[rstd.b])
            P.op("dve", "reciprocal", dict(out=rstd.t[:, :w], in_=rstd.t[:, :w]), reads=[rstd.b], writes=[rstd.b])
            for c in range(8):
                P.op("dve", "scalar_tensor_tensor", dict(out=y_.t[:, c, :w], in0=xt.t[:, c, :w],
                                                         scalar=cst.t[:, goff + c:goff + c + 1],
                                                         in1=rstd.t[:, :w], op0=ALU.mult, op1=ALU.mult),
                     reads=[xt.b, rstd.b, cst.b], writes=[y_.b])
            if i + 2 < NT_:
                fload(i + 2)
            P.dma("sp", dict(out=yout_v[:, :, i * 512:i * 512 + w], in_=y_.t[:, :, :w]), reads=[y_.b],
                  stream="fy%d" % (i % 2))
        P.barrier()
        P.replay()


def build_consts(inputs, core):
    cp = ConstPack()
    odd = core % 2
    cp.add("mask", np.full((128, 1), float(odd), np.float32))
    cp.add("eps", np.full((128, 1), EPS, np.float32))
    cp.add("ident", np.eye(128, dtype=np.float32))
    cp.add("maskT", np.triu(np.ones((128, 128), np.float32)))
    cp.add("lnqs", np.full((128, 1), np.log(128.0 ** -0.5), np.float32))
    cp.add("m_norm", chan_cols(inputs["norm_mix"][0]))
    mcw = np.asarray(inputs["m_conv_w"][0], np.float32)
    cp.add("m_cw", mcw.reshape(4, 8, 128).transpose(2, 1, 0).reshape(128, 32))
    cp.add("m_cb", chan_cols(inputs["m_conv_b"][0]))
    cp.add("m_bg", np.tile(np.asarray(inputs["m_b_gates"][0], np.float32)[None, :], (128, 4)))
    cp.add("m_hn", chan_cols(inputs["m_head_norm"][0]))
    cp.add("fin_norm", chan_cols(inputs["norm_final"]))
    cp.add("c_norm", chan_cols(inputs["norm_mix"][1]))
    cp.add("c_bin", chan_cols(inputs["c_b_in"][0]))
    dw = np.asarray(inputs["c_dw_w"][0], np.float32)
    cp.add("c_dw", dw.reshape(CK, 8, 128).transpose(2, 1, 0).reshape(128, 8 * CK))
    cp.add("c_db", chan_cols(inputs["c_dw_b"][0]))
    cp.add("c_lg", chan_cols(inputs["c_ln_g"][0]))
    cp.add("c_lb", chan_cols(inputs["c_ln_b"][0]))
    cp.add("c_bout", chan_cols(inputs["c_b_out"][0]))
    for l in range(2):
        cp.add("f%d_norm" % l, chan_cols(inputs["norm_ffn"][l]))
        cw = np.asarray(inputs["f_conv_w"][l], np.float32)
        cwl = cw.reshape(3, 2 * NJ, 128).transpose(2, 1, 0).reshape(128, 2 * NJ * 3)
        cp.add("f%d_cw" % l, cwl)
        cp.add("f%d_cb" % l, chan_cols(inputs["f_conv_b"][l]))
    return cp


def build_stage_prog(stage, co, ncst):
    nc = bass.Bass("TRN2", target_bir_lowering=False)
    x_in = nc.dram_tensor("x_in", [D, HALO + T], F32, kind="ExternalInput")
    x_out = nc.dram_tensor("x_out", [D, HALO + T], F32, kind="ExternalOutput")
    cst_d = nc.dram_tensor("consts", [128, ncst], F32, kind="ExternalInput")
    with ExitStack() as st:
        P = Prog(nc, st)
        E = Env(nc, P, st)
        cst = E.sb("cst", (128, ncst), F32)
        psb = [E.ps("ps%d" % i) for i in range(8)]
        P.dma("sp", dict(out=cst.t[:, :], in_=cst_d[:, :]), writes=[cst.b])
        if stage in ("f0", "f1"):
            w_up_d = nc.dram_tensor("w_up", [128, 8, 2 * DFF], F32, kind="ExternalInput")
            w_dn_d = nc.dram_tensor("w_dn", [128, NJ, D], F32, kind="ExternalInput")
            stage_ffn(P, nc, x_in, x_out, w_up_d, w_dn_d, cst, co, stage, psb)
        elif stage == "m":
            xp = nc.dram_tensor("xp", [D, NPRE], F32, kind="ExternalInput")
            w_in_d = nc.dram_tensor("w_in", [128, 8, MW], F32, kind="ExternalInput")
            w_out_d = nc.dram_tensor("w_out", [128, 8, D], F32, kind="ExternalInput")
            stage_mlstm(P, nc, x_in, xp, x_out, w_in_d, w_out_d, cst, co, psb)
        elif stage == "fin":
            stage_final(P, nc, x_in, x_out, cst, co, psb)
        elif stage == "c":
            w_in_d = nc.dram_tensor("w_in", [128, 8, 2 * D], F32, kind="ExternalInput")
            w_out_d = nc.dram_tensor("w_out", [128, 8, D], F32, kind="ExternalInput")
            stage_conf(P, nc, x_in, x_out, w_in_d, w_out_d, cst, co, psb)
        P.finish()
    return nc


def stage_meta(stage, inputs):
    cp = build_consts(inputs, 0)
    return cp.off, cp.n


def stage_inputs(stage, inputs, core):
    cp = build_consts(inputs, core)
    m = {"consts": cp.build()}
    if stage in ("f0", "f1"):
        l = int(stage[1])
        m["w_up"] = wlayout(glu_interleave(np.asarray(inputs["f_w_up"][l], np.float32), DFF))
        m["w_dn"] = wlayout(np.asarray(inputs["f_w_down"][l], np.float32))
    elif stage == "m":
        m["w_in"] = wlayout(np.asarray(inputs["m_w_in"][0], np.float32))
        m["w_out"] = wlayout(np.asarray(inputs["m_w_out"][0], np.float32))
    elif stage == "c":
        m["w_in"] = wlayout(glu_interleave(np.asarray(inputs["c_w_in"][0], np.float32), D))
        m["w_out"] = wlayout(np.asarray(inputs["c_w_out"][0], np.float32))
    return m


def core_inputs(inputs, core):
    x = np.asarray(inputs["x"], np.float32)
    b, half = core // 2, core % 2
    t0 = half * T
    x_in = np.zeros((D, HALO + T), np.float32)
    x_in[:, HALO:] = x[b, t0:t0 + T].T
    xp = np.zeros((D, NPRE), np.float32)
    if half:
        x_in[:, :HALO] = x[b, t0 - HALO:t0].T
        xp[:] = x[b, 0:NPRE].T
    return x_in, xp


WEIGHTS = (("m_w_in", "m_w_in", 0, (128, 8, MW)), ("m_w_out", "m_w_out", 0, (128, 8, D)),
           ("f0_w_up", "f_w_up", 0, (128, 8, 2 * DFF)), ("f0_w_dn", "f_w_down", 0, (128, NJ, D)),
           ("c_w_in", "c_w_in", 0, (128, 8, 2 * D)), ("c_w_out", "c_w_out", 0, (128, 8, D)),
           ("f1_w_up", "f_w_up", 1, (128, 8, 2 * DFF)), ("f1_w_dn", "f_w_down", 1, (128, NJ, D)))


def build_fused(co, ncst):
    nc = bass.Bass("TRN2", target_bir_lowering=False)
    x_in = nc.dram_tensor("x_in", [D, HALO + T], F32, kind="ExternalInput")
    xp = nc.dram_tensor("xp", [D, NPRE], F32, kind="ExternalInput")
    cst_d = nc.dram_tensor("consts", [128, ncst], F32, kind="ExternalInput")
    wd = {n: nc.dram_tensor(n, list(shp), F32, kind="ExternalInput") for (n, _, _, shp) in WEIGHTS}
    sa = nc.dram_tensor("scr_a", [D, HALO + T], F32)
    sb = nc.dram_tensor("scr_b", [D, HALO + T], F32)
    y = nc.dram_tensor("y", [D, T], F32, kind="ExternalOutput")
    with ExitStack() as st:
        P = Prog(nc, st)
        E = Env(nc, P, st)
        cst = E.sb("cst", (128, ncst), F32)
        psb = [E.ps("ps%d" % i) for i in range(8)]
        P.dma("sp", dict(out=cst.t[:, :], in_=cst_d[:, :]), writes=[cst.b])
        P.dma("sp", dict(out=sa[:, 0:4], in_=x_in[:, 0:4]))
        wb = {}
        del PREFETCH_CASTS[:]
        for (n, _, _, shp) in WEIGHTS:
            if n.startswith("m_"):
                continue
            t = nc.dram_tensor("bf_" + n, list(shp), BF16)
            rb = Buf("bf_" + n)
            wb[n] = (t, rb)
            PREFETCH_CASTS.append((t, wd[n], rb))
        stage_mlstm(P, nc, x_in, xp, sa, wd["m_w_in"], wd["m_w_out"], cst, co, psb)
        del PREFETCH_CASTS[:]
        stage_ffn(P, nc, sa, sb, wb["f0_w_up"], wb["f0_w_dn"], cst, co, "f0", psb)
        stage_conf(P, nc, sb, sa, wb["c_w_in"], wb["c_w_out"], cst, co, psb)
        stage_ffn(P, nc, sa, sb, wb["f1_w_up"], wb["f1_w_dn"], cst, co, "f1", psb)
        stage_final(P, nc, sb, y, cst, co, psb)
        P.finish()
    return nc


FUSED = True


def kernel(**inputs):
    inputs = {k: np.asarray(v) for k, v in inputs.items()}
    x = inputs["x"]
    out = np.empty((BATCH, SEQ, D), np.float32)
    cps = [build_consts(inputs, c) for c in range(NCORES)]
    co, ncst = cps[0].off, cps[0].n
    consts = [cp.build() for cp in cps]
    cores = list(range(NCORES))
    xin_xp = [core_inputs(inputs, c) for c in cores]
    if FUSED:
        wl = {}
        for (n, src, l, _) in WEIGHTS:
            wsrc = np.asarray(inputs[src][l], np.float32)
            if src == "f_w_up":
                wsrc = glu_interleave(wsrc, DFF)
            elif src == "c_w_in":
                wsrc = glu_interleave(wsrc, D)
            wl[n] = wlayout(wsrc)
        nc = build_fused(co, ncst)
        in_maps = []
        for c in cores:
            m = {"x_in": xin_xp[c][0], "xp": xin_xp[c][1], "consts": consts[c]}
            m.update(wl)
            in_maps.append(m)
        res = run_bass_kernel_spmd(nc, in_maps, core_ids=cores)
        ys = [res.results[c]["y"] for c in cores]
    else:
        cur = [xin_xp[c][0] for c in cores]
        for stage in ("m", "f0", "c", "f1", "fin"):
            nc = build_stage_prog(stage, co, ncst)
            in_maps = []
            for c in cores:
                m = stage_inputs(stage, inputs, c)
                m["consts"] = consts[c]
                m["x_in"] = cur[c]
                if stage == "m":
                    m["xp"] = xin_xp[c][1]
                in_maps.append(m)
            res = run_bass_kernel_spmd(nc, in_maps, core_ids=cores)
            cur = [np.asarray(res.results[c]["x_out"]) for c in cores]
        ys = [cur[c][:, :T] for c in cores]
    for c in cores:
        b, half = c // 2, c % 2
        out[b, half * T:(half + 1) * T, :] = np.asarray(ys[c]).T
    return out
```

```python
import numpy as np
from contextlib import ExitStack
import concourse.bass as bass
import concourse.mybir as mybir
from concourse.bass_utils import run_bass_kernel_spmd

F32 = mybir.dt.float32
BF16 = mybir.dt.bfloat16
ALU = mybir.AluOpType
AF = mybir.ActivationFunctionType

D = 1024
SEQ = 8192
BATCH = 4
NCORES = 8
T = 4096
HALO = 132
HV = 128
DFF = 2816
NJ = DFF // 128
EPS = 1e-6
CK = 31
KD = 8


class Buf:
    __slots__ = ("name", "last_w", "readers", "dma_readers", "excl")

    def __init__(self, name, excl=False):
        self.name = name
        self.last_w = None
        self.readers = []
        self.dma_readers = []
        self.excl = excl


class Op:
    __slots__ = ("eng", "fn", "deps", "odeps", "signal", "sig_idx", "is_dma", "dma_sem", "dma_val",
                 "idx", "batch", "nun", "succ", "ready", "finish")

    def __init__(self, eng, fn, is_dma=False):
        self.eng = eng
        self.fn = fn
        self.deps = []
        self.odeps = []
        self.signal = False
        self.sig_idx = None
        self.is_dma = is_dma
        self.dma_sem = None
        self.dma_val = None
        self.succ = []
        self.ready = 0.0
        self.finish = 0.0


ACT_SETS = {AF.Silu: "silu", AF.Sigmoid: "sigmoid", AF.Exp: "lnexp", AF.Ln: "lnexp", AF.Sqrt: "sqrt",
            AF.Tanh: "silu"}


def _free_elems(ap):
    n = 1
    for d in ap.shape[1:]:
        n *= d
    return n


def est_cost(op):
    if op.fn is None:
        return 0.0, 0.0
    meth, kw = op.fn
    if op.is_dma:
        nb = 1
        for d in kw["out"].shape:
            nb *= d
        return 0.08, 2.0 + nb * 4 / 180e3
    if op.eng == "pe":
        if meth == "transpose":
            return 0.07, 0.3
        n = _free_elems(kw["rhs"])
        f32 = kw["rhs"].dtype == F32
        c = max(0.035, n / 2400.0) * (4 if f32 else 1)
        return c, c + 0.12
    n = _free_elems(kw["out"]) if "out" in kw else _free_elems(kw["ap"])
    if op.eng == "act":
        c = 0.2 + n / 1200.0
    elif op.eng == "dve":
        k = 6.3 if meth == "reciprocal" else 1.0
        c = 0.08 + k * n / 960.0
    else:
        c = 0.3 + n / 400.0
    return c, c + 0.05


class Prog:
    ENGS = ("pe", "act", "dve", "pool", "sp")

    def __init__(self, nc, st):
        self.nc = nc
        self.st = st
        self.ops = {e: [] for e in self.ENGS}
        self.esem = {e: st.enter_context(nc.semaphore("s_" + e)) for e in self.ENGS}
        self.stream_sem = {}
        self.stream_cnt = {}
        self.all_dmas = []
        self.barrier_for = {}
        self.nsem = 0
        self.nops = 0
        self.batch = 0
        self.sched_on = False

    def new_sem(self, name):
        self.nsem += 1
        return self.st.enter_context(self.nc.semaphore(name))

    def _add_dep(self, op, p, kind):
        if p is None or p is op:
            return
        if p.is_dma or op.is_dma or p.eng != op.eng:
            if not p.is_dma:
                p.signal = True
            op.deps.append(p)
        elif kind == "raw" and op.eng != "pe":
            p.signal = True
            op.deps.append(p)
        else:
            op.odeps.append(p)

    def _emit(self, op, reads, writes):
        eng = op.eng
        op.idx = self.nops
        self.nops += 1
        op.batch = self.batch
        if any(b.excl for b in reads):
            writes = list(writes) + [b for b in reads if b.excl]
            reads = [b for b in reads if not b.excl]
        b = self.barrier_for.pop(eng, None)
        if b:
            for p in b:
                self._add_dep(op, p, "raw")
        for bf in reads:
            self._add_dep(op, bf.last_w, "raw")
        for bf in writes:
            lw = bf.last_w
            if lw is not None and not (op.is_dma and lw.is_dma and not bf.readers and not bf.dma_readers):
                self._add_dep(op, lw, "waw")
            for r in bf.readers:
                self._add_dep(op, r, "war")
            for r in bf.dma_readers:
                self._add_dep(op, r, "war")
        for bf in reads:
            if op.is_dma:
                bf.dma_readers.append(op)
            else:
                rd = bf.readers
                if not self.sched_on:
                    rd[:] = [r for r in rd if r.eng != eng]
                rd.append(op)
        for bf in writes:
            bf.last_w = op
            bf.readers = []
            bf.dma_readers = []
        self.ops[eng].append(op)
        return op

    def op(self, eng, meth, kw, reads=(), writes=()):
        return self._emit(Op(eng, (meth, kw)), reads, writes)

    def dma(self, eng, kw, reads=(), writes=(), stream=None):
        op = Op(eng, ("dma_start", kw), is_dma=True)
        if stream is None:
            stream = "_d%d" % len(self.stream_sem)
        if stream not in self.stream_sem:
            self.stream_sem[stream] = self.new_sem("q_" + stream)
            self.stream_cnt[stream] = 0
        self.stream_cnt[stream] += 16
        op.dma_sem = self.stream_sem[stream]
        op.dma_val = self.stream_cnt[stream]
        self.all_dmas.append(op)
        return self._emit(op, reads, writes)

    def barrier(self):
        deps = []
        for e in self.ENGS:
            for p in reversed(self.ops[e]):
                if not p.is_dma:
                    deps.append(p)
                    break
        deps += self.all_dmas
        self.all_dmas = []
        for e in self.ENGS:
            self.barrier_for[e] = list(deps)

    def finish(self, final_eng="sp"):
        self.barrier()
        self._emit(Op(final_eng, None), (), ())
        self.replay()

    def _schedule(self, done):
        import heapq
        batch = []
        for e in self.ENGS:
            batch += self.ops[e][done[e]:]
        cur = self.batch
        last_stream = {}
        for op in sorted(batch, key=lambda o: o.idx):
            if op.is_dma:
                k = id(op.dma_sem)
                if k in last_stream:
                    op.odeps.append(last_stream[k])
                last_stream[k] = op
        for op in batch:
            op.succ = []
            op.nun = 0
            op.ready = 0.0
        for op in batch:
            for p in op.deps + op.odeps:
                if p.batch == cur:
                    p.succ.append(op)
                    op.nun += 1
        cost = {}
        bl = {}
        for op in sorted(batch, key=lambda o: -o.idx):
            c = est_cost(op)
            cost[id(op)] = c
            m = 0.0
            for s_ in op.succ:
                v = bl[id(s_)]
                if v > m:
                    m = v
            bl[id(op)] = c[1] + 0.15 + m
        free = {e: 0.0 for e in self.ENGS}
        order = {e: [] for e in self.ENGS}
        rdy = {e: [] for e in self.ENGS}
        fut = {e: [] for e in self.ENGS}
        for op in batch:
            if op.nun == 0:
                heapq.heappush(fut[op.eng], (0.0, op.idx, op))
        nsched = 0
        cur_set = None
        TBL_PEN = 3.0

        def act_set(op):
            if op.eng != "act" or op.fn is None or op.is_dma:
                return None
            return ACT_SETS.get(op.fn[1].get("func"))

        total = len(batch)
        while nsched < total:
            best = None
            for e in self.ENGS:
                f_, r_ = fut[e], rdy[e]
                while f_ and f_[0][0] <= free[e] + 1e-9:
                    _, _, o = heapq.heappop(f_)
                    pri = bl[id(o)]
                    heapq.heappush(r_, (-pri, o.idx, o))
                if r_:
                    st = free[e]
                elif f_:
                    st = f_[0][0]
                else:
                    continue
                if best is None or st < best[0]:
                    best = (st, e)
            st, e = best
            if rdy[e]:
                if e == "act" and cur_set is not None:
                    cands = heapq.nsmallest(6, rdy[e])
                    pick = cands[0]
                    for c_ in cands:
                        a = act_set(c_[2])
                        if a is None or a == cur_set:
                            if -c_[0] >= -cands[0][0] - TBL_PEN:
                                pick = c_
                            break
                    rdy[e].remove(pick)
                    heapq.heapify(rdy[e])
                    op = pick[2]
                else:
                    op = heapq.heappop(rdy[e])[2]
            else:
                op = heapq.heappop(fut[e])[2]
            a = act_set(op)
            if a is not None:
                if cur_set is not None and a != cur_set:
                    st += 1.4
                cur_set = a
            busy, lat = cost[id(op)]
            free[e] = st + busy
            op.finish = st + lat
            order[e].append(op)
            nsched += 1
            for s_ in op.succ:
                s_.nun -= 1
                if op.finish + 0.15 > s_.ready:
                    s_.ready = op.finish + 0.15
                if s_.nun == 0:
                    heapq.heappush(fut[s_.eng], (s_.ready, s_.idx, s_))
        assert nsched == len(batch), (nsched, len(batch))
        for e in self.ENGS:
            self.ops[e][done[e]:] = order[e]
        return max(free.values())

    def replay(self):
        nc = self.nc
        if not hasattr(self, "sigcnt"):
            self.sigcnt = {e: 0 for e in self.ENGS}
            self.known = {e: {} for e in self.ENGS}
            self.done = {e: 0 for e in self.ENGS}
        if self.sched_on:
            self.est_us = self._schedule(self.done)
        self.batch += 1
        for e in self.ENGS:
            for op in self.ops[e][self.done[e]:]:
                if op.signal and not op.is_dma:
                    self.sigcnt[e] += 1
                    op.sig_idx = self.sigcnt[e]
        esem = self.esem

        def run(ename, eng):
            known = self.known[ename]
            for op in self.ops[ename][self.done[ename]:]:
                need = {}
                for p in op.deps:
                    if p.is_dma:
                        s, v = p.dma_sem, p.dma_val
                    else:
                        if p.sig_idx is None:
                            continue
                        s, v = esem[p.eng], p.sig_idx
                    key = id(s)
                    if known.get(key, 0) >= v:
                        continue
                    if key not in need or need[key][1] < v:
                        need[key] = (s, v)
                for key, (s, v) in need.items():
                    eng.wait_ge(s, v)
                    known[key] = v
                if op.fn is None:
                    continue
                ins = getattr(eng, op.fn[0])(**op.fn[1])
                if op.is_dma:
                    ins.then_inc(op.dma_sem, 16)
                elif op.signal:
                    ins.then_inc(esem[ename], 1)
            self.done[ename] = len(self.ops[ename])

        with nc.Block() as block:
            @block.tensor
            def _(eng):
                run("pe", eng)

            @block.scalar
            def _(eng):
                run("act", eng)

            @block.vector
            def _(eng):
                run("dve", eng)

            @block.gpsimd
            def _(eng):
                run("pool", eng)

            @block.sync
            def _(eng):
                run("sp", eng)


class Tl:
    def __init__(self, t, name, nb=1):
        self.t = t
        self.b = Buf(name)
        self.bs = [Buf("%s_%d" % (name, i)) for i in range(nb)] if nb > 1 else [self.b]


class Env:
    _n = 0

    def __init__(self, nc, P, st):
        self.nc, self.P, self.st = nc, P, st
        Env._n += 1
        self.pfx = "e%d_" % Env._n

    def sb(self, name, shape, dtype, nb=1):
        t = self.st.enter_context(self.nc.sbuf_tensor(self.pfx + name, list(shape), dtype))
        return Tl(t, name, nb)

    def ps(self, name, shape=(128, 512), dtype=F32):
        t = self.st.enter_context(self.nc.psum_tensor(name, list(shape), dtype))
        tl = Tl(t, name)
        tl.b.excl = True
        return tl


def chan_cols(v):
    v = np.asarray(v, np.float32)
    return np.ascontiguousarray(v.reshape(-1, 128).T)


class ConstPack:
    def __init__(self):
        self.cols = []
        self.off = {}
        self.n = 0

    def add(self, name, arr):
        arr = np.asarray(arr, np.float32)
        assert arr.shape[0] == 128
        arr = arr.reshape(128, -1)
        self.off[name] = self.n
        self.cols.append(arr)
        self.n += arr.shape[1]

    def build(self):
        return np.ascontiguousarray(np.concatenate(self.cols, axis=1))


def glu_interleave(w, half):
    K = w.shape[0]
    nj = half // 128
    a = w[:, :half].reshape(K, nj, 128)
    b = w[:, half:].reshape(K, nj, 128)
    return np.ascontiguousarray(np.concatenate([a, b], axis=2).reshape(K, 2 * half))


def wlayout(w):
    K, N = w.shape
    return np.ascontiguousarray(w.reshape(K // 128, 128, N).transpose(1, 0, 2))


def ffn_tiles(start):
    tiles = []
    s = start
    while s < T:
        wv = min(510, T - s)
        tiles.append((s, wv))
        s += wv
    return tiles


def load_weight(P, nc, tl, dram, nsplit, eng="pool"):
    KC = dram.shape[1]
    per = (KC + nsplit - 1) // nsplit
    groups = []
    k = 0
    gi = 0
    while k < KC:
        k1 = min(KC, k + per)
        bf = tl.bs[gi] if len(tl.bs) > 1 else tl.b
        P.dma(eng, dict(out=tl.t[:, k:k1, :], in_=dram[:, k:k1, :]), writes=[bf])
        groups.append((k, k1, bf))
        k = k1
        gi += 1
    return groups


def load_weight_cols(P, nc, tl, dram, bounds, eng="pool"):
    groups = []
    for gi, (b0, b1) in enumerate(bounds):
        bf = Buf("%s_cb%d" % (tl.b.name, gi))
        P.dma(eng, dict(out=tl.t[:, :, b0:b1], in_=dram[:, :, b0:b1]), writes=[bf])
        groups.append((b0, b1, bf))
    return groups


def grp_buf(groups, k):
    for k0, k1, bf in groups:
        if k0 <= k < k1:
            return bf
    raise KeyError(k)


def emit_norm_prep(P, xt, xn, rstd, ps_stat, ones, cst, goff, epso, w, x_src, lnexp=False):
    if x_src is not None:
        P.dma("sp", dict(out=xt.t[:, :, :w], in_=x_src), writes=[xt.b], stream="xt")
    P.op("act", "activation", dict(out=xn.t[:, :, :w], in_=xt.t[:, :, :w], func=AF.Square),
         reads=[xt.b], writes=xn.bs)
    for c in range(8):
        P.op("pe", "matmul", dict(out=ps_stat.t[:, :w], lhsT=ones.t[:, :], rhs=xn.t[:, c, :w],
                                  start=(c == 0), stop=(c == 7)),
             reads=[xn.bs[c], ones.b], writes=[ps_stat.b])
    if lnexp:
        P.op("act", "activation", dict(out=rstd.t[:, :w], in_=ps_stat.t[:, :w], func=AF.Ln,
                                       bias=cst.t[:, epso:epso + 1], scale=1.0),
             reads=[ps_stat.b, cst.b], writes=[rstd.b])
        P.op("act", "activation", dict(out=rstd.t[:, :w], in_=rstd.t[:, :w], func=AF.Exp, scale=-0.5),
             reads=[rstd.b], writes=[rstd.b])
    else:
        P.op("act", "activation", dict(out=rstd.t[:, :w], in_=ps_stat.t[:, :w], func=AF.Sqrt,
                                       bias=cst.t[:, epso:epso + 1], scale=1.0),
             reads=[ps_stat.b, cst.b], writes=[rstd.b])
        P.op("dve", "reciprocal", dict(out=rstd.t[:, :w], in_=rstd.t[:, :w]), reads=[rstd.b], writes=[rstd.b])
    for c in range(8):
        eng = "dve"
        P.op(eng, "scalar_tensor_tensor", dict(out=xn.t[:, c, :w], in0=xt.t[:, c, :w],
                                               scalar=cst.t[:, goff + c:goff + c + 1],
                                               in1=rstd.t[:, :w], op0=ALU.mult, op1=ALU.mult),
             reads=[xt.b, rstd.b, cst.b], writes=[xn.bs[c]])


def stage_ffn(P, nc, x_in, x_out, w_up_d, w_dn_d, cst, co, lname, psb):
    P.sched_on = SCHED_F
    with ExitStack() as st:
        E = Env(nc, P, st)
        wup = E.sb("wup", (128, 8, 2 * DFF), BF16, nb=8)
        wdn = E.sb("wdn", (128, NJ, D), BF16, nb=4)
        xt = E.sb("xt", (128, 8, 512), F32)
        xn = E.sb("xn", (128, 8, 512), BF16, nb=8)
        rstd = E.sb("rstd", (128, 512), F32)
        ones = E.sb("ones", (128, 128), BF16)
        ya = [E.sb("ya%d" % i, (128, 512), F32) for i in range(2)]
        yb = [E.sb("yb%d" % i, (128, 512), F32) for i in range(2)]
        sa = [E.sb("sa%d" % i, (128, 512), F32) for i in range(2)]
        g = E.sb("g", (128, NJ, 512), BF16, nb=NJ)
        xr = [E.sb("xr%d" % i, (128, 512), F32) for i in range(2)]
        xo = [E.sb("xo%d" % i, (128, 512), F32) for i in range(2)]
        psA = [psb[0], psb[1]]
        psB = [psb[2], psb[3]]
        ps_stat = psb[4]
        psD = [psb[5], psb[6], psb[7]]

        P.op("pool", "memset", dict(ap=ones.t[:, :], constant=1.0 / D), writes=[ones.b])
        gup = load_weight_cols(P, nc, wup, w_up_d, [(j * 256, min(NJ, j + 2) * 256) for j in range(0, NJ, 2)])
        gdn = load_weight(P, nc, wdn, w_dn_d, 4)

        goff = co[lname + "_norm"]
        cwo = co[lname + "_cw"]
        cbo = co[lname + "_cb"]
        mko = co["mask"]
        tiles = ffn_tiles(-34 if lname == "f0" else 0)
        xin_v = x_in.ap().rearrange("(c p) t -> p c t", p=128)

        def prep(i):
            s, wv = tiles[i]
            c0 = s + HALO - 2
            w = wv + 2
            emit_norm_prep(P, xt, xn, rstd, ps_stat, ones, cst, goff, co["eps"], w, xin_v[:, :, c0:c0 + w])

        dcount = 0
        prep(0)
        for i, (s, wv) in enumerate(tiles):
            c0 = s + HALO - 2
            w = wv + 2
            for j in range(NJ):
                pa, pb = psA[j % 2], psB[j % 2]
                for (pp, colbase) in ((pa, j * 256), (pb, j * 256 + 128)):
                    for c in range(8):
                        P.op("pe", "matmul", dict(out=pp.t[:, :w], lhsT=wup.t[:, c, colbase:colbase + 128],
                                                  rhs=xn.t[:, c, :w], start=(c == 0), stop=(c == 7)),
                             reads=[grp_buf(gup, colbase), xn.bs[c]], writes=[pp.b])
                yy = (ya[j % 2], yb[j % 2])
                for h, (pp, y) in enumerate(((pa, yy[0]), (pb, yy[1]))):
                    jj = j + h * NJ
                    P.op("act", "activation", dict(out=y.t[:, :wv], in_=pp.t[:, 2:w], func=AF.Identity,
                                                   bias=cst.t[:, cbo + jj:cbo + jj + 1],
                                                   scale=cst.t[:, cwo + jj * 3 + 2:cwo + jj * 3 + 3]),
                         reads=[pp.b, cst.b], writes=[y.b])
                    for k in (1, 0):
                        P.op("dve", "scalar_tensor_tensor", dict(
                            out=y.t[:, :wv], in0=pp.t[:, k:k + wv],
                            scalar=cst.t[:, cwo + jj * 3 + k:cwo + jj * 3 + k + 1],
                            in1=y.t[:, :wv], op0=ALU.mult, op1=ALU.add),
                            reads=[pp.b, y.b, cst.b], writes=[y.b])
                s_ = sa[j % 2]
                P.op("act", "activation", dict(out=s_.t[:, :wv], in_=yy[0].t[:, :wv], func=AF.Silu),
                     reads=[yy[0].b], writes=[s_.b])
                P.op("pool", "tensor_tensor", dict(out=g.t[:, j, :wv], in0=s_.t[:, :wv], in1=yy[1].t[:, :wv],
                                                   op=ALU.mult),
                     reads=[s_.b, yy[1].b], writes=[g.bs[j]])
            if i + 1 < len(tiles):
                prep(i + 1)
            nneg = max(0, min(wv, -s))
            for o in range(8):
                slot = dcount % 2
                pd = psD[dcount % 3]
                dcount += 1
                xr_, xo_ = xr[slot], xo[slot]
                P.dma("sp", dict(out=xr_.t[:, :wv], in_=x_in[o * 128:(o + 1) * 128, c0 + 2:c0 + 2 + wv]),
                      writes=[xr_.b], stream="xr%d" % slot)
                for k in range(NJ):
                    P.op("pe", "matmul", dict(out=pd.t[:, :wv], lhsT=wdn.t[:, k, o * 128:(o + 1) * 128],
                                              rhs=g.t[:, k, :wv], start=(k == 0), stop=(k == NJ - 1)),
                         reads=[grp_buf(gdn, k), g.bs[k]], writes=[pd.b])
                if nneg > 0:
                    P.op("dve", "scalar_tensor_tensor", dict(
                        out=xo_.t[:, :nneg], in0=pd.t[:, :nneg], scalar=cst.t[:, mko:mko + 1],
                        in1=xr_.t[:, :nneg], op0=ALU.mult, op1=ALU.add),
                        reads=[pd.b, xr_.b, cst.b], writes=[xo_.b])
                if nneg < wv:
                    P.op("dve", "tensor_tensor", dict(out=xo_.t[:, nneg:wv], in0=pd.t[:, nneg:wv],
                                                      in1=xr_.t[:, nneg:wv], op=ALU.add),
                         reads=[pd.b, xr_.b], writes=[xo_.b])
                P.dma("sp", dict(out=x_out[o * 128:(o + 1) * 128, c0 + 2:c0 + 2 + wv], in_=xo_.t[:, :wv]),
                      reads=[xo_.b], stream="xo%d" % slot)
        P.barrier()
        P.replay()


def std_tiles(halo=HV):
    tiles = [(-halo, halo)]
    s = 0
    while s < T:
        tiles.append((s, 512))
        s += 512
    return tiles


def stage_conf(P, nc, x_in, x_out, w_in_d, w_out_d, cst, co, psb):
    P.sched_on = SCHED_C
    with ExitStack() as st:
        E = Env(nc, P, st)
        win = E.sb("win", (128, 8, 2 * D), BF16, nb=8)
        wout = E.sb("wout", (128, 8, D), BF16, nb=4)
        dg = E.sb("dg", (128, 8, CK, 128), BF16, nb=8)
        xt = E.sb("xt", (128, 8, 512), F32)
        xn = E.sb("xn", (128, 8, 512), BF16, nb=8)
        rstd = E.sb("rstd", (128, 512), F32)
        ones = E.sb("ones", (128, 128), BF16)
        ub = E.sb("ub", (128, 8, 30 + 512), BF16, nb=8)
        sg = [E.sb("sg%d" % i, (128, 512), F32) for i in range(2)]
        y = E.sb("y", (128, 8, 512), F32, nb=8)
        ybf = E.sb("ybf", (128, 8, 512), BF16, nb=8)
        ysq = E.sb("ysq", (128, 8, 512), BF16, nb=8)
        zs = ybf
        mu = E.sb("mu", (128, 512), F32)
        var = E.sb("var", (128, 512), F32)
        tmp = [E.sb("tmp%d" % i, (128, 512), F32) for i in range(2)]
        xr = [E.sb("xr%d" % i, (128, 512), F32) for i in range(2)]
        xo = [E.sb("xo%d" % i, (128, 512), F32) for i in range(2)]
        psA = [psb[0], psb[1]]
        psB = [psb[2], psb[3]]
        psC = [psb[4], psb[7]]
        ps_stat, ps_mu, ps_m2 = psb[4], psb[5], psb[6]

        P.op("pool", "memset", dict(ap=ones.t[:, :], constant=1.0 / D), writes=[ones.b])
        P.op("pool", "memset", dict(ap=ub.t[:, :, 0:30], constant=0.0), writes=ub.bs)
        gin = load_weight_cols(P, nc, win, w_in_d, [(j * 512, (j + 1) * 512) for j in range(4)])
        gout = load_weight(P, nc, wout, w_out_d, 4)
        ido = co["ident"]
        dwo = co["c_dw"]
        goff = co["c_norm"]
        bio, dbo, lgo, lbo, boo, mko = co["c_bin"], co["c_db"], co["c_lg"], co["c_lb"], co["c_bout"], co["mask"]
        xin_v = x_in.ap().rearrange("(c p) t -> p c t", p=128)
        tiles = std_tiles(32)
        cnt = {"o": 0}

        def load_x(i):
            s, w = tiles[i]
            c0 = s + HALO
            P.dma("sp", dict(out=xt.t[:, :, :w], in_=xin_v[:, :, c0:c0 + w]), writes=[xt.b], stream="xt")

        def prep(i):
            s, w = tiles[i]
            emit_norm_prep(P, xt, xn, rstd, ps_stat, ones, cst, goff, co["eps"], w, None, lnexp=True)

        def phase_a(i):
            s, w = tiles[i]
            for j in range(8):
                pa, pb = psA[j % 2], psB[j % 2]
                for (pp, colbase) in ((pa, j * 256), (pb, j * 256 + 128)):
                    for c in range(8):
                        P.op("pe", "matmul", dict(out=pp.t[:, :w], lhsT=win.t[:, c, colbase:colbase + 128],
                                                  rhs=xn.t[:, c, :w], start=(c == 0), stop=(c == 7)),
                             reads=[grp_buf(gin, colbase), xn.bs[c]], writes=[pp.b])
                sg_ = sg[j % 2]
                P.op("act", "activation", dict(out=sg_.t[:, :w], in_=pb.t[:, :w], func=AF.Sigmoid,
                                               bias=cst.t[:, bio + 8 + j:bio + 9 + j], scale=1.0),
                     reads=[pb.b, cst.b], writes=[sg_.b])
                P.op("dve", "scalar_tensor_tensor", dict(out=ub.t[:, j, 30:30 + w], in0=pa.t[:, :w],
                                                         scalar=cst.t[:, bio + j:bio + j + 1], in1=sg_.t[:, :w],
                                                         op0=ALU.add, op1=ALU.mult),
                     reads=[pa.b, sg_.b, cst.b], writes=[ub.bs[j]])
                if s < 0:
                    P.op("act", "activation", dict(out=ub.t[:, j, 30:30 + w], in_=ub.t[:, j, 30:30 + w],
                                                   func=AF.Identity, scale=cst.t[:, mko:mko + 1]),
                         reads=[ub.bs[j], cst.b], writes=[ub.bs[j]])

        def phase_b(i):
            s, w = tiles[i]
            for c in range(8):
                pc = psC[c % 2]
                for k in range(KD, CK):
                    P.op("pe", "matmul", dict(out=pc.t[:, :w], lhsT=dg.t[:, c, k, :], rhs=ub.t[:, c, k:k + w],
                                              start=(k == KD), stop=(k == CK - 1)),
                         reads=[dg.bs[c], ub.bs[c]], writes=[pc.b])
                P.op("act", "activation", dict(out=y.t[:, c, :w], in_=pc.t[:, :w], func=AF.Identity,
                                               bias=cst.t[:, dbo + c:dbo + c + 1], scale=1.0),
                     reads=[pc.b, cst.b], writes=[y.bs[c]])
                for k in range(KD):
                    P.op("dve", "scalar_tensor_tensor", dict(
                        out=y.t[:, c, :w], in0=ub.t[:, c, k:k + w],
                        scalar=cst.t[:, dwo + c * CK + k:dwo + c * CK + k + 1], in1=y.t[:, c, :w],
                        op0=ALU.mult, op1=ALU.add),
                        reads=[ub.bs[c], y.bs[c], cst.b], writes=[y.bs[c]])
                P.op("act", "activation", dict(out=ysq.t[:, c, :w], in_=y.t[:, c, :w], func=AF.Square),
                     reads=[y.bs[c]], writes=[ysq.bs[c]])
                P.op("dve", "tensor_copy", dict(out=ybf.t[:, c, :w], in_=y.t[:, c, :w]),
                     reads=[y.bs[c]], writes=[ybf.bs[c]])
            P.op("pool", "tensor_copy", dict(out=ub.t[:, :, 0:30], in_=ub.t[:, :, w:w + 30]),
                 reads=ub.bs, writes=ub.bs)
            for c in range(8):
                P.op("pe", "matmul", dict(out=ps_mu.t[:, :w], lhsT=ones.t[:, :], rhs=ybf.t[:, c, :w],
                                          start=(c == 0), stop=(c == 7)),
                     reads=[ybf.bs[c], ones.b], writes=[ps_mu.b])
            for c in range(8):
                P.op("pe", "matmul", dict(out=ps_m2.t[:, :w], lhsT=ones.t[:, :], rhs=ysq.t[:, c, :w],
                                          start=(c == 0), stop=(c == 7)),
                     reads=[ysq.bs[c], ones.b], writes=[ps_m2.b])

        def phase_c_elem(i):
            s, w = tiles[i]
            P.op("act", "activation", dict(out=mu.t[:, :w], in_=ps_mu.t[:, :w], func=AF.Identity),
                 reads=[ps_mu.b], writes=[mu.b])
            P.op("dve", "scalar_tensor_tensor", dict(out=var.t[:, :w], in0=mu.t[:, :w], scalar=-1.0,
                                                     in1=mu.t[:, :w], op0=ALU.mult, op1=ALU.mult),
                 reads=[mu.b], writes=[var.b])
            P.op("dve", "tensor_tensor", dict(out=var.t[:, :w], in0=ps_m2.t[:, :w], in1=var.t[:, :w], op=ALU.add),
                 reads=[ps_m2.b, var.b], writes=[var.b])
            P.op("act", "activation", dict(out=var.t[:, :w], in_=var.t[:, :w], func=AF.Ln,
                                           bias=cst.t[:, co["eps"]:co["eps"] + 1], scale=1.0),
                 reads=[var.b, cst.b], writes=[var.b])
            P.op("act", "activation", dict(out=var.t[:, :w], in_=var.t[:, :w], func=AF.Exp, scale=-0.5),
                 reads=[var.b], writes=[var.b])
            for c in range(8):
                t_ = tmp[c % 2]
                P.op("dve", "tensor_tensor", dict(out=t_.t[:, :w], in0=y.t[:, c, :w], in1=mu.t[:, :w],
                                                  op=ALU.subtract),
                     reads=[y.bs[c], mu.b], writes=[t_.b])
                P.op("dve", "scalar_tensor_tensor", dict(out=t_.t[:, :w], in0=t_.t[:, :w],
                                                         scalar=cst.t[:, lgo + c:lgo + c + 1], in1=var.t[:, :w],
                                                         op0=ALU.mult, op1=ALU.mult),
                     reads=[t_.b, var.b, cst.b], writes=[t_.b])
                P.op("act", "activation", dict(out=zs.t[:, c, :w], in_=t_.t[:, :w], func=AF.Silu,
                                               bias=cst.t[:, lbo + c:lbo + c + 1], scale=1.0),
                     reads=[t_.b, cst.b], writes=[zs.bs[c]])

        def phase_c_pe(i):
            s, w = tiles[i]
            c0 = s + HALO
            for o in range(8):
                po = psB[o % 2]
                slot = cnt["o"] % 2
                cnt["o"] += 1
                xo_, xr_ = xo[slot], xr[slot]
                P.dma("sp", dict(out=xr_.t[:, :w], in_=x_in[o * 128:(o + 1) * 128, c0:c0 + w]),
                      writes=[xr_.b], stream="xr%d" % slot)
                for c in range(8):
                    P.op("pe", "matmul", dict(out=po.t[:, :w], lhsT=wout.t[:, c, o * 128:(o + 1) * 128],
                                              rhs=zs.t[:, c, :w], start=(c == 0), stop=(c == 7)),
                         reads=[grp_buf(gout, c), zs.bs[c]], writes=[po.b])
                if s < 0:
                    P.op("dve", "tensor_scalar", dict(out=xo_.t[:, :w], in0=po.t[:, :w],
                                                      scalar1=cst.t[:, boo + o:boo + o + 1],
                                                      scalar2=cst.t[:, mko:mko + 1], op0=ALU.add, op1=ALU.mult),
                         reads=[po.b, cst.b], writes=[xo_.b])
                    P.op("dve", "tensor_tensor", dict(out=xo_.t[:, :w], in0=xo_.t[:, :w], in1=xr_.t[:, :w],
                                                      op=ALU.add),
                         reads=[xo_.b, xr_.b], writes=[xo_.b])
                else:
                    P.op("dve", "scalar_tensor_tensor", dict(out=xo_.t[:, :w], in0=po.t[:, :w],
                                                             scalar=cst.t[:, boo + o:boo + o + 1],
                                                             in1=xr_.t[:, :w], op0=ALU.add, op1=ALU.add),
                         reads=[po.b, xr_.b, cst.b], writes=[xo_.b])
                P.dma("sp", dict(out=x_out[o * 128:(o + 1) * 128, c0:c0 + w], in_=xo_.t[:, :w]),
                      reads=[xo_.b], stream="xo%d" % slot)

        n = len(tiles)
        load_x(0)
        prep(0)
        if n > 1:
            load_x(1)
        ident_b = cst.t[:, ido:ido + 128].rearrange("p (a b) -> p a b", a=1).broadcast_to([128, CK, 128])
        for c in range(8):
            dwc = cst.t[:, dwo + c * CK:dwo + (c + 1) * CK].rearrange("p (a b) -> p a b", b=1).broadcast_to(
                [128, CK, 128])
            P.op("dve", "tensor_tensor", dict(out=dg.t[:, c, :, :], in0=ident_b, in1=dwc, op=ALU.mult),
                 reads=[cst.b], writes=[dg.bs[c]])
        phase_a(0)
        for i in range(n):
            phase_b(i)
            if i + 1 < n:
                prep(i + 1)
                if i + 2 < n:
                    load_x(i + 2)
            phase_c_elem(i)
            if i + 1 < n:
                phase_a(i + 1)
            phase_c_pe(i)
        P.barrier()
        P.replay()


SCHED_M, SCHED_C, SCHED_F = True, True, False
NPRE = T - HV
MW = 3080


def stage_mlstm(P, nc, x_in, xp, x_out, w_in_d, w_out_d, cst, co, psb):
    P.sched_on = SCHED_M
    with ExitStack() as st:
        E = Env(nc, P, st)
        win = E.sb("win", (128, 8, MW), BF16, nb=8)
        wout = E.sb("wout", (128, 8, D), BF16, nb=8)
        xt = E.sb("xt", (128, 8, 512), F32)
        xn2 = [E.sb("xn%d" % i, (128, 8, 512), BF16, nb=8) for i in range(2)]
        rstd = E.sb("rstd", (128, 512), F32)
        ones = E.sb("ones", (128, 128), BF16)
        onesf = E.sb("onesf", (128, 128), F32)
        identb = E.sb("identb", (128, 128), BF16)
        pq = E.sb("pq", (128, 8, 3 + 512), F32, nb=8)
        cv = [E.sb("cv%d" % i, (128, 512), F32) for i in range(2)]
        qk2 = [E.sb("qk%d" % i, (128, 8, 512), BF16, nb=8) for i in range(2)]
        vh2 = [E.sb("vh%d" % i, (128, 4, 4, 260), BF16, nb=4) for i in range(2)]
        sgo2 = [E.sb("sgo%d" % i, (128, 4, D), BF16, nb=4) for i in range(2)]
        gsb = E.sb("gsb", (128, 4, 8), F32)
        sp_ = E.sb("sp", (128, 4, 4), F32)
        tmpg = E.sb("tmpg", (128, 4, 4), F32)
        ek3 = [E.sb("ek%d" % i, (128, 16), F32) for i in range(3)]
        ebq_2 = [E.sb("ebq%d" % i, (128, 16), F32) for i in range(3)]
        ebq2_2 = [E.sb("ebqq%d" % i, (128, 16), F32) for i in range(3)]
        ebl = [E.sb("ebl%d" % i, (128, 16), F32) for i in range(3)]
        elast = E.sb("elast", (128, 4), F32)
        stb = [E.sb("stb0", (128, 4, 128), BF16)] * 2
        ktok = [E.sb("ktok0", (128, 4, 128), BF16)] * 2
        G = E.sb("G", (128, 4, 260), F32)
        Cb = E.sb("Cb", (128, 4, 260), BF16)
        ssq = E.sb("ssq", (128, 4), F32)
        junk = E.sb("junk", (128, 256), BF16)
        sm = [E.sb("sm%d" % i, (128, 4), F32) for i in range(4)]
        hn = [E.sb("hn0", (128, D), BF16)] * 2
        numraw = [E.sb("numraw%d" % i, (128, D), F32) for i in range(2)]
        denraw = [E.sb("denraw%d" % i, (128, 4), F32) for i in range(2)]
        hnT = E.sb("hnT", (128, 8, 512), BF16, nb=4)
        xr = [E.sb("xr%d" % i, (128, 512), F32) for i in range(2)]
        xo = [E.sb("xo%d" % i, (128, 512), F32) for i in range(2)]
        psA = [psb[0], psb[1], psb[2]]
        ps_stat = psb[7]
        misc = psb[3]
        ps_s = psb[4]
        ps_num = [psb[5], psb[6]]
        ps_U = [psb[7], psb[7]]
        g_ps = misc.t[:, 0:32]
        bbn_ps = misc.t[:, 32:48]
        tot_ps = misc.t[:, 48:64]
        den_ps = misc.t[:, 64:68]
        nu_ps = misc.t[:, 72:76]
        psT4 = psb[4].t.bitcast(BF16)

        ido, mko, epo = co["ident"], co["mask"], co["eps"]
        tro = co["maskT"]
        P.op("pool", "memset", dict(ap=ones.t[:, :], constant=1.0 / D), writes=[ones.b])
        P.op("pool", "memset", dict(ap=onesf.t[:, :], constant=1.0), writes=[onesf.b])
        P.op("pool", "memset", dict(ap=pq.t[:, :, 0:3], constant=0.0), writes=pq.bs)
        P.op("pool", "memset", dict(ap=G.t[:, :, :], constant=0.0), writes=[G.b])
        P.op("pool", "memset", dict(ap=elast.t[:, :], constant=1.0), writes=[elast.b])
        for i in range(2):
            P.op("pool", "memset", dict(ap=vh2[i].t[:, :, :, :], constant=0.0), writes=vh2[i].bs)
        P.op("dve", "tensor_copy", dict(out=identb.t[:, :], in_=cst.t[:, ido:ido + 128]),
             reads=[cst.b], writes=[identb.b])
        gin = load_weight_cols(P, nc, win, w_in_d, [(0, 512), (512, 1024), (3072, 3080), (1024, 2048), (2048, 3072)])
        gout = load_weight(P, nc, wout, w_out_d, 8)
        hno = co["m_hn"]
        for j in range(8):
            P.op("act", "activation", dict(out=wout.t[:, j, :], in_=wout.t[:, j, :], func=AF.Identity,
                                           scale=cst.t[:, hno + j:hno + j + 1]),
                 reads=[grp_buf(gout, j), cst.b], writes=[grp_buf(gout, j)])

        goff, cwo, cbo, bgo, lqo = co["m_norm"], co["m_cw"], co["m_cb"], co["m_bg"], co["lnqs"]
        xin_v = x_in.ap().rearrange("(c p) t -> p c t", p=128)
        xp_v = xp.ap().rearrange("(c p) t -> p c t", p=128)

        tiles = []
        s = 0
        while s < NPRE:
            w = min(512, NPRE - s)
            tiles.append(("pre", xp_v[:, :, s:s + w], w, s - T))
            s += w
        for (s, w) in std_tiles():
            tiles.append(("main", xin_v[:, :, s + HALO:s + HALO + w], w, s))
        cnt = {"t": 0, "o": 0, "k": 0, "e": 0}

        def gen_g(ti):
            kind, xsrc, w, s0 = tiles[ti]
            main = kind == "main"
            nch = w // 128
            xn = xn2[ti % 2]
            ek, ebq, ebq2, ebl_c = ek3[ti % 3], ebq_2[ti % 3], ebq2_2[ti % 3], ebl[ti % 3]
            if ti == 0:
                P.dma("sp", dict(out=xt.t[:, :, :w], in_=xsrc), writes=[xt.b], stream="xt")
            emit_norm_prep(P, xt, xn, rstd, ps_stat, ones, cst, goff, epo, w, None, lnexp=True)
            if ti + 1 < len(tiles):
                w2 = tiles[ti + 1][2]
                P.dma("sp", dict(out=xt.t[:, :, :w2], in_=tiles[ti + 1][1]), writes=[xt.b], stream="xt")
            yield
            for ch in range(nch):
                for c in range(8):
                    P.op("pe", "matmul", dict(out=g_ps[:, ch * 8:(ch + 1) * 8],
                                              lhsT=xn.t[:, c, ch * 128:(ch + 1) * 128], rhs=win.t[:, c, 3072:3080],
                                              start=(c == 0), stop=(c == 7)),
                         reads=[xn.bs[c], grp_buf(gin, 3072)], writes=[misc.b])
            gs3 = gsb.t[:, 0:nch, :]
            P.op("dve", "tensor_tensor", dict(out=gsb.t[:, 0:nch, :].rearrange("p a b -> p (a b)"),
                                              in0=g_ps[:, 0:nch * 8], in1=cst.t[:, bgo:bgo + nch * 8], op=ALU.add),
                 reads=[misc.b, cst.b], writes=[gsb.b])
            P.op("act", "activation", dict(out=sp_.t[:, 0:nch, :], in_=gs3[:, :, 4:8], func=AF.Exp, scale=-1.0),
                 reads=[gsb.b], writes=[sp_.b])
            P.op("act", "activation", dict(out=sp_.t[:, 0:nch, :], in_=sp_.t[:, 0:nch, :], func=AF.Ln,
                                           bias=1.0, scale=1.0),
                 reads=[sp_.b], writes=[sp_.b])
            spf = sp_.t[:, 0:nch, :].rearrange("p a b -> p (a b)")
            P.op("pe", "matmul", dict(out=bbn_ps[:, 0:nch * 4], lhsT=cst.t[:, tro:tro + 128], rhs=spf,
                                      start=True, stop=True),
                 reads=[sp_.b, cst.b], writes=[misc.b])
            P.op("pe", "matmul", dict(out=tot_ps[:, 0:nch * 4], lhsT=onesf.t[:, :], rhs=spf, start=True, stop=True),
                 reads=[sp_.b, onesf.b], writes=[misc.b])
            P.op("dve", "tensor_tensor", dict(out=tmpg.t[:, 0:nch, :], in0=gs3[:, :, 0:4],
                                              in1=bbn_ps[:, 0:nch * 4].rearrange("p (a b) -> p a b", b=4),
                                              op=ALU.add),
                 reads=[gsb.b, misc.b], writes=[tmpg.b])
            P.op("act", "activation", dict(out=ek.t[:, 0:nch * 4],
                                           in_=tmpg.t[:, 0:nch, :].rearrange("p a b -> p (a b)"), func=AF.Exp),
                 reads=[tmpg.b], writes=[ek.b])
            if s0 < 0:
                P.op("dve", "tensor_scalar", dict(out=ek.t[:, 0:nch * 4], in0=ek.t[:, 0:nch * 4],
                                                  scalar1=cst.t[:, mko:mko + 1], scalar2=None, op0=ALU.mult),
                     reads=[ek.b, cst.b], writes=[ek.b])
            P.op("act", "activation", dict(out=ebl_c.t[:, 0:nch * 4], in_=tot_ps[:, 0:nch * 4], func=AF.Exp,
                                           scale=-1.0),
                 reads=[misc.b], writes=[ebl_c.b])
            if main:
                P.op("act", "activation", dict(out=ebq.t[:, 0:nch * 4], in_=bbn_ps[:, 0:nch * 4], func=AF.Exp,
                                               scale=-1.0, bias=cst.t[:, lqo:lqo + 1]),
                     reads=[misc.b, cst.b], writes=[ebq.b])
                P.op("dve", "tensor_tensor", dict(out=ebq2.t[:, 0:nch * 4], in0=ebq.t[:, 0:nch * 4],
                                                  in1=ebq.t[:, 0:nch * 4], op=ALU.mult),
                     reads=[ebq.b], writes=[ebq2.b])
            yield

        def gen_p(ti):
            kind, xsrc, w, s0 = tiles[ti]
            main = kind == "main"
            last_pre = (not main) and tiles[ti + 1][0] == "main"
            nch = w // 128
            par = ti % 2
            xn = xn2[ti % 2]
            qk, vh, sgo = qk2[par], vh2[par], sgo2[par]
            ek = ek3[ti % 3]
            jlist = list(range(8)) if (main or last_pre) else [4, 5, 6, 7]
            for j in jlist:
                pa = psA[cnt["t"] % 3]
                cv_ = cv[cnt["t"] % 2]
                cnt["t"] += 1
                for c in range(8):
                    P.op("pe", "matmul", dict(out=pa.t[:, :w], lhsT=win.t[:, c, j * 128:(j + 1) * 128],
                                              rhs=xn.t[:, c, :w], start=(c == 0), stop=(c == 7)),
                         reads=[grp_buf(gin, j * 128), xn.bs[c]], writes=[pa.b])
                P.op("act", "activation", dict(out=pq.t[:, j, 3:3 + w], in_=pa.t[:, :w], func=AF.Identity),
                     reads=[pa.b], writes=[pq.bs[j]])
                P.op("act", "activation", dict(out=cv_.t[:, :w], in_=pa.t[:, :w], func=AF.Identity,
                                               bias=cst.t[:, cbo + j:cbo + j + 1],
                                               scale=cst.t[:, cwo + j * 4 + 3:cwo + j * 4 + 4]),
                     reads=[pa.b, cst.b], writes=[cv_.b])
                for k in (2, 1, 0):
                    P.op("dve", "scalar_tensor_tensor", dict(
                        out=cv_.t[:, :w], in0=pq.t[:, j, k:k + w],
                        scalar=cst.t[:, cwo + j * 4 + k:cwo + j * 4 + k + 1], in1=cv_.t[:, :w],
                        op0=ALU.mult, op1=ALU.add),
                        reads=[pq.bs[j], cv_.b, cst.b], writes=[cv_.b])
                P.op("act", "activation", dict(out=qk.t[:, j, :w], in_=cv_.t[:, :w], func=AF.Silu),
                     reads=[cv_.b], writes=[qk.bs[j]])
                yield
            P.op("pool", "tensor_copy", dict(out=pq.t[:, :, 0:3], in_=pq.t[:, :, w:w + 3]),
                 reads=pq.bs, writes=pq.bs)
            for ch in range(nch):
                tok = slice(ch * 128, (ch + 1) * 128)
                for half in range(2):
                    pa = psA[cnt["t"] % 3]
                    cnt["t"] += 1
                    for c in range(8):
                        P.op("pe", "matmul", dict(out=pa.t[:, :], lhsT=xn.t[:, c, tok],
                                                  rhs=win.t[:, c, 1024 + half * 512:1024 + (half + 1) * 512],
                                                  start=(c == 0), stop=(c == 7)),
                             reads=[xn.bs[c], grp_buf(gin, 1024)], writes=[pa.b])
                    for hh in range(2):
                        h = half * 2 + hh
                        if hh == 0:
                            P.op("act", "activation", dict(out=vh.t[:, ch, h, 0:256],
                                                           in_=pa.t[:, hh * 256:(hh + 1) * 256], func=AF.Identity,
                                                           scale=ek.t[:, ch * 4 + h:ch * 4 + h + 1]),
                                 reads=[pa.b, ek.b], writes=[vh.bs[ch]])
                        else:
                            P.op("dve", "tensor_scalar", dict(out=vh.t[:, ch, h, 0:256],
                                                              in0=pa.t[:, hh * 256:(hh + 1) * 256],
                                                              scalar1=ek.t[:, ch * 4 + h:ch * 4 + h + 1],
                                                              scalar2=None, op0=ALU.mult),
                                 reads=[pa.b, ek.b], writes=[vh.bs[ch]])
                    yield
                P.op("dve", "tensor_copy", dict(out=vh.t[:, ch, :, 256:257],
                                                in_=ek.t[:, ch * 4:ch * 4 + 4].rearrange("p (a b) -> p a b", b=1)),
                     reads=[ek.b], writes=[vh.bs[ch]])
                if main:
                    for half in range(2):
                        pa = psA[cnt["t"] % 3]
                        cnt["t"] += 1
                        for c in range(8):
                            P.op("pe", "matmul", dict(out=pa.t[:, :], lhsT=xn.t[:, c, tok],
                                                      rhs=win.t[:, c, 2048 + half * 512:2048 + (half + 1) * 512],
                                                      start=(c == 0), stop=(c == 7)),
                                 reads=[xn.bs[c], grp_buf(gin, 2048)], writes=[pa.b])
                        P.op("act", "activation", dict(out=sgo.t[:, ch, half * 512:(half + 1) * 512], in_=pa.t[:, :],
                                                       func=AF.Sigmoid),
                             reads=[pa.b], writes=[sgo.bs[ch]])
                        yield

        def gen_chunks(ti):
            kind, xsrc, w, s0 = tiles[ti]
            main = kind == "main"
            nch = w // 128
            par = ti % 2
            qk, vh, sgo = qk2[par], vh2[par], sgo2[par]
            ebq, ebq2, ebl_c = ebq_2[ti % 3], ebq2_2[ti % 3], ebl[ti % 3]

            def state_part(ch):
                tok = slice(ch * 128, (ch + 1) * 128)
                kt = ktok[cnt["k"] % 2]
                sb_ = stb[cnt["k"] % 2]
                nr = numraw[cnt["k"] % 2]
                dr = denraw[cnt["k"] % 2]
                cnt["k"] += 1
                if ch > 0:
                    eprev = ebl_c.t[:, (ch - 1) * 4:(ch - 1) * 4 + 4]
                    eprev_b = ebl_c.b
                else:
                    eprev = elast.t[:, 0:4]
                    eprev_b = elast.b
                for h in range(4):
                    P.op("pe", "transpose", dict(out=psT4[:, h * 128:(h + 1) * 128], in_=qk.t[:, 4 + h, tok],
                                                 identity=identb.t[:, :]),
                         reads=[qk.bs[4 + h], identb.b], writes=[ps_s.b])
                P.op("dve", "tensor_copy", dict(out=kt.t[:, :, :].rearrange("p a b -> p (a b)"), in_=psT4[:, 0:512]),
                     reads=[ps_s.b], writes=[kt.b])
                yield
                if main:
                    for h in range(4):
                        P.op("act", "activation", dict(out=Cb.t[:, h, 0:257], in_=G.t[:, h, 0:257], func=AF.Identity,
                                                       scale=eprev[:, h:h + 1]),
                             reads=[G.b, eprev_b], writes=[Cb.b])
                    for h in range(4):
                        P.op("pe", "matmul", dict(out=ps_s.t[:, h * 128:(h + 1) * 128], lhsT=qk.t[:, 4 + h, tok],
                                                  rhs=qk.t[:, h, tok], start=True, stop=True),
                             reads=[qk.bs[4 + h], qk.bs[h]], writes=[ps_s.b])
                    P.op("dve", "tensor_tensor", dict(
                        out=sb_.t[:, :, :], in0=ps_s.t[:, :].rearrange("p (a b) -> p a b", b=128),
                        in1=cst.t[:, tro:tro + 128].rearrange("p (a b) -> p a b", a=1).broadcast_to([128, 4, 128]),
                        op=ALU.mult),
                        reads=[ps_s.b, cst.b], writes=[sb_.b])
                    yield
                def u_pair(hp):
                    for h in (2 * hp, 2 * hp + 1):
                        pu = ps_U[h // 2]
                        P.op("pe", "matmul", dict(out=pu.t[:, (h % 2) * 256:(h % 2 + 1) * 256], lhsT=kt.t[:, h, :],
                                                  rhs=vh.t[:, ch, h, 0:256], start=True, stop=True),
                             reads=[kt.b, vh.bs[ch]], writes=[pu.b])
                        P.op("pe", "matmul", dict(out=nu_ps[:, h:h + 1], lhsT=kt.t[:, h, :],
                                                  rhs=vh.t[:, ch, h, 256:257], start=True, stop=True),
                             reads=[kt.b, vh.bs[ch]], writes=[misc.b])

                def g_pair(hp):
                    for h in (2 * hp, 2 * hp + 1):
                        pu = ps_U[h // 2]
                        P.op("dve", "scalar_tensor_tensor", dict(out=G.t[:, h, 0:256], in0=G.t[:, h, 0:256],
                                                                 scalar=eprev[:, h:h + 1],
                                                                 in1=pu.t[:, (h % 2) * 256:(h % 2 + 1) * 256],
                                                                 op0=ALU.mult, op1=ALU.add),
                             reads=[G.b, eprev_b, pu.b], writes=[G.b])
                u_pair(0)
                if main:
                    for h in range(4):
                        pn = ps_num[h // 2]
                        P.op("pe", "matmul", dict(out=pn.t[:, (h % 2) * 256:(h % 2 + 1) * 256], lhsT=sb_.t[:, h, :],
                                                  rhs=vh.t[:, ch, h, 0:256], start=True, stop=False),
                             reads=[sb_.b, vh.bs[ch]], writes=[pn.b])
                        P.op("pe", "matmul", dict(out=pn.t[:, (h % 2) * 256:(h % 2 + 1) * 256], lhsT=qk.t[:, h, tok],
                                                  rhs=Cb.t[:, h, 0:256], start=False, stop=True),
                             reads=[qk.bs[h], Cb.b], writes=[pn.b])
                        P.op("pe", "matmul", dict(out=den_ps[:, h:h + 1], lhsT=sb_.t[:, h, :],
                                                  rhs=vh.t[:, ch, h, 256:257], start=True, stop=False),
                             reads=[sb_.b, vh.bs[ch]], writes=[misc.b])
                        P.op("pe", "matmul", dict(out=den_ps[:, h:h + 1], lhsT=qk.t[:, h, tok],
                                                  rhs=Cb.t[:, h, 256:257], start=False, stop=True),
                             reads=[qk.bs[h], Cb.b], writes=[misc.b])
                g_pair(0)
                u_pair(1)
                g_pair(1)
                gn = G.t[:, :, 256:257].rearrange("p a b -> p (a b)")
                P.op("dve", "tensor_tensor", dict(out=gn, in0=gn, in1=eprev, op=ALU.mult),
                     reads=[G.b, eprev_b], writes=[G.b])
                P.op("dve", "tensor_tensor", dict(out=gn, in0=gn, in1=nu_ps[:, 0:4], op=ALU.add),
                     reads=[G.b, misc.b], writes=[G.b])
                if main:
                    for q_ in range(2):
                        P.op("dve", "tensor_copy", dict(out=nr.t[:, q_ * 512:(q_ + 1) * 512], in_=ps_num[q_].t[:, :]),
                             reads=[ps_num[q_].b], writes=[nr.b])
                    P.op("dve", "tensor_copy", dict(out=dr.t[:, :], in_=den_ps[:, 0:4]),
                         reads=[misc.b], writes=[dr.b])
                yield
                return (nr, dr)

            def epilogue(ch, nr, dr):
                tok = slice(ch * 128, (ch + 1) * 128)
                hn_ = hn[cnt["e"] % 2]
                cnt["e"] += 1
                P.op("pool", "memset", dict(ap=ssq.t[:, :], constant=0.0), writes=[ssq.b])
                for h in range(4):
                    P.op("act", "activation", dict(out=junk.t[:, :], in_=nr.t[:, h * 256:(h + 1) * 256],
                                                   func=AF.Square, accum_out=ssq.t[:, h:h + 1]),
                         reads=[nr.b, ssq.b], writes=[junk.b, ssq.b])
                e_ = ebq.t[:, ch * 4:ch * 4 + 4]
                e2_ = ebq2.t[:, ch * 4:ch * 4 + 4]
                a0, a1, a2, a3 = sm
                P.op("dve", "tensor_tensor", dict(out=a0.t[:, :], in0=dr.t[:, :], in1=e_, op=ALU.mult),
                     reads=[dr.b, ebq.b], writes=[a0.b])
                P.op("dve", "tensor_tensor", dict(out=a1.t[:, :], in0=a0.t[:, :], in1=a0.t[:, :], op=ALU.mult),
                     reads=[a0.b], writes=[a1.b])
                P.op("dve", "tensor_single_scalar", dict(out=a1.t[:, :], in_=a1.t[:, :], scalar=1.0, op=ALU.max),
                     reads=[a1.b], writes=[a1.b])
                P.op("dve", "scalar_tensor_tensor", dict(out=a2.t[:, :], in0=ssq.t[:, :], scalar=1.0 / 256,
                                                         in1=e2_, op0=ALU.mult, op1=ALU.mult),
                     reads=[ssq.b, ebq2.b], writes=[a2.b])
                P.op("dve", "scalar_tensor_tensor", dict(out=a2.t[:, :], in0=a1.t[:, :], scalar=EPS,
                                                         in1=a2.t[:, :], op0=ALU.mult, op1=ALU.add),
                     reads=[a1.b, a2.b], writes=[a2.b])
                P.op("act", "activation", dict(out=a2.t[:, :], in_=a2.t[:, :], func=AF.Ln),
                     reads=[a2.b], writes=[a2.b])
                P.op("act", "activation", dict(out=a2.t[:, :], in_=a2.t[:, :], func=AF.Exp, scale=-0.5),
                     reads=[a2.b], writes=[a2.b])
                P.op("dve", "tensor_tensor", dict(out=a3.t[:, :], in0=a2.t[:, :], in1=e_, op=ALU.mult),
                     reads=[a2.b, ebq.b], writes=[a3.b])
                for h in range(4):
                    P.op("dve", "scalar_tensor_tensor", dict(out=hn_.t[:, h * 256:(h + 1) * 256],
                                                             in0=nr.t[:, h * 256:(h + 1) * 256],
                                                             scalar=a3.t[:, h:h + 1],
                                                             in1=sgo.t[:, ch, h * 256:(h + 1) * 256],
                                                             op0=ALU.mult, op1=ALU.mult),
                         reads=[nr.b, a3.b, sgo.bs[ch]], writes=[hn_.b])
                yield
                return (ch, hn_)

            def htrans(ch, hn_):
                tok = slice(ch * 128, (ch + 1) * 128)
                for j in range(8):
                    P.op("pe", "transpose", dict(out=psT4[:, j * 128:(j + 1) * 128],
                                                 in_=hn_.t[:, j * 128:(j + 1) * 128], identity=identb.t[:, :]),
                         reads=[hn_.b, identb.b], writes=[ps_s.b])
                P.op("act", "activation", dict(out=hnT.t[:, :, tok],
                                               in_=psT4[:, :].rearrange("p (a b) -> p a b", b=128),
                                               func=AF.Identity),
                     reads=[ps_s.b], writes=[hnT.bs[ch]])
                yield

            pend = None
            pend_h = None
            for ch in range(nch):
                raw = yield from state_part(ch)
                if main:
                    if pend_h is not None:
                        yield from htrans(*pend_h)
                        pend_h = None
                    if pend is not None:
                        pend_h = yield from epilogue(*pend)
                    pend = (ch, raw[0], raw[1])
            if pend_h is not None:
                yield from htrans(*pend_h)
            if pend is not None:
                pend_h = yield from epilogue(*pend)
                yield
                yield from htrans(*pend_h)
            P.op("dve", "tensor_copy", dict(out=elast.t[:, :], in_=ebl_c.t[:, (nch - 1) * 4:nch * 4]),
                 reads=[ebl_c.b], writes=[elast.b])
            if main:
                c0 = s0 + HALO
                for o in range(8):
                    po = psA[cnt["t"] % 3]
                    cnt["t"] += 1
                    slot = cnt["o"] % 2
                    cnt["o"] += 1
                    xo_, xr_ = xo[slot], xr[slot]
                    P.dma("sp", dict(out=xr_.t[:, :w], in_=x_in[o * 128:(o + 1) * 128, c0:c0 + w]),
                          writes=[xr_.b], stream="xr%d" % slot)
                    for j in range(8):
                        P.op("pe", "matmul", dict(out=po.t[:, :w], lhsT=wout.t[:, j, o * 128:(o + 1) * 128],
                                                  rhs=hnT.t[:, j, :w], start=(j == 0), stop=(j == 7)),
                             reads=[grp_buf(gout, j)] + hnT.bs[:nch], writes=[po.b])
                    if s0 < 0:
                        P.op("dve", "scalar_tensor_tensor", dict(out=xo_.t[:, :w], in0=po.t[:, :w],
                                                                 scalar=cst.t[:, mko:mko + 1], in1=xr_.t[:, :w],
                                                                 op0=ALU.mult, op1=ALU.add),
                             reads=[po.b, xr_.b, cst.b], writes=[xo_.b])
                    else:
                        P.op("dve", "tensor_tensor", dict(out=xo_.t[:, :w], in0=po.t[:, :w], in1=xr_.t[:, :w],
                                                          op=ALU.add),
                             reads=[po.b, xr_.b], writes=[xo_.b])
                    P.dma("sp", dict(out=x_out[o * 128:(o + 1) * 128, c0:c0 + w], in_=xo_.t[:, :w]),
                          reads=[xo_.b], stream="xo%d" % slot)
                    yield

        n = len(tiles)

        def run_all(g):
            for _ in g:
                pass

        def interleave(gens):
            alive = [True] * len(gens)
            while any(alive):
                for i, g in enumerate(gens):
                    if alive[i]:
                        try:
                            next(g)
                        except StopIteration:
                            alive[i] = False

        run_all(gen_g(0))
        if n > 1:
            run_all(gen_g(1))
        run_all(gen_p(0))
        for ti in range(n):
            gens = [gen_chunks(ti)]
            if ti + 1 < n:
                gens.append(gen_p(ti + 1))
            if ti + 2 < n:
                gens.append(gen_g(ti + 2))
            interleave(gens)
        P.barrier()
        P.replay()


def stage_final(P, nc, x_in, y_out, cst, co, psb):
    P.sched_on = False
    with ExitStack() as st:
        E = Env(nc, P, st)
        xts = [E.sb("xt%d" % i, (128, 8, 512), F32) for i in range(2)]
        xn = E.sb("xn", (128, 8, 512), BF16, nb=8)
        rstd = E.sb("rstd", (128, 512), F32)
        ones = E.sb("ones", (128, 128), BF16)
        yo = [E.sb("yo%d" % i, (128, 8, 512), F32) for i in range(2)]
        ps_stat = [psb[0], psb[1]]
        P.op("pool", "memset", dict(ap=ones.t[:, :], constant=1.0 / D), writes=[ones.b])
        goff, epso = co["fin_norm"], co["eps"]
        xin_v = x_in.ap().rearrange("(c p) t -> p c t", p=128)
        yout_v = y_out.ap().rearrange("(c p) t -> p c t", p=128)
        NT_ = T // 512
        w = 512

        def fload(i):
            c0 = HALO + i * 512
            P.dma("sp", dict(out=xts[i % 2].t[:, :, :w], in_=xin_v[:, :, c0:c0 + w]), writes=[xts[i % 2].b],
                  stream="fx%d" % (i % 2))

        fload(0)
        fload(1)
        for i in range(NT_):
            xt = xts[i % 2]
            y_ = yo[i % 2]
            pst = ps_stat[i % 2]
            P.op("act", "activation", dict(out=xn.t[:, :, :w], in_=xt.t[:, :, :w], func=AF.Square),
                 reads=[xt.b], writes=xn.bs)
            for c in range(8):
                P.op("pe", "matmul", dict(out=pst.t[:, :w], lhsT=ones.t[:, :], rhs=xn.t[:, c, :w],
                                          start=(c == 0), stop=(c == 7)),
                     reads=[xn.bs[c], ones.b], writes=[pst.b])
            P.op("act", "activation", dict(out=rstd.t[:, :w], in_=pst.t[:, :w], func=AF.Ln,
                                           bias=cst.t[:, epso:epso + 1], scale=1.0),
                 reads=[pst.b, cst.b], writes=[rstd.b])
            P.op("act", "activation", dict(out=rstd.t[:, :w], in_=rstd.t[:, :w], func=AF.Exp, scale=-0.5),
                 reads=[rstd.b], writes=[rstd.b])
            for c in range(8):
                P.op("dve", "scalar_tensor_tensor", dict(out=y_.t[:, c, :w], in0=xt.t[:, c, :w],
                                                         scalar=cst.t[:, goff + c:goff + c + 1],
                                                         in1=rstd.t[:, :w], op0=ALU.mult, op1=ALU.mult),
                     reads=[xt.b, rstd.b, cst.b], writes=[y_.b])
            if i + 2 < NT_:
                fload(i + 2)
            P.dma("sp", dict(out=yout_v[:, :, i * 512:i * 512 + w], in_=y_.t[:, :, :w]), reads=[y_.b],
                  stream="fy%d" % (i % 2))
        P.barrier()
        P.replay()


def build_consts(inputs, core):
    cp = ConstPack()
    odd = core % 2
    cp.add("mask", np.full((128, 1), float(odd), np.float32))
    cp.add("eps", np.full((128, 1), EPS, np.float32))
    cp.add("ident", np.eye(128, dtype=np.float32))
    cp.add("maskT", np.triu(np.ones((128, 128), np.float32)))
    cp.add("lnqs", np.full((128, 1), np.log(128.0 ** -0.5), np.float32))
    cp.add("m_norm", chan_cols(inputs["norm_mix"][0]))
    mcw = np.asarray(inputs["m_conv_w"][0], np.float32)
    cp.add("m_cw", mcw.reshape(4, 8, 128).transpose(2, 1, 0).reshape(128, 32))
    cp.add("m_cb", chan_cols(inputs["m_conv_b"][0]))
    cp.add("m_bg", np.tile(np.asarray(inputs["m_b_gates"][0], np.float32)[None, :], (128, 4)))
    cp.add("m_hn", chan_cols(inputs["m_head_norm"][0]))
    cp.add("fin_norm", chan_cols(inputs["norm_final"]))
    cp.add("c_norm", chan_cols(inputs["norm_mix"][1]))
    cp.add("c_bin", chan_cols(inputs["c_b_in"][0]))
    dw = np.asarray(inputs["c_dw_w"][0], np.float32)
    cp.add("c_dw", dw.reshape(CK, 8, 128).transpose(2, 1, 0).reshape(128, 8 * CK))
    cp.add("c_db", chan_cols(inputs["c_dw_b"][0]))
    cp.add("c_lg", chan_cols(inputs["c_ln_g"][0]))
    cp.add("c_lb", chan_cols(inputs["c_ln_b"][0]))
    cp.add("c_bout", chan_cols(inputs["c_b_out"][0]))
    for l in range(2):
        cp.add("f%d_norm" % l, chan_cols(inputs["norm_ffn"][l]))
        cw = np.asarray(inputs["f_conv_w"][l], np.float32)
        cwl = cw.reshape(3, 2 * NJ, 128).transpose(2, 1, 0).reshape(128, 2 * NJ * 3)
        cp.add("f%d_cw" % l, cwl)
        cp.add("f%d_cb" % l, chan_cols(inputs["f_conv_b"][l]))
    return cp


def build_stage_prog(stage, co, ncst):
    nc = bass.Bass("TRN2", target_bir_lowering=False)
    x_in = nc.dram_tensor("x_in", [D, HALO + T], F32, kind="ExternalInput")
    x_out = nc.dram_tensor("x_out", [D, HALO + T], F32, kind="ExternalOutput")
    cst_d = nc.dram_tensor("consts", [128, ncst], F32, kind="ExternalInput")
    with ExitStack() as st:
        P = Prog(nc, st)
        E = Env(nc, P, st)
        cst = E.sb("cst", (128, ncst), F32)
        psb = [E.ps("ps%d" % i) for i in range(8)]
        P.dma("sp", dict(out=cst.t[:, :], in_=cst_d[:, :]), writes=[cst.b])
        if stage in ("f0", "f1"):
            w_up_d = nc.dram_tensor("w_up", [128, 8, 2 * DFF], F32, kind="ExternalInput")
            w_dn_d = nc.dram_tensor("w_dn", [128, NJ, D], F32, kind="ExternalInput")
            stage_ffn(P, nc, x_in, x_out, w_up_d, w_dn_d, cst, co, stage, psb)
        elif stage == "m":
            xp = nc.dram_tensor("xp", [D, NPRE], F32, kind="ExternalInput")
            w_in_d = nc.dram_tensor("w_in", [128, 8, MW], F32, kind="ExternalInput")
            w_out_d = nc.dram_tensor("w_out", [128, 8, D], F32, kind="ExternalInput")
            stage_mlstm(P, nc, x_in, xp, x_out, w_in_d, w_out_d, cst, co, psb)
        elif stage == "fin":
            stage_final(P, nc, x_in, x_out, cst, co, psb)
        elif stage == "c":
            w_in_d = nc.dram_tensor("w_in", [128, 8, 2 * D], F32, kind="ExternalInput")
            w_out_d = nc.dram_tensor("w_out", [128, 8, D], F32, kind="ExternalInput")
            stage_conf(P, nc, x_in, x_out, w_in_d, w_out_d, cst, co, psb)
        P.finish()
    return nc


def stage_meta(stage, inputs):
    cp = build_consts(inputs, 0)
    return cp.off, cp.n


def stage_inputs(stage, inputs, core):
    cp = build_consts(inputs, core)
    m = {"consts": cp.build()}
    if stage in ("f0", "f1"):
        l = int(stage[1])
        m["w_up"] = wlayout(glu_interleave(np.asarray(inputs["f_w_up"][l], np.float32), DFF))
        m["w_dn"] = wlayout(np.asarray(inputs["f_w_down"][l], np.float32))
    elif stage == "m":
        m["w_in"] = wlayout(np.asarray(inputs["m_w_in"][0], np.float32))
        m["w_out"] = wlayout(np.asarray(inputs["m_w_out"][0], np.float32))
    elif stage == "c":
        m["w_in"] = wlayout(glu_interleave(np.asarray(inputs["c_w_in"][0], np.float32), D))
        m["w_out"] = wlayout(np.asarray(inputs["c_w_out"][0], np.float32))
    return m


def core_inputs(inputs, core):
    x = np.asarray(inputs["x"], np.float32)
    b, half = core // 2, core % 2
    t0 = half * T
    x_in = np.zeros((D, HALO + T), np.float32)
    x_in[:, HALO:] = x[b, t0:t0 + T].T
    xp = np.zeros((D, NPRE), np.float32)
    if half:
        x_in[:, :HALO] = x[b, t0 - HALO:t0].T
        xp[:] = x[b, 0:NPRE].T
    return x_in, xp


WEIGHTS = (("m_w_in", "m_w_in", 0, (128, 8, MW)), ("m_w_out", "m_w_out", 0, (128, 8, D)),
           ("f0_w_up", "f_w_up", 0, (128, 8, 2 * DFF)), ("f0_w_dn", "f_w_down", 0, (128, NJ, D)),
           ("c_w_in", "c_w_in", 0, (128, 8, 2 * D)), ("c_w_out", "c_w_out", 0, (128, 8, D)),
           ("f1_w_up", "f_w_up", 1, (128, 8, 2 * DFF)), ("f1_w_dn", "f_w_down", 1, (128, NJ, D)))


def build_fused(co, ncst):
    nc = bass.Bass("TRN2", target_bir_lowering=False)
    x_in = nc.dram_tensor("x_in", [D, HALO + T], F32, kind="ExternalInput")
    xp = nc.dram_tensor("xp", [D, NPRE], F32, kind="ExternalInput")
    cst_d = nc.dram_tensor("consts", [128, ncst], F32, kind="ExternalInput")
    wd = {n: nc.dram_tensor(n, list(shp), F32, kind="ExternalInput") for (n, _, _, shp) in WEIGHTS}
    sa = nc.dram_tensor("scr_a", [D, HALO + T], F32)
    sb = nc.dram_tensor("scr_b", [D, HALO + T], F32)
    y = nc.dram_tensor("y", [D, T], F32, kind="ExternalOutput")
    with ExitStack() as st:
        P = Prog(nc, st)
        E = Env(nc, P, st)
        cst = E.sb("cst", (128, ncst), F32)
        psb = [E.ps("ps%d" % i) for i in range(8)]
        P.dma("sp", dict(out=cst.t[:, :], in_=cst_d[:, :]), writes=[cst.b])
        P.dma("sp", dict(out=sa[:, 0:4], in_=x_in[:, 0:4]))
        stage_mlstm(P, nc, x_in, xp, sa, wd["m_w_in"], wd["m_w_out"], cst, co, psb)
        stage_ffn(P, nc, sa, sb, wd["f0_w_up"], wd["f0_w_dn"], cst, co, "f0", psb)
        stage_conf(P, nc, sb, sa, wd["c_w_in"], wd["c_w_out"], cst, co, psb)
        stage_ffn(P, nc, sa, sb, wd["f1_w_up"], wd["f1_w_dn"], cst, co, "f1", psb)
        stage_final(P, nc, sb, y, cst, co, psb)
        P.finish()
    return nc


FUSED = True


def kernel(**inputs):
    inputs = {k: np.asarray(v) for k, v in inputs.items()}
    x = inputs["x"]
    out = np.empty((BATCH, SEQ, D), np.float32)
    cps = [build_consts(inputs, c) for c in range(NCORES)]
    co, ncst = cps[0].off, cps[0].n
    consts = [cp.build() for cp in cps]
    cores = list(range(NCORES))
    xin_xp = [core_inputs(inputs, c) for c in cores]
    if FUSED:
        wl = {}
        for (n, src, l, _) in WEIGHTS:
            wsrc = np.asarray(inputs[src][l], np.float32)
            if src == "f_w_up":
                wsrc = glu_interleave(wsrc, DFF)
            elif src == "c_w_in":
                wsrc = glu_interleave(wsrc, D)
            wl[n] = wlayout(wsrc)
        nc = build_fused(co, ncst)
        in_maps = []
        for c in cores:
            m = {"x_in": xin_xp[c][0], "xp": xin_xp[c][1], "consts": consts[c]}
            m.update(wl)
            in_maps.append(m)
        res = run_bass_kernel_spmd(nc, in_maps, core_ids=cores)
        ys = [res.results[c]["y"] for c in cores]
    else:
        cur = [xin_xp[c][0] for c in cores]
        for stage in ("m", "f0", "c", "f1", "fin"):
            nc = build_stage_prog(stage, co, ncst)
            in_maps = []
            for c in cores:
                m = stage_inputs(stage, inputs, c)
                m["consts"] = consts[c]
                m["x_in"] = cur[c]
                if stage == "m":
                    m["xp"] = xin_xp[c][1]
                in_maps.append(m)
            res = run_bass_kernel_spmd(nc, in_maps, core_ids=cores)
            cur = [np.asarray(res.results[c]["x_out"]) for c in cores]
        ys = [cur[c][:, :T] for c in cores]
    for c in cores:
        b, half = c // 2, c % 2
        out[b, half * T:(half + 1) * T, :] = np.asarray(ys[c]).T
    return out
```

```python
import numpy as np
from contextlib import ExitStack
import concourse.bass as bass
import concourse.mybir as mybir
from concourse.bass_utils import run_bass_kernel_spmd

F32 = mybir.dt.float32
BF16 = mybir.dt.bfloat16
ALU = mybir.AluOpType
AF = mybir.ActivationFunctionType

D = 1024
SEQ = 8192
BATCH = 4
NCORES = 8
T = 4096
HALO = 132
HV = 128
DFF = 2816
NJ = DFF // 128
EPS = 1e-6
CK = 31
KD = 9


class Buf:
    __slots__ = ("name", "last_w", "readers", "dma_readers", "excl")

    def __init__(self, name, excl=False):
        self.name = name
        self.last_w = None
        self.readers = []
        self.dma_readers = []
        self.excl = excl


class Op:
    __slots__ = ("eng", "fn", "deps", "odeps", "signal", "sig_idx", "is_dma", "dma_sem", "dma_val",
                 "idx", "batch", "nun", "succ", "ready", "finish")

    def __init__(self, eng, fn, is_dma=False):
        self.eng = eng
        self.fn = fn
        self.deps = []
        self.odeps = []
        self.signal = False
        self.sig_idx = None
        self.is_dma = is_dma
        self.dma_sem = None
        self.dma_val = None
        self.succ = []
        self.ready = 0.0
        self.finish = 0.0


ACT_SETS = {AF.Silu: "silu", AF.Sigmoid: "sigmoid", AF.Exp: "lnexp", AF.Ln: "lnexp", AF.Sqrt: "sqrt",
            AF.Tanh: "silu"}


def _free_elems(ap):
    n = 1
    for d in ap.shape[1:]:
        n *= d
    return n


def est_cost(op):
    if op.fn is None:
        return 0.0, 0.0
    meth, kw = op.fn
    if op.is_dma:
        nb = 1
        for d in kw["out"].shape:
            nb *= d
        return 0.08, 2.0 + nb * 4 / 180e3
    if op.eng == "pe":
        if meth == "transpose":
            return 0.07, 0.3
        n = _free_elems(kw["rhs"])
        f32 = kw["rhs"].dtype == F32
        c = max(0.035, n / 2400.0) * (4 if f32 else 1)
        return c, c + 0.12
    n = _free_elems(kw["out"]) if "out" in kw else _free_elems(kw["ap"])
    if op.eng == "act":
        c = 0.2 + n / 1200.0
    elif op.eng == "dve":
        k = 6.3 if meth == "reciprocal" else 1.0
        c = 0.08 + k * n / 960.0
    else:
        c = 0.3 + n / 400.0
    return c, c + 0.05


class Prog:
    ENGS = ("pe", "act", "dve", "pool", "sp")

    def __init__(self, nc, st):
        self.nc = nc
        self.st = st
        self.ops = {e: [] for e in self.ENGS}
        self.esem = {e: st.enter_context(nc.semaphore("s_" + e)) for e in self.ENGS}
        self.stream_sem = {}
        self.stream_cnt = {}
        self.all_dmas = []
        self.barrier_for = {}
        self.nsem = 0
        self.nops = 0
        self.batch = 0
        self.sched_on = False

    def new_sem(self, name):
        self.nsem += 1
        return self.st.enter_context(self.nc.semaphore(name))

    def _add_dep(self, op, p, kind):
        if p is None or p is op:
            return
        if p.is_dma or op.is_dma or p.eng != op.eng:
            if not p.is_dma:
                p.signal = True
            op.deps.append(p)
        elif kind == "raw" and op.eng != "pe":
            p.signal = True
            op.deps.append(p)
        else:
            op.odeps.append(p)

    def _emit(self, op, reads, writes):
        eng = op.eng
        op.idx = self.nops
        self.nops += 1
        op.batch = self.batch
        if any(b.excl for b in reads):
            writes = list(writes) + [b for b in reads if b.excl]
            reads = [b for b in reads if not b.excl]
        b = self.barrier_for.pop(eng, None)
        if b:
            for p in b:
                self._add_dep(op, p, "raw")
        for bf in reads:
            self._add_dep(op, bf.last_w, "raw")
        for bf in writes:
            lw = bf.last_w
            if lw is not None and not (op.is_dma and lw.is_dma and not bf.readers and not bf.dma_readers):
                self._add_dep(op, lw, "waw")
            for r in bf.readers:
                self._add_dep(op, r, "war")
            for r in bf.dma_readers:
                self._add_dep(op, r, "war")
        for bf in reads:
            if op.is_dma:
                bf.dma_readers.append(op)
            else:
                rd = bf.readers
                if not self.sched_on:
                    rd[:] = [r for r in rd if r.eng != eng]
                rd.append(op)
        for bf in writes:
            bf.last_w = op
            bf.readers = []
            bf.dma_readers = []
        self.ops[eng].append(op)
        return op

    def op(self, eng, meth, kw, reads=(), writes=()):
        return self._emit(Op(eng, (meth, kw)), reads, writes)

    def dma(self, eng, kw, reads=(), writes=(), stream=None):
        op = Op(eng, ("dma_start", kw), is_dma=True)
        if stream is None:
            stream = "_d%d" % len(self.stream_sem)
        if stream not in self.stream_sem:
            self.stream_sem[stream] = self.new_sem("q_" + stream)
            self.stream_cnt[stream] = 0
        self.stream_cnt[stream] += 16
        op.dma_sem = self.stream_sem[stream]
        op.dma_val = self.stream_cnt[stream]
        self.all_dmas.append(op)
        return self._emit(op, reads, writes)

    def barrier(self):
        deps = []
        for e in self.ENGS:
            for p in reversed(self.ops[e]):
                if not p.is_dma:
                    deps.append(p)
                    break
        deps += self.all_dmas
        self.all_dmas = []
        for e in self.ENGS:
            self.barrier_for[e] = list(deps)

    def finish(self, final_eng="sp"):
        self.barrier()
        self._emit(Op(final_eng, None), (), ())
        self.replay()

    def _schedule(self, done):
        import heapq
        batch = []
        for e in self.ENGS:
            batch += self.ops[e][done[e]:]
        cur = self.batch
        last_stream = {}
        for op in sorted(batch, key=lambda o: o.idx):
            if op.is_dma:
                k = id(op.dma_sem)
                if k in last_stream:
                    op.odeps.append(last_stream[k])
                last_stream[k] = op
        for op in batch:
            op.succ = []
            op.nun = 0
            op.ready = 0.0
        for op in batch:
            for p in op.deps + op.odeps:
                if p.batch == cur:
                    p.succ.append(op)
                    op.nun += 1
        cost = {}
        bl = {}
        for op in sorted(batch, key=lambda o: -o.idx):
            c = est_cost(op)
            cost[id(op)] = c
            m = 0.0
            for s_ in op.succ:
                v = bl[id(s_)]
                if v > m:
                    m = v
            bl[id(op)] = c[1] + 0.15 + m
        free = {e: 0.0 for e in self.ENGS}
        order = {e: [] for e in self.ENGS}
        rdy = {e: [] for e in self.ENGS}
        fut = {e: [] for e in self.ENGS}
        for op in batch:
            if op.nun == 0:
                heapq.heappush(fut[op.eng], (0.0, op.idx, op))
        nsched = 0
        cur_set = None
        TBL_PEN = 3.0

        def act_set(op):
            if op.eng != "act" or op.fn is None or op.is_dma:
                return None
            return ACT_SETS.get(op.fn[1].get("func"))

        total = len(batch)
        while nsched < total:
            best = None
            for e in self.ENGS:
                f_, r_ = fut[e], rdy[e]
                while f_ and f_[0][0] <= free[e] + 1e-9:
                    _, _, o = heapq.heappop(f_)
                    pri = bl[id(o)]
                    heapq.heappush(r_, (-pri, o.idx, o))
                if r_:
                    st = free[e]
                elif f_:
                    st = f_[0][0]
                else:
                    continue
                if best is None or st < best[0]:
                    best = (st, e)
            st, e = best
            if rdy[e]:
                if e == "act" and cur_set is not None:
                    cands = heapq.nsmallest(6, rdy[e])
                    pick = cands[0]
                    for c_ in cands:
                        a = act_set(c_[2])
                        if a is None or a == cur_set:
                            if -c_[0] >= -cands[0][0] - TBL_PEN:
                                pick = c_
                            break
                    rdy[e].remove(pick)
                    heapq.heapify(rdy[e])
                    op = pick[2]
                else:
                    op = heapq.heappop(rdy[e])[2]
            else:
                op = heapq.heappop(fut[e])[2]
            a = act_set(op)
            if a is not None:
                if cur_set is not None and a != cur_set:
                    st += 1.4
                cur_set = a
            busy, lat = cost[id(op)]
            free[e] = st + busy
            op.finish = st + lat
            order[e].append(op)
            nsched += 1
            for s_ in op.succ:
                s_.nun -= 1
                if op.finish + 0.15 > s_.ready:
                    s_.ready = op.finish + 0.15
                if s_.nun == 0:
                    heapq.heappush(fut[s_.eng], (s_.ready, s_.idx, s_))
        assert nsched == len(batch), (nsched, len(batch))
        for e in self.ENGS:
            self.ops[e][done[e]:] = order[e]
        return max(free.values())

    def replay(self):
        nc = self.nc
        if not hasattr(self, "sigcnt"):
            self.sigcnt = {e: 0 for e in self.ENGS}
            self.known = {e: {} for e in self.ENGS}
            self.done = {e: 0 for e in self.ENGS}
        if self.sched_on:
            self.est_us = self._schedule(self.done)
        self.batch += 1
        for e in self.ENGS:
            for op in self.ops[e][self.done[e]:]:
                if op.signal and not op.is_dma:
                    self.sigcnt[e] += 1
                    op.sig_idx = self.sigcnt[e]
        esem = self.esem

        def run(ename, eng):
            known = self.known[ename]
            for op in self.ops[ename][self.done[ename]:]:
                need = {}
                for p in op.deps:
                    if p.is_dma:
                        s, v = p.dma_sem, p.dma_val
                    else:
                        if p.sig_idx is None:
                            continue
                        s, v = esem[p.eng], p.sig_idx
                    key = id(s)
                    if known.get(key, 0) >= v:
                        continue
                    if key not in need or need[key][1] < v:
                        need[key] = (s, v)
                for key, (s, v) in need.items():
                    eng.wait_ge(s, v)
                    known[key] = v
                if op.fn is None:
                    continue
                ins = getattr(eng, op.fn[0])(**op.fn[1])
                if op.is_dma:
                    ins.then_inc(op.dma_sem, 16)
                elif op.signal:
                    ins.then_inc(esem[ename], 1)
            self.done[ename] = len(self.ops[ename])

        with nc.Block() as block:
            @block.tensor
            def _(eng):
                run("pe", eng)

            @block.scalar
            def _(eng):
                run("act", eng)

            @block.vector
            def _(eng):
                run("dve", eng)

            @block.gpsimd
            def _(eng):
                run("pool", eng)

            @block.sync
            def _(eng):
                run("sp", eng)


class Tl:
    def __init__(self, t, name, nb=1):
        self.t = t
        self.b = Buf(name)
        self.bs = [Buf("%s_%d" % (name, i)) for i in range(nb)] if nb > 1 else [self.b]


class Env:
    _n = 0

    def __init__(self, nc, P, st):
        self.nc, self.P, self.st = nc, P, st
        Env._n += 1
        self.pfx = "e%d_" % Env._n

    def sb(self, name, shape, dtype, nb=1):
        t = self.st.enter_context(self.nc.sbuf_tensor(self.pfx + name, list(shape), dtype))
        return Tl(t, name, nb)

    def ps(self, name, shape=(128, 512), dtype=F32):
        t = self.st.enter_context(self.nc.psum_tensor(name, list(shape), dtype))
        tl = Tl(t, name)
        tl.b.excl = True
        return tl


def chan_cols(v):
    v = np.asarray(v, np.float32)
    return np.ascontiguousarray(v.reshape(-1, 128).T)


class ConstPack:
    def __init__(self):
        self.cols = []
        self.off = {}
        self.n = 0

    def add(self, name, arr):
        arr = np.asarray(arr, np.float32)
        assert arr.shape[0] == 128
        arr = arr.reshape(128, -1)
        self.off[name] = self.n
        self.cols.append(arr)
        self.n += arr.shape[1]

    def build(self):
        return np.ascontiguousarray(np.concatenate(self.cols, axis=1))


def glu_interleave(w, half):
    K = w.shape[0]
    nj = half // 128
    a = w[:, :half].reshape(K, nj, 128)
    b = w[:, half:].reshape(K, nj, 128)
    return np.ascontiguousarray(np.concatenate([a, b], axis=2).reshape(K, 2 * half))


def wlayout(w):
    K, N = w.shape
    return np.ascontiguousarray(w.reshape(K // 128, 128, N).transpose(1, 0, 2))


def ffn_tiles(start):
    tiles = []
    s = start
    while s < T:
        wv = min(510, T - s)
        tiles.append((s, wv))
        s += wv
    return tiles


def load_weight(P, nc, tl, dram, nsplit, eng="pool"):
    KC = dram.shape[1]
    per = (KC + nsplit - 1) // nsplit
    groups = []
    k = 0
    gi = 0
    while k < KC:
        k1 = min(KC, k + per)
        bf = tl.bs[gi] if len(tl.bs) > 1 else tl.b
        P.dma(eng, dict(out=tl.t[:, k:k1, :], in_=dram[:, k:k1, :]), writes=[bf])
        groups.append((k, k1, bf))
        k = k1
        gi += 1
    return groups


def load_weight_cols(P, nc, tl, dram, bounds, eng="pool"):
    groups = []
    for gi, (b0, b1) in enumerate(bounds):
        bf = Buf("%s_cb%d" % (tl.b.name, gi))
        P.dma(eng, dict(out=tl.t[:, :, b0:b1], in_=dram[:, :, b0:b1]), writes=[bf])
        groups.append((b0, b1, bf))
    return groups


def grp_buf(groups, k):
    for k0, k1, bf in groups:
        if k0 <= k < k1:
            return bf
    raise KeyError(k)


def emit_norm_prep(P, xt, xn, rstd, ps_stat, ones, cst, goff, epso, w, x_src, lnexp=False):
    if x_src is not None:
        P.dma("sp", dict(out=xt.t[:, :, :w], in_=x_src), writes=[xt.b], stream="xt")
    P.op("act", "activation", dict(out=xn.t[:, :, :w], in_=xt.t[:, :, :w], func=AF.Square),
         reads=[xt.b], writes=xn.bs)
    for c in range(8):
        P.op("pe", "matmul", dict(out=ps_stat.t[:, :w], lhsT=ones.t[:, :], rhs=xn.t[:, c, :w],
                                  start=(c == 0), stop=(c == 7)),
             reads=[xn.bs[c], ones.b], writes=[ps_stat.b])
    if lnexp:
        P.op("act", "activation", dict(out=rstd.t[:, :w], in_=ps_stat.t[:, :w], func=AF.Ln,
                                       bias=cst.t[:, epso:epso + 1], scale=1.0),
             reads=[ps_stat.b, cst.b], writes=[rstd.b])
        P.op("act", "activation", dict(out=rstd.t[:, :w], in_=rstd.t[:, :w], func=AF.Exp, scale=-0.5),
             reads=[rstd.b], writes=[rstd.b])
    else:
        P.op("act", "activation", dict(out=rstd.t[:, :w], in_=ps_stat.t[:, :w], func=AF.Sqrt,
                                       bias=cst.t[:, epso:epso + 1], scale=1.0),
             reads=[ps_stat.b, cst.b], writes=[rstd.b])
        P.op("dve", "reciprocal", dict(out=rstd.t[:, :w], in_=rstd.t[:, :w]), reads=[rstd.b], writes=[rstd.b])
    for c in range(8):
        eng = "dve"
        P.op(eng, "scalar_tensor_tensor", dict(out=xn.t[:, c, :w], in0=xt.t[:, c, :w],
                                               scalar=cst.t[:, goff + c:goff + c + 1],
                                               in1=rstd.t[:, :w], op0=ALU.mult, op1=ALU.mult),
             reads=[xt.b, rstd.b, cst.b], writes=[xn.bs[c]])


def stage_ffn(P, nc, x_in, x_out, w_up_d, w_dn_d, cst, co, lname, psb):
    P.sched_on = SCHED_F
    with ExitStack() as st:
        E = Env(nc, P, st)
        wup = E.sb("wup", (128, 8, 2 * DFF), BF16, nb=8)
        wdn = E.sb("wdn", (128, NJ, D), BF16, nb=4)
        xt = E.sb("xt", (128, 8, 512), F32)
        xn = E.sb("xn", (128, 8, 512), BF16, nb=8)
        rstd = E.sb("rstd", (128, 512), F32)
        ones = E.sb("ones", (128, 128), BF16)
        ya = [E.sb("ya%d" % i, (128, 512), F32) for i in range(2)]
        yb = [E.sb("yb%d" % i, (128, 512), F32) for i in range(2)]
        sa = [E.sb("sa%d" % i, (128, 512), F32) for i in range(2)]
        g = E.sb("g", (128, NJ, 512), BF16, nb=NJ)
        xr = [E.sb("xr%d" % i, (128, 512), F32) for i in range(2)]
        xo = [E.sb("xo%d" % i, (128, 512), F32) for i in range(2)]
        psA = [psb[0], psb[1]]
        psB = [psb[2], psb[3]]
        ps_stat = psb[4]
        psD = [psb[5], psb[6], psb[7]]

        P.op("pool", "memset", dict(ap=ones.t[:, :], constant=1.0 / D), writes=[ones.b])
        gup = load_weight_cols(P, nc, wup, w_up_d, [(j * 256, min(NJ, j + 2) * 256) for j in range(0, NJ, 2)])
        gdn = load_weight(P, nc, wdn, w_dn_d, 4)

        goff = co[lname + "_norm"]
        cwo = co[lname + "_cw"]
        cbo = co[lname + "_cb"]
        mko = co["mask"]
        tiles = ffn_tiles(-34 if lname == "f0" else 0)
        xin_v = x_in.ap().rearrange("(c p) t -> p c t", p=128)

        def prep(i):
            s, wv = tiles[i]
            c0 = s + HALO - 2
            w = wv + 2
            emit_norm_prep(P, xt, xn, rstd, ps_stat, ones, cst, goff, co["eps"], w, xin_v[:, :, c0:c0 + w],
                           lnexp=True)

        dcount = 0
        prep(0)
        for i, (s, wv) in enumerate(tiles):
            c0 = s + HALO - 2
            w = wv + 2
            for j in range(NJ):
                pa, pb = psA[j % 2], psB[j % 2]
                for (pp, colbase) in ((pa, j * 256), (pb, j * 256 + 128)):
                    for c in range(8):
                        P.op("pe", "matmul", dict(out=pp.t[:, :w], lhsT=wup.t[:, c, colbase:colbase + 128],
                                                  rhs=xn.t[:, c, :w], start=(c == 0), stop=(c == 7)),
                             reads=[grp_buf(gup, colbase), xn.bs[c]], writes=[pp.b])
                yy = (ya[j % 2], yb[j % 2])
                for h, (pp, y) in enumerate(((pa, yy[0]), (pb, yy[1]))):
                    jj = j + h * NJ
                    P.op("act", "activation", dict(out=y.t[:, :wv], in_=pp.t[:, 2:w], func=AF.Identity,
                                                   bias=cst.t[:, cbo + jj:cbo + jj + 1],
                                                   scale=cst.t[:, cwo + jj * 3 + 2:cwo + jj * 3 + 3]),
                         reads=[pp.b, cst.b], writes=[y.b])
                    for k in (1, 0):
                        P.op("dve", "scalar_tensor_tensor", dict(
                            out=y.t[:, :wv], in0=pp.t[:, k:k + wv],
                            scalar=cst.t[:, cwo + jj * 3 + k:cwo + jj * 3 + k + 1],
                            in1=y.t[:, :wv], op0=ALU.mult, op1=ALU.add),
                            reads=[pp.b, y.b, cst.b], writes=[y.b])
                s_ = sa[j % 2]
                P.op("act", "activation", dict(out=s_.t[:, :wv], in_=yy[0].t[:, :wv], func=AF.Silu),
                     reads=[yy[0].b], writes=[s_.b])
                P.op("pool", "tensor_tensor", dict(out=g.t[:, j, :wv], in0=s_.t[:, :wv], in1=yy[1].t[:, :wv],
                                                   op=ALU.mult),
                     reads=[s_.b, yy[1].b], writes=[g.bs[j]])
            if i + 1 < len(tiles):
                prep(i + 1)
            nneg = max(0, min(wv, -s))
            for o in range(8):
                slot = dcount % 2
                pd = psD[dcount % 3]
                dcount += 1
                xr_, xo_ = xr[slot], xo[slot]
                P.dma("sp", dict(out=xr_.t[:, :wv], in_=x_in[o * 128:(o + 1) * 128, c0 + 2:c0 + 2 + wv]),
                      writes=[xr_.b], stream="xr%d" % slot)
                for k in range(NJ):
                    P.op("pe", "matmul", dict(out=pd.t[:, :wv], lhsT=wdn.t[:, k, o * 128:(o + 1) * 128],
                                              rhs=g.t[:, k, :wv], start=(k == 0), stop=(k == NJ - 1)),
                         reads=[grp_buf(gdn, k), g.bs[k]], writes=[pd.b])
                if nneg > 0:
                    P.op("dve", "scalar_tensor_tensor", dict(
                        out=xo_.t[:, :nneg], in0=pd.t[:, :nneg], scalar=cst.t[:, mko:mko + 1],
                        in1=xr_.t[:, :nneg], op0=ALU.mult, op1=ALU.add),
                        reads=[pd.b, xr_.b, cst.b], writes=[xo_.b])
                if nneg < wv:
                    P.op("dve", "tensor_tensor", dict(out=xo_.t[:, nneg:wv], in0=pd.t[:, nneg:wv],
                                                      in1=xr_.t[:, nneg:wv], op=ALU.add),
                         reads=[pd.b, xr_.b], writes=[xo_.b])
                P.dma("sp", dict(out=x_out[o * 128:(o + 1) * 128, c0 + 2:c0 + 2 + wv], in_=xo_.t[:, :wv]),
                      reads=[xo_.b], stream="xo%d" % slot)
        P.barrier()
        P.replay()


def std_tiles(halo=HV):
    tiles = [(-halo, halo)]
    s = 0
    while s < T:
        tiles.append((s, 512))
        s += 512
    return tiles


def stage_conf(P, nc, x_in, x_out, w_in_d, w_out_d, cst, co, psb):
    P.sched_on = SCHED_C
    with ExitStack() as st:
        E = Env(nc, P, st)
        win = E.sb("win", (128, 8, 2 * D), BF16, nb=8)
        wout = E.sb("wout", (128, 8, D), BF16, nb=4)
        dg = E.sb("dg", (128, 8, CK, 128), BF16, nb=8)
        xt = E.sb("xt", (128, 8, 512), F32)
        xn = E.sb("xn", (128, 8, 512), BF16, nb=8)
        rstd = E.sb("rstd", (128, 512), F32)
        ones = E.sb("ones", (128, 128), BF16)
        ub = E.sb("ub", (128, 8, 30 + 512), BF16, nb=8)
        sg = [E.sb("sg%d" % i, (128, 512), F32) for i in range(2)]
        y = E.sb("y", (128, 8, 512), F32, nb=8)
        ybf = E.sb("ybf", (128, 8, 512), BF16, nb=8)
        ysq = E.sb("ysq", (128, 8, 512), BF16, nb=8)
        zs = ybf
        mu = E.sb("mu", (128, 512), F32)
        var = E.sb("var", (128, 512), F32)
        tmp = [E.sb("tmp%d" % i, (128, 512), F32) for i in range(2)]
        xr = [E.sb("xr%d" % i, (128, 512), F32) for i in range(2)]
        xo = [E.sb("xo%d" % i, (128, 512), F32) for i in range(2)]
        psA = [psb[0], psb[1]]
        psB = [psb[2], psb[3]]
        psC = [psb[4], psb[7]]
        ps_stat, ps_mu, ps_m2 = psb[4], psb[5], psb[6]

        P.op("pool", "memset", dict(ap=ones.t[:, :], constant=1.0 / D), writes=[ones.b])
        P.op("pool", "memset", dict(ap=ub.t[:, :, 0:30], constant=0.0), writes=ub.bs)
        gin = load_weight_cols(P, nc, win, w_in_d, [(j * 512, (j + 1) * 512) for j in range(4)])
        gout = load_weight(P, nc, wout, w_out_d, 4)
        ido = co["ident"]
        dwo = co["c_dw"]
        goff = co["c_norm"]
        bio, dbo, lgo, lbo, boo, mko = co["c_bin"], co["c_db"], co["c_lg"], co["c_lb"], co["c_bout"], co["mask"]
        xin_v = x_in.ap().rearrange("(c p) t -> p c t", p=128)
        tiles = std_tiles(32)
        cnt = {"o": 0}

        def load_x(i):
            s, w = tiles[i]
            c0 = s + HALO
            P.dma("sp", dict(out=xt.t[:, :, :w], in_=xin_v[:, :, c0:c0 + w]), writes=[xt.b], stream="xt")

        def prep(i):
            s, w = tiles[i]
            emit_norm_prep(P, xt, xn, rstd, ps_stat, ones, cst, goff, co["eps"], w, None, lnexp=True)

        def phase_a(i):
            s, w = tiles[i]
            for j in range(8):
                pa, pb = psA[j % 2], psB[j % 2]
                for (pp, colbase) in ((pa, j * 256), (pb, j * 256 + 128)):
                    for c in range(8):
                        P.op("pe", "matmul", dict(out=pp.t[:, :w], lhsT=win.t[:, c, colbase:colbase + 128],
                                                  rhs=xn.t[:, c, :w], start=(c == 0), stop=(c == 7)),
                             reads=[grp_buf(gin, colbase), xn.bs[c]], writes=[pp.b])
                sg_ = sg[j % 2]
                P.op("act", "activation", dict(out=sg_.t[:, :w], in_=pb.t[:, :w], func=AF.Sigmoid,
                                               bias=cst.t[:, bio + 8 + j:bio + 9 + j], scale=1.0),
                     reads=[pb.b, cst.b], writes=[sg_.b])
                P.op("dve", "scalar_tensor_tensor", dict(out=ub.t[:, j, 30:30 + w], in0=pa.t[:, :w],
                                                         scalar=cst.t[:, bio + j:bio + j + 1], in1=sg_.t[:, :w],
                                                         op0=ALU.add, op1=ALU.mult),
                     reads=[pa.b, sg_.b, cst.b], writes=[ub.bs[j]])
                if s < 0:
                    P.op("act", "activation", dict(out=ub.t[:, j, 30:30 + w], in_=ub.t[:, j, 30:30 + w],
                                                   func=AF.Identity, scale=cst.t[:, mko:mko + 1]),
                         reads=[ub.bs[j], cst.b], writes=[ub.bs[j]])

        def phase_b(i):
            s, w = tiles[i]
            for c in range(8):
                pc = psC[c % 2]
                for k in range(KD, CK):
                    P.op("pe", "matmul", dict(out=pc.t[:, :w], lhsT=dg.t[:, c, k, :], rhs=ub.t[:, c, k:k + w],
                                              start=(k == KD), stop=(k == CK - 1)),
                         reads=[dg.bs[c], ub.bs[c]], writes=[pc.b])
                P.op("act", "activation", dict(out=y.t[:, c, :w], in_=pc.t[:, :w], func=AF.Identity,
                                               bias=cst.t[:, dbo + c:dbo + c + 1], scale=1.0),
                     reads=[pc.b, cst.b], writes=[y.bs[c]])
                for k in range(KD):
                    P.op("dve", "scalar_tensor_tensor", dict(
                        out=y.t[:, c, :w], in0=ub.t[:, c, k:k + w],
                        scalar=cst.t[:, dwo + c * CK + k:dwo + c * CK + k + 1], in1=y.t[:, c, :w],
                        op0=ALU.mult, op1=ALU.add),
                        reads=[ub.bs[c], y.bs[c], cst.b], writes=[y.bs[c]])
                P.op("act", "activation", dict(out=ysq.t[:, c, :w], in_=y.t[:, c, :w], func=AF.Square),
                     reads=[y.bs[c]], writes=[ysq.bs[c]])
                P.op("dve", "tensor_copy", dict(out=ybf.t[:, c, :w], in_=y.t[:, c, :w]),
                     reads=[y.bs[c]], writes=[ybf.bs[c]])
            P.op("pool", "tensor_copy", dict(out=ub.t[:, :, 0:30], in_=ub.t[:, :, w:w + 30]),
                 reads=ub.bs, writes=ub.bs)
            for c in range(8):
                P.op("pe", "matmul", dict(out=ps_mu.t[:, :w], lhsT=ones.t[:, :], rhs=ybf.t[:, c, :w],
                                          start=(c == 0), stop=(c == 7)),
                     reads=[ybf.bs[c], ones.b], writes=[ps_mu.b])
            for c in range(8):
                P.op("pe", "matmul", dict(out=ps_m2.t[:, :w], lhsT=ones.t[:, :], rhs=ysq.t[:, c, :w],
                                          start=(c == 0), stop=(c == 7)),
                     reads=[ysq.bs[c], ones.b], writes=[ps_m2.b])

        def phase_c_elem(i):
            s, w = tiles[i]
            P.op("act", "activation", dict(out=mu.t[:, :w], in_=ps_mu.t[:, :w], func=AF.Identity),
                 reads=[ps_mu.b], writes=[mu.b])
            P.op("dve", "scalar_tensor_tensor", dict(out=var.t[:, :w], in0=mu.t[:, :w], scalar=-1.0,
                                                     in1=mu.t[:, :w], op0=ALU.mult, op1=ALU.mult),
                 reads=[mu.b], writes=[var.b])
            P.op("dve", "tensor_tensor", dict(out=var.t[:, :w], in0=ps_m2.t[:, :w], in1=var.t[:, :w], op=ALU.add),
                 reads=[ps_m2.b, var.b], writes=[var.b])
            P.op("act", "activation", dict(out=var.t[:, :w], in_=var.t[:, :w], func=AF.Ln,
                                           bias=cst.t[:, co["eps"]:co["eps"] + 1], scale=1.0),
                 reads=[var.b, cst.b], writes=[var.b])
            P.op("act", "activation", dict(out=var.t[:, :w], in_=var.t[:, :w], func=AF.Exp, scale=-0.5),
                 reads=[var.b], writes=[var.b])
            for c in range(8):
                t_ = tmp[c % 2]
                P.op("dve", "tensor_tensor", dict(out=t_.t[:, :w], in0=y.t[:, c, :w], in1=mu.t[:, :w],
                                                  op=ALU.subtract),
                     reads=[y.bs[c], mu.b], writes=[t_.b])
                P.op("dve", "scalar_tensor_tensor", dict(out=t_.t[:, :w], in0=t_.t[:, :w],
                                                         scalar=cst.t[:, lgo + c:lgo + c + 1], in1=var.t[:, :w],
                                                         op0=ALU.mult, op1=ALU.mult),
                     reads=[t_.b, var.b, cst.b], writes=[t_.b])
                P.op("act", "activation", dict(out=zs.t[:, c, :w], in_=t_.t[:, :w], func=AF.Silu,
                                               bias=cst.t[:, lbo + c:lbo + c + 1], scale=1.0),
                     reads=[t_.b, cst.b], writes=[zs.bs[c]])

        def phase_c_pe(i):
            s, w = tiles[i]
            c0 = s + HALO
            for o in range(8):
                po = psB[o % 2]
                slot = cnt["o"] % 2
                cnt["o"] += 1
                xo_, xr_ = xo[slot], xr[slot]
                P.dma("sp", dict(out=xr_.t[:, :w], in_=x_in[o * 128:(o + 1) * 128, c0:c0 + w]),
                      writes=[xr_.b], stream="xr%d" % slot)
                for c in range(8):
                    P.op("pe", "matmul", dict(out=po.t[:, :w], lhsT=wout.t[:, c, o * 128:(o + 1) * 128],
                                              rhs=zs.t[:, c, :w], start=(c == 0), stop=(c == 7)),
                         reads=[grp_buf(gout, c), zs.bs[c]], writes=[po.b])
                if s < 0:
                    P.op("dve", "tensor_scalar", dict(out=xo_.t[:, :w], in0=po.t[:, :w],
                                                      scalar1=cst.t[:, boo + o:boo + o + 1],
                                                      scalar2=cst.t[:, mko:mko + 1], op0=ALU.add, op1=ALU.mult),
                         reads=[po.b, cst.b], writes=[xo_.b])
                    P.op("dve", "tensor_tensor", dict(out=xo_.t[:, :w], in0=xo_.t[:, :w], in1=xr_.t[:, :w],
                                                      op=ALU.add),
                         reads=[xo_.b, xr_.b], writes=[xo_.b])
                else:
                    P.op("dve", "scalar_tensor_tensor", dict(out=xo_.t[:, :w], in0=po.t[:, :w],
                                                             scalar=cst.t[:, boo + o:boo + o + 1],
                                                             in1=xr_.t[:, :w], op0=ALU.add, op1=ALU.add),
                         reads=[po.b, xr_.b, cst.b], writes=[xo_.b])
                P.dma("sp", dict(out=x_out[o * 128:(o + 1) * 128, c0:c0 + w], in_=xo_.t[:, :w]),
                      reads=[xo_.b], stream="xo%d" % slot)

        n = len(tiles)
        load_x(0)
        prep(0)
        if n > 1:
            load_x(1)
        ident_b = cst.t[:, ido:ido + 128].rearrange("p (a b) -> p a b", a=1).broadcast_to([128, CK, 128])
        for c in range(8):
            dwc = cst.t[:, dwo + c * CK:dwo + (c + 1) * CK].rearrange("p (a b) -> p a b", b=1).broadcast_to(
                [128, CK, 128])
            P.op("dve", "tensor_tensor", dict(out=dg.t[:, c, :, :], in0=ident_b, in1=dwc, op=ALU.mult),
                 reads=[cst.b], writes=[dg.bs[c]])
        phase_a(0)
        for i in range(n):
            phase_b(i)
            if i + 1 < n:
                prep(i + 1)
                if i + 2 < n:
                    load_x(i + 2)
            phase_c_elem(i)
            if i + 1 < n:
                phase_a(i + 1)
            phase_c_pe(i)
        P.barrier()
        P.replay()


SCHED_M, SCHED_C, SCHED_F = True, True, False
NPRE = T - HV
MW = 3080


def stage_mlstm(P, nc, x_in, xp, x_out, w_in_d, w_out_d, cst, co, psb):
    P.sched_on = SCHED_M
    with ExitStack() as st:
        E = Env(nc, P, st)
        win = E.sb("win", (128, 8, MW), BF16, nb=8)
        wout = E.sb("wout", (128, 8, D), BF16, nb=8)
        xt = E.sb("xt", (128, 8, 512), F32)
        xn2 = [E.sb("xn%d" % i, (128, 8, 512), BF16, nb=8) for i in range(2)]
        rstd = E.sb("rstd", (128, 512), F32)
        ones = E.sb("ones", (128, 128), BF16)
        onesf = E.sb("onesf", (128, 128), F32)
        identb = E.sb("identb", (128, 128), BF16)
        pq = E.sb("pq", (128, 8, 3 + 512), F32, nb=8)
        cv = [E.sb("cv%d" % i, (128, 512), F32) for i in range(2)]
        qk2 = [E.sb("qk%d" % i, (128, 8, 512), BF16, nb=8) for i in range(2)]
        vh2 = [E.sb("vh%d" % i, (128, 4, 4, 260), BF16, nb=4) for i in range(2)]
        sgo2 = [E.sb("sgo%d" % i, (128, 4, D), BF16, nb=4) for i in range(2)]
        gsb = E.sb("gsb", (128, 4, 8), F32)
        sp_ = E.sb("sp", (128, 4, 4), F32)
        tmpg = E.sb("tmpg", (128, 4, 4), F32)
        ek3 = [E.sb("ek%d" % i, (128, 16), F32) for i in range(3)]
        ebq_2 = [E.sb("ebq%d" % i, (128, 16), F32) for i in range(3)]
        ebq2_2 = [E.sb("ebqq%d" % i, (128, 16), F32) for i in range(3)]
        ebl = [E.sb("ebl%d" % i, (128, 16), F32) for i in range(3)]
        elast = E.sb("elast", (128, 4), F32)
        stb = [E.sb("stb0", (128, 4, 128), BF16)] * 2
        ktok = [E.sb("ktok0", (128, 4, 128), BF16)] * 2
        G = E.sb("G", (128, 4, 260), F32)
        Cb = E.sb("Cb", (128, 4, 260), BF16)
        ssq = E.sb("ssq", (128, 4), F32)
        junk = E.sb("junk", (128, 256), BF16)
        sm = [E.sb("sm%d" % i, (128, 4), F32) for i in range(4)]
        hn = [E.sb("hn0", (128, D), BF16)] * 2
        numraw = [E.sb("numraw%d" % i, (128, D), F32) for i in range(2)]
        denraw = [E.sb("denraw%d" % i, (128, 4), F32) for i in range(2)]
        hnT = E.sb("hnT", (128, 8, 512), BF16, nb=4)
        xr = [E.sb("xr%d" % i, (128, 512), F32) for i in range(2)]
        xo = [E.sb("xo%d" % i, (128, 512), F32) for i in range(2)]
        psA = [psb[0], psb[1], psb[2]]
        ps_stat = psb[7]
        misc = psb[3]
        ps_s = psb[4]
        ps_num = [psb[5], psb[6]]
        ps_U = [psb[7], psb[7]]
        g_ps = misc.t[:, 0:32]
        bbn_ps = misc.t[:, 32:48]
        tot_ps = misc.t[:, 48:64]
        den_ps = misc.t[:, 64:68]
        nu_ps = misc.t[:, 72:76]
        psT4 = psb[4].t.bitcast(BF16)

        ido, mko, epo = co["ident"], co["mask"], co["eps"]
        tro = co["maskT"]
        P.op("pool", "memset", dict(ap=ones.t[:, :], constant=1.0 / D), writes=[ones.b])
        P.op("pool", "memset", dict(ap=onesf.t[:, :], constant=1.0), writes=[onesf.b])
        P.op("pool", "memset", dict(ap=pq.t[:, :, 0:3], constant=0.0), writes=pq.bs)
        P.op("pool", "memset", dict(ap=G.t[:, :, :], constant=0.0), writes=[G.b])
        P.op("pool", "memset", dict(ap=elast.t[:, :], constant=1.0), writes=[elast.b])
        for i in range(2):
            P.op("pool", "memset", dict(ap=vh2[i].t[:, :, :, :], constant=0.0), writes=vh2[i].bs)
        P.op("dve", "tensor_copy", dict(out=identb.t[:, :], in_=cst.t[:, ido:ido + 128]),
             reads=[cst.b], writes=[identb.b])
        gin = load_weight_cols(P, nc, win, w_in_d, [(0, 512), (512, 1024), (3072, 3080), (1024, 2048), (2048, 3072)])
        gout = load_weight(P, nc, wout, w_out_d, 8)
        hno = co["m_hn"]
        for j in range(8):
            P.op("act", "activation", dict(out=wout.t[:, j, :], in_=wout.t[:, j, :], func=AF.Identity,
                                           scale=cst.t[:, hno + j:hno + j + 1]),
                 reads=[grp_buf(gout, j), cst.b], writes=[grp_buf(gout, j)])

        goff, cwo, cbo, bgo, lqo = co["m_norm"], co["m_cw"], co["m_cb"], co["m_bg"], co["lnqs"]
        xin_v = x_in.ap().rearrange("(c p) t -> p c t", p=128)
        xp_v = xp.ap().rearrange("(c p) t -> p c t", p=128)

        tiles = []
        s = 0
        while s < NPRE:
            w = min(512, NPRE - s)
            tiles.append(("pre", xp_v[:, :, s:s + w], w, s - T))
            s += w
        for (s, w) in std_tiles():
            tiles.append(("main", xin_v[:, :, s + HALO:s + HALO + w], w, s))
        cnt = {"t": 0, "o": 0, "k": 0, "e": 0}

        def gen_g(ti):
            kind, xsrc, w, s0 = tiles[ti]
            main = kind == "main"
            nch = w // 128
            xn = xn2[ti % 2]
            ek, ebq, ebq2, ebl_c = ek3[ti % 3], ebq_2[ti % 3], ebq2_2[ti % 3], ebl[ti % 3]
            if ti == 0:
                P.dma("sp", dict(out=xt.t[:, :, :w], in_=xsrc), writes=[xt.b], stream="xt")
            emit_norm_prep(P, xt, xn, rstd, ps_stat, ones, cst, goff, epo, w, None, lnexp=True)
            if ti + 1 < len(tiles):
                w2 = tiles[ti + 1][2]
                P.dma("sp", dict(out=xt.t[:, :, :w2], in_=tiles[ti + 1][1]), writes=[xt.b], stream="xt")
            yield
            for ch in range(nch):
                for c in range(8):
                    P.op("pe", "matmul", dict(out=g_ps[:, ch * 8:(ch + 1) * 8],
                                              lhsT=xn.t[:, c, ch * 128:(ch + 1) * 128], rhs=win.t[:, c, 3072:3080],
                                              start=(c == 0), stop=(c == 7)),
                         reads=[xn.bs[c], grp_buf(gin, 3072)], writes=[misc.b])
            gs3 = gsb.t[:, 0:nch, :]
            P.op("dve", "tensor_tensor", dict(out=gsb.t[:, 0:nch, :].rearrange("p a b -> p (a b)"),
                                              in0=g_ps[:, 0:nch * 8], in1=cst.t[:, bgo:bgo + nch * 8], op=ALU.add),
                 reads=[misc.b, cst.b], writes=[gsb.b])
            P.op("act", "activation", dict(out=sp_.t[:, 0:nch, :], in_=gs3[:, :, 4:8], func=AF.Exp, scale=-1.0),
                 reads=[gsb.b], writes=[sp_.b])
            P.op("act", "activation", dict(out=sp_.t[:, 0:nch, :], in_=sp_.t[:, 0:nch, :], func=AF.Ln,
                                           bias=1.0, scale=1.0),
                 reads=[sp_.b], writes=[sp_.b])
            spf = sp_.t[:, 0:nch, :].rearrange("p a b -> p (a b)")
            P.op("pe", "matmul", dict(out=bbn_ps[:, 0:nch * 4], lhsT=cst.t[:, tro:tro + 128], rhs=spf,
                                      start=True, stop=True),
                 reads=[sp_.b, cst.b], writes=[misc.b])
            P.op("pe", "matmul", dict(out=tot_ps[:, 0:nch * 4], lhsT=onesf.t[:, :], rhs=spf, start=True, stop=True),
                 reads=[sp_.b, onesf.b], writes=[misc.b])
            P.op("dve", "tensor_tensor", dict(out=tmpg.t[:, 0:nch, :], in0=gs3[:, :, 0:4],
                                              in1=bbn_ps[:, 0:nch * 4].rearrange("p (a b) -> p a b", b=4),
                                              op=ALU.add),
                 reads=[gsb.b, misc.b], writes=[tmpg.b])
            P.op("act", "activation", dict(out=ek.t[:, 0:nch * 4],
                                           in_=tmpg.t[:, 0:nch, :].rearrange("p a b -> p (a b)"), func=AF.Exp),
                 reads=[tmpg.b], writes=[ek.b])
            if s0 < 0:
                P.op("dve", "tensor_scalar", dict(out=ek.t[:, 0:nch * 4], in0=ek.t[:, 0:nch * 4],
                                                  scalar1=cst.t[:, mko:mko + 1], scalar2=None, op0=ALU.mult),
                     reads=[ek.b, cst.b], writes=[ek.b])
            P.op("act", "activation", dict(out=ebl_c.t[:, 0:nch * 4], in_=tot_ps[:, 0:nch * 4], func=AF.Exp,
                                           scale=-1.0),
                 reads=[misc.b], writes=[ebl_c.b])
            if main:
                P.op("act", "activation", dict(out=ebq.t[:, 0:nch * 4], in_=bbn_ps[:, 0:nch * 4], func=AF.Exp,
                                               scale=-1.0, bias=cst.t[:, lqo:lqo + 1]),
                     reads=[misc.b, cst.b], writes=[ebq.b])
                P.op("dve", "tensor_tensor", dict(out=ebq2.t[:, 0:nch * 4], in0=ebq.t[:, 0:nch * 4],
                                                  in1=ebq.t[:, 0:nch * 4], op=ALU.mult),
                     reads=[ebq.b], writes=[ebq2.b])
            yield

        def gen_p(ti):
            kind, xsrc, w, s0 = tiles[ti]
            main = kind == "main"
            last_pre = (not main) and tiles[ti + 1][0] == "main"
            nch = w // 128
            par = ti % 2
            xn = xn2[ti % 2]
            qk, vh, sgo = qk2[par], vh2[par], sgo2[par]
            ek = ek3[ti % 3]
            jlist = list(range(8)) if (main or last_pre) else [4, 5, 6, 7]
            for j in jlist:
                pa = psA[cnt["t"] % 3]
                cv_ = cv[cnt["t"] % 2]
                cnt["t"] += 1
                for c in range(8):
                    P.op("pe", "matmul", dict(out=pa.t[:, :w], lhsT=win.t[:, c, j * 128:(j + 1) * 128],
                                              rhs=xn.t[:, c, :w], start=(c == 0), stop=(c == 7)),
                         reads=[grp_buf(gin, j * 128), xn.bs[c]], writes=[pa.b])
                P.op("act", "activation", dict(out=pq.t[:, j, 3:3 + w], in_=pa.t[:, :w], func=AF.Identity),
                     reads=[pa.b], writes=[pq.bs[j]])
                P.op("act", "activation", dict(out=cv_.t[:, :w], in_=pa.t[:, :w], func=AF.Identity,
                                               bias=cst.t[:, cbo + j:cbo + j + 1],
                                               scale=cst.t[:, cwo + j * 4 + 3:cwo + j * 4 + 4]),
                     reads=[pa.b, cst.b], writes=[cv_.b])
                for k in (2, 1, 0):
                    P.op("dve", "scalar_tensor_tensor", dict(
                        out=cv_.t[:, :w], in0=pq.t[:, j, k:k + w],
                        scalar=cst.t[:, cwo + j * 4 + k:cwo + j * 4 + k + 1], in1=cv_.t[:, :w],
                        op0=ALU.mult, op1=ALU.add),
                        reads=[pq.bs[j], cv_.b, cst.b], writes=[cv_.b])
                P.op("act", "activation", dict(out=qk.t[:, j, :w], in_=cv_.t[:, :w], func=AF.Silu),
                     reads=[cv_.b], writes=[qk.bs[j]])
                yield
            P.op("pool", "tensor_copy", dict(out=pq.t[:, :, 0:3], in_=pq.t[:, :, w:w + 3]),
                 reads=pq.bs, writes=pq.bs)
            for ch in range(nch):
                tok = slice(ch * 128, (ch + 1) * 128)
                for half in range(2):
                    pa = psA[cnt["t"] % 3]
                    cnt["t"] += 1
                    for c in range(8):
                        P.op("pe", "matmul", dict(out=pa.t[:, :], lhsT=xn.t[:, c, tok],
                                                  rhs=win.t[:, c, 1024 + half * 512:1024 + (half + 1) * 512],
                                                  start=(c == 0), stop=(c == 7)),
                             reads=[xn.bs[c], grp_buf(gin, 1024)], writes=[pa.b])
                    for hh in range(2):
                        h = half * 2 + hh
                        if hh == 0:
                            P.op("act", "activation", dict(out=vh.t[:, ch, h, 0:256],
                                                           in_=pa.t[:, hh * 256:(hh + 1) * 256], func=AF.Identity,
                                                           scale=ek.t[:, ch * 4 + h:ch * 4 + h + 1]),
                                 reads=[pa.b, ek.b], writes=[vh.bs[ch]])
                        else:
                            P.op("dve", "tensor_scalar", dict(out=vh.t[:, ch, h, 0:256],
                                                              in0=pa.t[:, hh * 256:(hh + 1) * 256],
                                                              scalar1=ek.t[:, ch * 4 + h:ch * 4 + h + 1],
                                                              scalar2=None, op0=ALU.mult),
                                 reads=[pa.b, ek.b], writes=[vh.bs[ch]])
                    yield
                P.op("dve", "tensor_copy", dict(out=vh.t[:, ch, :, 256:257],
                                                in_=ek.t[:, ch * 4:ch * 4 + 4].rearrange("p (a b) -> p a b", b=1)),
                     reads=[ek.b], writes=[vh.bs[ch]])
                if main:
                    for half in range(2):
                        pa = psA[cnt["t"] % 3]
                        cnt["t"] += 1
                        for c in range(8):
                            P.op("pe", "matmul", dict(out=pa.t[:, :], lhsT=xn.t[:, c, tok],
                                                      rhs=win.t[:, c, 2048 + half * 512:2048 + (half + 1) * 512],
                                                      start=(c == 0), stop=(c == 7)),
                                 reads=[xn.bs[c], grp_buf(gin, 2048)], writes=[pa.b])
                        P.op("act", "activation", dict(out=sgo.t[:, ch, half * 512:(half + 1) * 512], in_=pa.t[:, :],
                                                       func=AF.Sigmoid),
                             reads=[pa.b], writes=[sgo.bs[ch]])
                        yield

        def gen_chunks(ti):
            kind, xsrc, w, s0 = tiles[ti]
            main = kind == "main"
            nch = w // 128
            par = ti % 2
            qk, vh, sgo = qk2[par], vh2[par], sgo2[par]
            ebq, ebq2, ebl_c = ebq_2[ti % 3], ebq2_2[ti % 3], ebl[ti % 3]

            def state_part(ch):
                tok = slice(ch * 128, (ch + 1) * 128)
                kt = ktok[cnt["k"] % 2]
                sb_ = stb[cnt["k"] % 2]
                nr = numraw[cnt["k"] % 2]
                dr = denraw[cnt["k"] % 2]
                cnt["k"] += 1
                if ch > 0:
                    eprev = ebl_c.t[:, (ch - 1) * 4:(ch - 1) * 4 + 4]
                    eprev_b = ebl_c.b
                else:
                    eprev = elast.t[:, 0:4]
                    eprev_b = elast.b
                for h in range(4):
                    P.op("pe", "transpose", dict(out=psT4[:, h * 128:(h + 1) * 128], in_=qk.t[:, 4 + h, tok],
                                                 identity=identb.t[:, :]),
                         reads=[qk.bs[4 + h], identb.b], writes=[ps_s.b])
                P.op("dve", "tensor_copy", dict(out=kt.t[:, :, :].rearrange("p a b -> p (a b)"), in_=psT4[:, 0:512]),
                     reads=[ps_s.b], writes=[kt.b])
                yield
                if main:
                    for h in range(4):
                        P.op("act", "activation", dict(out=Cb.t[:, h, 0:257], in_=G.t[:, h, 0:257], func=AF.Identity,
                                                       scale=eprev[:, h:h + 1]),
                             reads=[G.b, eprev_b], writes=[Cb.b])
                    for h in range(4):
                        P.op("pe", "matmul", dict(out=ps_s.t[:, h * 128:(h + 1) * 128], lhsT=qk.t[:, 4 + h, tok],
                                                  rhs=qk.t[:, h, tok], start=True, stop=True),
                             reads=[qk.bs[4 + h], qk.bs[h]], writes=[ps_s.b])
                    P.op("dve", "tensor_tensor", dict(
                        out=sb_.t[:, :, :], in0=ps_s.t[:, :].rearrange("p (a b) -> p a b", b=128),
                        in1=cst.t[:, tro:tro + 128].rearrange("p (a b) -> p a b", a=1).broadcast_to([128, 4, 128]),
                        op=ALU.mult),
                        reads=[ps_s.b, cst.b], writes=[sb_.b])
                    yield
                def u_pair(hp):
                    for h in (2 * hp, 2 * hp + 1):
                        pu = ps_U[h // 2]
                        P.op("pe", "matmul", dict(out=pu.t[:, (h % 2) * 256:(h % 2 + 1) * 256], lhsT=kt.t[:, h, :],
                                                  rhs=vh.t[:, ch, h, 0:256], start=True, stop=True),
                             reads=[kt.b, vh.bs[ch]], writes=[pu.b])
                        P.op("pe", "matmul", dict(out=nu_ps[:, h:h + 1], lhsT=kt.t[:, h, :],
                                                  rhs=vh.t[:, ch, h, 256:257], start=True, stop=True),
                             reads=[kt.b, vh.bs[ch]], writes=[misc.b])

                def g_pair(hp):
                    for h in (2 * hp, 2 * hp + 1):
                        pu = ps_U[h // 2]
                        P.op("dve", "scalar_tensor_tensor", dict(out=G.t[:, h, 0:256], in0=G.t[:, h, 0:256],
                                                                 scalar=eprev[:, h:h + 1],
                                                                 in1=pu.t[:, (h % 2) * 256:(h % 2 + 1) * 256],
                                                                 op0=ALU.mult, op1=ALU.add),
                             reads=[G.b, eprev_b, pu.b], writes=[G.b])
                u_pair(0)
                if main:
                    for h in range(4):
                        pn = ps_num[h // 2]
                        P.op("pe", "matmul", dict(out=pn.t[:, (h % 2) * 256:(h % 2 + 1) * 256], lhsT=sb_.t[:, h, :],
                                                  rhs=vh.t[:, ch, h, 0:256], start=True, stop=False),
                             reads=[sb_.b, vh.bs[ch]], writes=[pn.b])
                        P.op("pe", "matmul", dict(out=pn.t[:, (h % 2) * 256:(h % 2 + 1) * 256], lhsT=qk.t[:, h, tok],
                                                  rhs=Cb.t[:, h, 0:256], start=False, stop=True),
                             reads=[qk.bs[h], Cb.b], writes=[pn.b])
                        P.op("pe", "matmul", dict(out=den_ps[:, h:h + 1], lhsT=sb_.t[:, h, :],
                                                  rhs=vh.t[:, ch, h, 256:257], start=True, stop=False),
                             reads=[sb_.b, vh.bs[ch]], writes=[misc.b])
                        P.op("pe", "matmul", dict(out=den_ps[:, h:h + 1], lhsT=qk.t[:, h, tok],
                                                  rhs=Cb.t[:, h, 256:257], start=False, stop=True),
                             reads=[qk.bs[h], Cb.b], writes=[misc.b])
                g_pair(0)
                u_pair(1)
                g_pair(1)
                gn = G.t[:, :, 256:257].rearrange("p a b -> p (a b)")
                P.op("dve", "tensor_tensor", dict(out=gn, in0=gn, in1=eprev, op=ALU.mult),
                     reads=[G.b, eprev_b], writes=[G.b])
                P.op("dve", "tensor_tensor", dict(out=gn, in0=gn, in1=nu_ps[:, 0:4], op=ALU.add),
                     reads=[G.b, misc.b], writes=[G.b])
                if main:
                    for q_ in range(2):
                        P.op("dve", "tensor_copy", dict(out=nr.t[:, q_ * 512:(q_ + 1) * 512], in_=ps_num[q_].t[:, :]),
                             reads=[ps_num[q_].b], writes=[nr.b])
                    P.op("dve", "tensor_copy", dict(out=dr.t[:, :], in_=den_ps[:, 0:4]),
                         reads=[misc.b], writes=[dr.b])
                yield
                return (nr, dr)

            def epilogue(ch, nr, dr):
                tok = slice(ch * 128, (ch + 1) * 128)
                hn_ = hn[cnt["e"] % 2]
                cnt["e"] += 1
                P.op("pool", "memset", dict(ap=ssq.t[:, :], constant=0.0), writes=[ssq.b])
                for h in range(4):
                    P.op("act", "activation", dict(out=junk.t[:, :], in_=nr.t[:, h * 256:(h + 1) * 256],
                                                   func=AF.Square, accum_out=ssq.t[:, h:h + 1]),
                         reads=[nr.b, ssq.b], writes=[junk.b, ssq.b])
                e_ = ebq.t[:, ch * 4:ch * 4 + 4]
                e2_ = ebq2.t[:, ch * 4:ch * 4 + 4]
                a0, a1, a2, a3 = sm
                P.op("dve", "tensor_tensor", dict(out=a0.t[:, :], in0=dr.t[:, :], in1=e_, op=ALU.mult),
                     reads=[dr.b, ebq.b], writes=[a0.b])
                P.op("dve", "tensor_tensor", dict(out=a1.t[:, :], in0=a0.t[:, :], in1=a0.t[:, :], op=ALU.mult),
                     reads=[a0.b], writes=[a1.b])
                P.op("dve", "tensor_single_scalar", dict(out=a1.t[:, :], in_=a1.t[:, :], scalar=1.0, op=ALU.max),
                     reads=[a1.b], writes=[a1.b])
                P.op("dve", "scalar_tensor_tensor", dict(out=a2.t[:, :], in0=ssq.t[:, :], scalar=1.0 / 256,
                                                         in1=e2_, op0=ALU.mult, op1=ALU.mult),
                     reads=[ssq.b, ebq2.b], writes=[a2.b])
                P.op("dve", "scalar_tensor_tensor", dict(out=a2.t[:, :], in0=a1.t[:, :], scalar=EPS,
                                                         in1=a2.t[:, :], op0=ALU.mult, op1=ALU.add),
                     reads=[a1.b, a2.b], writes=[a2.b])
                P.op("act", "activation", dict(out=a2.t[:, :], in_=a2.t[:, :], func=AF.Ln),
                     reads=[a2.b], writes=[a2.b])
                P.op("act", "activation", dict(out=a2.t[:, :], in_=a2.t[:, :], func=AF.Exp, scale=-0.5),
                     reads=[a2.b], writes=[a2.b])
                P.op("dve", "tensor_tensor", dict(out=a3.t[:, :], in0=a2.t[:, :], in1=e_, op=ALU.mult),
                     reads=[a2.b, ebq.b], writes=[a3.b])
                for h in range(4):
                    P.op("dve", "scalar_tensor_tensor", dict(out=hn_.t[:, h * 256:(h + 1) * 256],
                                                             in0=nr.t[:, h * 256:(h + 1) * 256],
                                                             scalar=a3.t[:, h:h + 1],
                                                             in1=sgo.t[:, ch, h * 256:(h + 1) * 256],
                                                             op0=ALU.mult, op1=ALU.mult),
                         reads=[nr.b, a3.b, sgo.bs[ch]], writes=[hn_.b])
                yield
                return (ch, hn_)

            def htrans(ch, hn_):
                tok = slice(ch * 128, (ch + 1) * 128)
                for j in range(8):
                    P.op("pe", "transpose", dict(out=psT4[:, j * 128:(j + 1) * 128],
                                                 in_=hn_.t[:, j * 128:(j + 1) * 128], identity=identb.t[:, :]),
                         reads=[hn_.b, identb.b], writes=[ps_s.b])
                P.op("act", "activation", dict(out=hnT.t[:, :, tok],
                                               in_=psT4[:, :].rearrange("p (a b) -> p a b", b=128),
                                               func=AF.Identity),
                     reads=[ps_s.b], writes=[hnT.bs[ch]])
                yield

            pend = None
            pend_h = None
            for ch in range(nch):
                raw = yield from state_part(ch)
                if main:
                    if pend_h is not None:
                        yield from htrans(*pend_h)
                        pend_h = None
                    if pend is not None:
                        pend_h = yield from epilogue(*pend)
                    pend = (ch, raw[0], raw[1])
            if pend_h is not None:
                yield from htrans(*pend_h)
            if pend is not None:
                pend_h = yield from epilogue(*pend)
                yield
                yield from htrans(*pend_h)
            P.op("dve", "tensor_copy", dict(out=elast.t[:, :], in_=ebl_c.t[:, (nch - 1) * 4:nch * 4]),
                 reads=[ebl_c.b], writes=[elast.b])
            if main:
                c0 = s0 + HALO
                for o in range(8):
                    po = psA[cnt["t"] % 3]
                    cnt["t"] += 1
                    slot = cnt["o"] % 2
                    cnt["o"] += 1
                    xo_, xr_ = xo[slot], xr[slot]
                    P.dma("sp", dict(out=xr_.t[:, :w], in_=x_in[o * 128:(o + 1) * 128, c0:c0 + w]),
                          writes=[xr_.b], stream="xr%d" % slot)
                    for j in range(8):
                        P.op("pe", "matmul", dict(out=po.t[:, :w], lhsT=wout.t[:, j, o * 128:(o + 1) * 128],
                                                  rhs=hnT.t[:, j, :w], start=(j == 0), stop=(j == 7)),
                             reads=[grp_buf(gout, j)] + hnT.bs[:nch], writes=[po.b])
                    if s0 < 0:
                        P.op("dve", "scalar_tensor_tensor", dict(out=xo_.t[:, :w], in0=po.t[:, :w],
                                                                 scalar=cst.t[:, mko:mko + 1], in1=xr_.t[:, :w],
                                                                 op0=ALU.mult, op1=ALU.add),
                             reads=[po.b, xr_.b, cst.b], writes=[xo_.b])
                    else:
                        P.op("dve", "tensor_tensor", dict(out=xo_.t[:, :w], in0=po.t[:, :w], in1=xr_.t[:, :w],
                                                          op=ALU.add),
                             reads=[po.b, xr_.b], writes=[xo_.b])
                    P.dma("sp", dict(out=x_out[o * 128:(o + 1) * 128, c0:c0 + w], in_=xo_.t[:, :w]),
                          reads=[xo_.b], stream="xo%d" % slot)
                    yield

        n = len(tiles)

        def run_all(g):
            for _ in g:
                pass

        def interleave(gens):
            alive = [True] * len(gens)
            while any(alive):
                for i, g in enumerate(gens):
                    if alive[i]:
                        try:
                            next(g)
                        except StopIteration:
                            alive[i] = False

        run_all(gen_g(0))
        if n > 1:
            run_all(gen_g(1))
        run_all(gen_p(0))
        for ti in range(n):
            gens = [gen_chunks(ti)]
            if ti + 1 < n:
                gens.append(gen_p(ti + 1))
            if ti + 2 < n:
                gens.append(gen_g(ti + 2))
            interleave(gens)
        P.barrier()
        P.replay()


def stage_final(P, nc, x_in, y_out, cst, co, psb):
    P.sched_on = False
    with ExitStack() as st:
        E = Env(nc, P, st)
        xts = [E.sb("xt%d" % i, (128, 8, 512), F32) for i in range(2)]
        xn = E.sb("xn", (128, 8, 512), BF16, nb=8)
        rstd = E.sb("rstd", (128, 512), F32)
        ones = E.sb("ones", (128, 128), BF16)
        yo = [E.sb("yo%d" % i, (128, 8, 512), F32) for i in range(2)]
        ps_stat = [psb[0], psb[1]]
        P.op("pool", "memset", dict(ap=ones.t[:, :], constant=1.0 / D), writes=[ones.b])
        goff, epso = co["fin_norm"], co["eps"]
        xin_v = x_in.ap().rearrange("(c p) t -> p c t", p=128)
        yout_v = y_out.ap().rearrange("(c p) t -> p c t", p=128)
        NT_ = T // 512
        w = 512

        def fload(i):
            c0 = HALO + i * 512
            P.dma("sp", dict(out=xts[i % 2].t[:, :, :w], in_=xin_v[:, :, c0:c0 + w]), writes=[xts[i % 2].b],
                  stream="fx%d" % (i % 2))

        fload(0)
        fload(1)
        for i in range(NT_):
            xt = xts[i % 2]
            y_ = yo[i % 2]
            pst = ps_stat[i % 2]
            P.op("act", "activation", dict(out=xn.t[:, :, :w], in_=xt.t[:, :, :w], func=AF.Square),
                 reads=[xt.b], writes=xn.bs)
            for c in range(8):
                P.op("pe", "matmul", dict(out=pst.t[:, :w], lhsT=ones.t[:, :], rhs=xn.t[:, c, :w],
                                          start=(c == 0), stop=(c == 7)),
                     reads=[xn.bs[c], ones.b], writes=[pst.b])
            P.op("act", "activation", dict(out=rstd.t[:, :w], in_=pst.t[:, :w], func=AF.Ln,
                                           bias=cst.t[:, epso:epso + 1], scale=1.0),
                 reads=[pst.b, cst.b], writes=[rstd.b])
            P.op("act", "activation", dict(out=rstd.t[:, :w], in_=rstd.t[:, :w], func=AF.Exp, scale=-0.5),
                 reads=[rstd.b], writes=[rstd.b])
            for c in range(8):
                P.op("dve", "scalar_tensor_tensor", dict(out=y_.t[:, c, :w], in0=xt.t[:, c, :w],
                                                         scalar=cst.t[:, goff + c:goff + c + 1],
                                                         in1=rstd.t[:, :w], op0=ALU.mult, op1=ALU.mult),
                     reads=[xt.b, rstd.b, cst.b], writes=[y_.b])
            if i + 2 < NT_:
                fload(i + 2)
            P.dma("sp", dict(out=yout_v[:, :, i * 512:i * 512 + w], in_=y_.t[:, :, :w]), reads=[y_.b],
                  stream="fy%d" % (i % 2))
        P.barrier()
        P.replay()


def build_consts(inputs, core):
    cp = ConstPack()
    odd = core % 2
    cp.add("mask", np.full((128, 1), float(odd), np.float32))
    cp.add("eps", np.full((128, 1), EPS, np.float32))
    cp.add("ident", np.eye(128, dtype=np.float32))
    cp.add("maskT", np.triu(np.ones((128, 128), np.float32)))
    cp.add("lnqs", np.full((128, 1), np.log(128.0 ** -0.5), np.float32))
    cp.add("m_norm", chan_cols(inputs["norm_mix"][0]))
    mcw = np.asarray(inputs["m_conv_w"][0], np.float32)
    cp.add("m_cw", mcw.reshape(4, 8, 128).transpose(2, 1, 0).reshape(128, 32))
    cp.add("m_cb", chan_cols(inputs["m_conv_b"][0]))
    cp.add("m_bg", np.tile(np.asarray(inputs["m_b_gates"][0], np.float32)[None, :], (128, 4)))
    cp.add("m_hn", chan_cols(inputs["m_head_norm"][0]))
    cp.add("fin_norm", chan_cols(inputs["norm_final"]))
    cp.add("c_norm", chan_cols(inputs["norm_mix"][1]))
    cp.add("c_bin", chan_cols(inputs["c_b_in"][0]))
    dw = np.asarray(inputs["c_dw_w"][0], np.float32)
    cp.add("c_dw", dw.reshape(CK, 8, 128).transpose(2, 1, 0).reshape(128, 8 * CK))
    cp.add("c_db", chan_cols(inputs["c_dw_b"][0]))
    cp.add("c_lg", chan_cols(inputs["c_ln_g"][0]))
    cp.add("c_lb", chan_cols(inputs["c_ln_b"][0]))
    cp.add("c_bout", chan_cols(inputs["c_b_out"][0]))
    for l in range(2):
        cp.add("f%d_norm" % l, chan_cols(inputs["norm_ffn"][l]))
        cw = np.asarray(inputs["f_conv_w"][l], np.float32)
        cwl = cw.reshape(3, 2 * NJ, 128).transpose(2, 1, 0).reshape(128, 2 * NJ * 3)
        cp.add("f%d_cw" % l, cwl)
        cp.add("f%d_cb" % l, chan_cols(inputs["f_conv_b"][l]))
    return cp


def build_stage_prog(stage, co, ncst):
    nc = bass.Bass("TRN2", target_bir_lowering=False)
    x_in = nc.dram_tensor("x_in", [D, HALO + T], F32, kind="ExternalInput")
    x_out = nc.dram_tensor("x_out", [D, HALO + T], F32, kind="ExternalOutput")
    cst_d = nc.dram_tensor("consts", [128, ncst], F32, kind="ExternalInput")
    with ExitStack() as st:
        P = Prog(nc, st)
        E = Env(nc, P, st)
        cst = E.sb("cst", (128, ncst), F32)
        psb = [E.ps("ps%d" % i) for i in range(8)]
        P.dma("sp", dict(out=cst.t[:, :], in_=cst_d[:, :]), writes=[cst.b])
        if stage in ("f0", "f1"):
            w_up_d = nc.dram_tensor("w_up", [128, 8, 2 * DFF], F32, kind="ExternalInput")
            w_dn_d = nc.dram_tensor("w_dn", [128, NJ, D], F32, kind="ExternalInput")
            stage_ffn(P, nc, x_in, x_out, w_up_d, w_dn_d, cst, co, stage, psb)
        elif stage == "m":
            xp = nc.dram_tensor("xp", [D, NPRE], F32, kind="ExternalInput")
            w_in_d = nc.dram_tensor("w_in", [128, 8, MW], F32, kind="ExternalInput")
            w_out_d = nc.dram_tensor("w_out", [128, 8, D], F32, kind="ExternalInput")
            stage_mlstm(P, nc, x_in, xp, x_out, w_in_d, w_out_d, cst, co, psb)
        elif stage == "fin":
            stage_final(P, nc, x_in, x_out, cst, co, psb)
        elif stage == "c":
            w_in_d = nc.dram_tensor("w_in", [128, 8, 2 * D], F32, kind="ExternalInput")
            w_out_d = nc.dram_tensor("w_out", [128, 8, D], F32, kind="ExternalInput")
            stage_conf(P, nc, x_in, x_out, w_in_d, w_out_d, cst, co, psb)
        P.finish()
    return nc


def stage_meta(stage, inputs):
    cp = build_consts(inputs, 0)
    return cp.off, cp.n


def stage_inputs(stage, inputs, core):
    cp = build_consts(inputs, core)
    m = {"consts": cp.build()}
    if stage in ("f0", "f1"):
        l = int(stage[1])
        m["w_up"] = wlayout(glu_interleave(np.asarray(inputs["f_w_up"][l], np.float32), DFF))
        m["w_dn"] = wlayout(np.asarray(inputs["f_w_down"][l], np.float32))
    elif stage == "m":
        m["w_in"] = wlayout(np.asarray(inputs["m_w_in"][0], np.float32))
        m["w_out"] = wlayout(np.asarray(inputs["m_w_out"][0], np.float32))
    elif stage == "c":
        m["w_in"] = wlayout(glu_interleave(np.asarray(inputs["c_w_in"][0], np.float32), D))
        m["w_out"] = wlayout(np.asarray(inputs["c_w_out"][0], np.float32))
    return m


def core_inputs(inputs, core):
    x = np.asarray(inputs["x"], np.float32)
    b, half = core // 2, core % 2
    t0 = half * T
    x_in = np.zeros((D, HALO + T), np.float32)
    x_in[:, HALO:] = x[b, t0:t0 + T].T
    xp = np.zeros((D, NPRE), np.float32)
    if half:
        x_in[:, :HALO] = x[b, t0 - HALO:t0].T
        xp[:] = x[b, 0:NPRE].T
    return x_in, xp


WEIGHTS = (("m_w_in", "m_w_in", 0, (128, 8, MW)), ("m_w_out", "m_w_out", 0, (128, 8, D)),
           ("f0_w_up", "f_w_up", 0, (128, 8, 2 * DFF)), ("f0_w_dn", "f_w_down", 0, (128, NJ, D)),
           ("c_w_in", "c_w_in", 0, (128, 8, 2 * D)), ("c_w_out", "c_w_out", 0, (128, 8, D)),
           ("f1_w_up", "f_w_up", 1, (128, 8, 2 * DFF)), ("f1_w_dn", "f_w_down", 1, (128, NJ, D)))


def build_fused(co, ncst):
    nc = bass.Bass("TRN2", target_bir_lowering=False)
    x_in = nc.dram_tensor("x_in", [D, HALO + T], F32, kind="ExternalInput")
    xp = nc.dram_tensor("xp", [D, NPRE], F32, kind="ExternalInput")
    cst_d = nc.dram_tensor("consts", [128, ncst], F32, kind="ExternalInput")
    wd = {n: nc.dram_tensor(n, list(shp), F32, kind="ExternalInput") for (n, _, _, shp) in WEIGHTS}
    sa = nc.dram_tensor("scr_a", [D, HALO + T], F32)
    sb = nc.dram_tensor("scr_b", [D, HALO + T], F32)
    y = nc.dram_tensor("y", [D, T], F32, kind="ExternalOutput")
    with ExitStack() as st:
        P = Prog(nc, st)
        E = Env(nc, P, st)
        cst = E.sb("cst", (128, ncst), F32)
        psb = [E.ps("ps%d" % i) for i in range(8)]
        P.dma("sp", dict(out=cst.t[:, :], in_=cst_d[:, :]), writes=[cst.b])
        P.dma("sp", dict(out=sa[:, 0:4], in_=x_in[:, 0:4]))
        stage_mlstm(P, nc, x_in, xp, sa, wd["m_w_in"], wd["m_w_out"], cst, co, psb)
        stage_ffn(P, nc, sa, sb, wd["f0_w_up"], wd["f0_w_dn"], cst, co, "f0", psb)
        stage_conf(P, nc, sb, sa, wd["c_w_in"], wd["c_w_out"], cst, co, psb)
        stage_ffn(P, nc, sa, sb, wd["f1_w_up"], wd["f1_w_dn"], cst, co, "f1", psb)
        stage_final(P, nc, sb, y, cst, co, psb)
        P.finish()
    return nc


FUSED = True


def kernel(**inputs):
    inputs = {k: np.asarray(v) for k, v in inputs.items()}
    x = inputs["x"]
    out = np.empty((BATCH, SEQ, D), np.float32)
    cps = [build_consts(inputs, c) for c in range(NCORES)]
    co, ncst = cps[0].off, cps[0].n
    consts = [cp.build() for cp in cps]
    cores = list(range(NCORES))
    xin_xp = [core_inputs(inputs, c) for c in cores]
    if FUSED:
        wl = {}
        for (n, src, l, _) in WEIGHTS:
            wsrc = np.asarray(inputs[src][l], np.float32)
            if src == "f_w_up":
                wsrc = glu_interleave(wsrc, DFF)
            elif src == "c_w_in":
                wsrc = glu_interleave(wsrc, D)
            wl[n] = wlayout(wsrc)
        nc = build_fused(co, ncst)
        in_maps = []
        for c in cores:
            m = {"x_in": xin_xp[c][0], "xp": xin_xp[c][1], "consts": consts[c]}
            m.update(wl)
            in_maps.append(m)
        res = run_bass_kernel_spmd(nc, in_maps, core_ids=cores)
        ys = [res.results[c]["y"] for c in cores]
    else:
        cur = [xin_xp[c][0] for c in cores]
        for stage in ("m", "f0", "c", "f1", "fin"):
            nc = build_stage_prog(stage, co, ncst)
            in_maps = []
            for c in cores:
                m = stage_inputs(stage, inputs, c)
                m["consts"] = consts[c]
                m["x_in"] = cur[c]
                if stage == "m":
                    m["xp"] = xin_xp[c][1]
                in_maps.append(m)
            res = run_bass_kernel_spmd(nc, in_maps, core_ids=cores)
            cur = [np.asarray(res.results[c]["x_out"]) for c in cores]
        ys = [cur[c][:, :T] for c in cores]
    for c in cores:
        b, half = c // 2, c % 2
        out[b, half * T:(half + 1) * T, :] = np.asarray(ys[c]).T
    return out
```

```python
import numpy as np
from contextlib import ExitStack
import concourse.bass as bass
import concourse.mybir as mybir
from concourse.bass_utils import run_bass_kernel_spmd

F32 = mybir.dt.float32
BF16 = mybir.dt.bfloat16
ALU = mybir.AluOpType
AF = mybir.ActivationFunctionType

D = 1024
SEQ = 8192
BATCH = 4
NCORES = 8
T = 4096
HALO = 132
HV = 128
DFF = 2816
NJ = DFF // 128
EPS = 1e-6
CK = 31
KD = 8


class Buf:
    __slots__ = ("name", "last_w", "readers", "dma_readers", "excl")

    def __init__(self, name, excl=False):
        self.name = name
        self.last_w = None
        self.readers = []
        self.dma_readers = []
        self.excl = excl


class Op:
    __slots__ = ("eng", "fn", "deps", "odeps", "signal", "sig_idx", "is_dma", "dma_sem", "dma_val",
                 "idx", "batch", "nun", "succ", "ready", "finish")

    def __init__(self, eng, fn, is_dma=False):
        self.eng = eng
        self.fn = fn
        self.deps = []
        self.odeps = []
        self.signal = False
        self.sig_idx = None
        self.is_dma = is_dma
        self.dma_sem = None
        self.dma_val = None
        self.succ = []
        self.ready = 0.0
        self.finish = 0.0


ACT_SETS = {AF.Silu: "silu", AF.Sigmoid: "sigmoid", AF.Exp: "lnexp", AF.Ln: "lnexp", AF.Sqrt: "sqrt",
            AF.Tanh: "silu"}


def _free_elems(ap):
    n = 1
    for d in ap.shape[1:]:
        n *= d
    return n


def est_cost(op):
    if op.fn is None:
        return 0.0, 0.0
    meth, kw = op.fn
    if op.is_dma:
        nb = 1
        for d in kw["out"].shape:
            nb *= d
        return 0.08, 2.0 + nb * 4 / 180e3
    if op.eng == "pe":
        if meth == "transpose":
            return 0.07, 0.3
        n = _free_elems(kw["rhs"])
        f32 = kw["rhs"].dtype == F32
        c = max(0.035, n / 2400.0) * (4 if f32 else 1)
        return c, c + 0.12
    n = _free_elems(kw["out"]) if "out" in kw else _free_elems(kw["ap"])
    if op.eng == "act":
        c = 0.2 + n / 1200.0
    elif op.eng == "dve":
        k = 6.3 if meth == "reciprocal" else 1.0
        c = 0.08 + k * n / 960.0
    else:
        c = 0.3 + n / 400.0
    return c, c + 0.05


class Prog:
    ENGS = ("pe", "act", "dve", "pool", "sp")

    def __init__(self, nc, st):
        self.nc = nc
        self.st = st
        self.ops = {e: [] for e in self.ENGS}
        self.esem = {e: st.enter_context(nc.semaphore("s_" + e)) for e in self.ENGS}
        self.stream_sem = {}
        self.stream_cnt = {}
        self.all_dmas = []
        self.barrier_for = {}
        self.nsem = 0
        self.nops = 0
        self.batch = 0
        self.sched_on = False

    def new_sem(self, name):
        self.nsem += 1
        return self.st.enter_context(self.nc.semaphore(name))

    def _add_dep(self, op, p, kind):
        if p is None or p is op:
            return
        if p.is_dma or op.is_dma or p.eng != op.eng:
            if not p.is_dma:
                p.signal = True
            op.deps.append(p)
        elif kind == "raw" and op.eng != "pe":
            p.signal = True
            op.deps.append(p)
        else:
            op.odeps.append(p)

    def _emit(self, op, reads, writes):
        eng = op.eng
        op.idx = self.nops
        self.nops += 1
        op.batch = self.batch
        if any(b.excl for b in reads):
            writes = list(writes) + [b for b in reads if b.excl]
            reads = [b for b in reads if not b.excl]
        b = self.barrier_for.pop(eng, None)
        if b:
            for p in b:
                self._add_dep(op, p, "raw")
        for bf in reads:
            self._add_dep(op, bf.last_w, "raw")
        for bf in writes:
            lw = bf.last_w
            if lw is not None and not (op.is_dma and lw.is_dma and not bf.readers and not bf.dma_readers):
                self._add_dep(op, lw, "waw")
            for r in bf.readers:
                self._add_dep(op, r, "war")
            for r in bf.dma_readers:
                self._add_dep(op, r, "war")
        for bf in reads:
            if op.is_dma:
                bf.dma_readers.append(op)
            else:
                rd = bf.readers
                if not self.sched_on:
                    rd[:] = [r for r in rd if r.eng != eng]
                rd.append(op)
        for bf in writes:
            bf.last_w = op
            bf.readers = []
            bf.dma_readers = []
        self.ops[eng].append(op)
        return op

    def op(self, eng, meth, kw, reads=(), writes=()):
        return self._emit(Op(eng, (meth, kw)), reads, writes)

    def dma(self, eng, kw, reads=(), writes=(), stream=None):
        op = Op(eng, ("dma_start", kw), is_dma=True)
        if stream is None:
            stream = "_d%d" % len(self.stream_sem)
        if stream not in self.stream_sem:
            self.stream_sem[stream] = self.new_sem("q_" + stream)
            self.stream_cnt[stream] = 0
        self.stream_cnt[stream] += 16
        op.dma_sem = self.stream_sem[stream]
        op.dma_val = self.stream_cnt[stream]
        self.all_dmas.append(op)
        return self._emit(op, reads, writes)

    def barrier(self):
        deps = []
        for e in self.ENGS:
            for p in reversed(self.ops[e]):
                if not p.is_dma:
                    deps.append(p)
                    break
        deps += self.all_dmas
        self.all_dmas = []
        for e in self.ENGS:
            self.barrier_for[e] = list(deps)

    def finish(self, final_eng="sp"):
        self.barrier()
        self._emit(Op(final_eng, None), (), ())
        self.replay()

    def _schedule(self, done):
        import heapq
        batch = []
        for e in self.ENGS:
            batch += self.ops[e][done[e]:]
        cur = self.batch
        last_stream = {}
        for op in sorted(batch, key=lambda o: o.idx):
            if op.is_dma:
                k = id(op.dma_sem)
                if k in last_stream:
                    op.odeps.append(last_stream[k])
                last_stream[k] = op
        for op in batch:
            op.succ = []
            op.nun = 0
            op.ready = 0.0
        for op in batch:
            for p in op.deps + op.odeps:
                if p.batch == cur:
                    p.succ.append(op)
                    op.nun += 1
        cost = {}
        bl = {}
        for op in sorted(batch, key=lambda o: -o.idx):
            c = est_cost(op)
            cost[id(op)] = c
            m = 0.0
            for s_ in op.succ:
                v = bl[id(s_)]
                if v > m:
                    m = v
            bl[id(op)] = c[1] + 0.15 + m
        free = {e: 0.0 for e in self.ENGS}
        order = {e: [] for e in self.ENGS}
        rdy = {e: [] for e in self.ENGS}
        fut = {e: [] for e in self.ENGS}
        for op in batch:
            if op.nun == 0:
                heapq.heappush(fut[op.eng], (0.0, op.idx, op))
        nsched = 0
        cur_set = None
        TBL_PEN = 3.0

        def act_set(op):
            if op.eng != "act" or op.fn is None or op.is_dma:
                return None
            return ACT_SETS.get(op.fn[1].get("func"))

        total = len(batch)
        while nsched < total:
            best = None
            for e in self.ENGS:
                f_, r_ = fut[e], rdy[e]
                while f_ and f_[0][0] <= free[e] + 1e-9:
                    _, _, o = heapq.heappop(f_)
                    pri = bl[id(o)]
                    heapq.heappush(r_, (-pri, o.idx, o))
                if r_:
                    st = free[e]
                elif f_:
                    st = f_[0][0]
                else:
                    continue
                if best is None or st < best[0]:
                    best = (st, e)
            st, e = best
            if rdy[e]:
                if e == "act" and cur_set is not None:
                    cands = heapq.nsmallest(6, rdy[e])
                    pick = cands[0]
                    for c_ in cands:
                        a = act_set(c_[2])
                        if a is None or a == cur_set:
                            if -c_[0] >= -cands[0][0] - TBL_PEN:
                                pick = c_
                            break
                    rdy[e].remove(pick)
                    heapq.heapify(rdy[e])
                    op = pick[2]
                else:
                    op = heapq.heappop(rdy[e])[2]
            else:
                op = heapq.heappop(fut[e])[2]
            a = act_set(op)
            if a is not None:
                if cur_set is not None and a != cur_set:
                    st += 1.4
                cur_set = a
            busy, lat = cost[id(op)]
            free[e] = st + busy
            op.finish = st + lat
            order[e].append(op)
            nsched += 1
            for s_ in op.succ:
                s_.nun -= 1
                if op.finish + 0.15 > s_.ready:
                    s_.ready = op.finish + 0.15
                if s_.nun == 0:
                    heapq.heappush(fut[s_.eng], (s_.ready, s_.idx, s_))
        assert nsched == len(batch), (nsched, len(batch))
        for e in self.ENGS:
            self.ops[e][done[e]:] = order[e]
        return max(free.values())

    def replay(self):
        nc = self.nc
        if not hasattr(self, "sigcnt"):
            self.sigcnt = {e: 0 for e in self.ENGS}
            self.known = {e: {} for e in self.ENGS}
            self.done = {e: 0 for e in self.ENGS}
        if self.sched_on:
            self.est_us = self._schedule(self.done)
        self.batch += 1
        for e in self.ENGS:
            for op in self.ops[e][self.done[e]:]:
                if op.signal and not op.is_dma:
                    self.sigcnt[e] += 1
                    op.sig_idx = self.sigcnt[e]
        esem = self.esem

        def run(ename, eng):
            known = self.known[ename]
            for op in self.ops[ename][self.done[ename]:]:
                need = {}
                for p in op.deps:
                    if p.is_dma:
                        s, v = p.dma_sem, p.dma_val
                    else:
                        if p.sig_idx is None:
                            continue
                        s, v = esem[p.eng], p.sig_idx
                    key = id(s)
                    if known.get(key, 0) >= v:
                        continue
                    if key not in need or need[key][1] < v:
                        need[key] = (s, v)
                for key, (s, v) in need.items():
                    eng.wait_ge(s, v)
                    known[key] = v
                if op.fn is None:
                    continue
                ins = getattr(eng, op.fn[0])(**op.fn[1])
                if op.is_dma:
                    ins.then_inc(op.dma_sem, 16)
                elif op.signal:
                    ins.then_inc(esem[ename], 1)
            self.done[ename] = len(self.ops[ename])

        with nc.Block() as block:
            @block.tensor
            def _(eng):
                run("pe", eng)

            @block.scalar
            def _(eng):
                run("act", eng)

            @block.vector
            def _(eng):
                run("dve", eng)

            @block.gpsimd
            def _(eng):
                run("pool", eng)

            @block.sync
            def _(eng):
                run("sp", eng)


class Tl:
    def __init__(self, t, name, nb=1):
        self.t = t
        self.b = Buf(name)
        self.bs = [Buf("%s_%d" % (name, i)) for i in range(nb)] if nb > 1 else [self.b]


class Env:
    _n = 0

    def __init__(self, nc, P, st):
        self.nc, self.P, self.st = nc, P, st
        Env._n += 1
        self.pfx = "e%d_" % Env._n

    def sb(self, name, shape, dtype, nb=1):
        t = self.st.enter_context(self.nc.sbuf_tensor(self.pfx + name, list(shape), dtype))
        return Tl(t, name, nb)

    def ps(self, name, shape=(128, 512), dtype=F32):
        t = self.st.enter_context(self.nc.psum_tensor(name, list(shape), dtype))
        tl = Tl(t, name)
        tl.b.excl = True
        return tl


def chan_cols(v):
    v = np.asarray(v, np.float32)
    return np.ascontiguousarray(v.reshape(-1, 128).T)


class ConstPack:
    def __init__(self):
        self.cols = []
        self.off = {}
        self.n = 0

    def add(self, name, arr):
        arr = np.asarray(arr, np.float32)
        assert arr.shape[0] == 128
        arr = arr.reshape(128, -1)
        self.off[name] = self.n
        self.cols.append(arr)
        self.n += arr.shape[1]

    def build(self):
        return np.ascontiguousarray(np.concatenate(self.cols, axis=1))


def glu_interleave(w, half):
    K = w.shape[0]
    nj = half // 128
    a = w[:, :half].reshape(K, nj, 128)
    b = w[:, half:].reshape(K, nj, 128)
    return np.ascontiguousarray(np.concatenate([a, b], axis=2).reshape(K, 2 * half))


def wlayout(w):
    K, N = w.shape
    return np.ascontiguousarray(w.reshape(K // 128, 128, N).transpose(1, 0, 2))


def ffn_tiles(start):
    total = T - start
    n = -(-total // 510)
    base, rem = divmod(total, n)
    tiles = []
    s = start
    for k in range(n):
        wv = base + (1 if k < rem else 0)
        tiles.append((s, wv))
        s += wv
    return tiles


def load_weight(P, nc, tl, dram, nsplit, eng="pool"):
    KC = dram.shape[1]
    per = (KC + nsplit - 1) // nsplit
    groups = []
    k = 0
    gi = 0
    while k < KC:
        k1 = min(KC, k + per)
        bf = tl.bs[gi] if len(tl.bs) > 1 else tl.b
        P.dma(eng, dict(out=tl.t[:, k:k1, :], in_=dram[:, k:k1, :]), writes=[bf])
        groups.append((k, k1, bf))
        k = k1
        gi += 1
    return groups


def load_weight_cols(P, nc, tl, dram, bounds, eng="pool"):
    groups = []
    for gi, (b0, b1) in enumerate(bounds):
        bf = Buf("%s_cb%d" % (tl.b.name, gi))
        P.dma(eng, dict(out=tl.t[:, :, b0:b1], in_=dram[:, :, b0:b1]), writes=[bf])
        groups.append((b0, b1, bf))
    return groups


def grp_buf(groups, k):
    for k0, k1, bf in groups:
        if k0 <= k < k1:
            return bf
    raise KeyError(k)


def emit_norm_prep(P, xt, xn, rstd, ps_stat, ones, cst, goff, epso, w, x_src, lnexp=False):
    if x_src is not None:
        P.dma("sp", dict(out=xt.t[:, :, :w], in_=x_src), writes=[xt.b], stream="xt")
    P.op("act", "activation", dict(out=xn.t[:, :, :w], in_=xt.t[:, :, :w], func=AF.Square),
         reads=[xt.b], writes=xn.bs)
    for c in range(8):
        P.op("pe", "matmul", dict(out=ps_stat.t[:, :w], lhsT=ones.t[:, :], rhs=xn.t[:, c, :w],
                                  start=(c == 0), stop=(c == 7)),
             reads=[xn.bs[c], ones.b], writes=[ps_stat.b])
    if lnexp:
        P.op("act", "activation", dict(out=rstd.t[:, :w], in_=ps_stat.t[:, :w], func=AF.Ln,
                                       bias=cst.t[:, epso:epso + 1], scale=1.0),
             reads=[ps_stat.b, cst.b], writes=[rstd.b])
        P.op("act", "activation", dict(out=rstd.t[:, :w], in_=rstd.t[:, :w], func=AF.Exp, scale=-0.5),
             reads=[rstd.b], writes=[rstd.b])
    else:
        P.op("act", "activation", dict(out=rstd.t[:, :w], in_=ps_stat.t[:, :w], func=AF.Sqrt,
                                       bias=cst.t[:, epso:epso + 1], scale=1.0),
             reads=[ps_stat.b, cst.b], writes=[rstd.b])
        P.op("dve", "reciprocal", dict(out=rstd.t[:, :w], in_=rstd.t[:, :w]), reads=[rstd.b], writes=[rstd.b])
    for c in range(8):
        eng = "dve"
        P.op(eng, "scalar_tensor_tensor", dict(out=xn.t[:, c, :w], in0=xt.t[:, c, :w],
                                               scalar=cst.t[:, goff + c:goff + c + 1],
                                               in1=rstd.t[:, :w], op0=ALU.mult, op1=ALU.mult),
             reads=[xt.b, rstd.b, cst.b], writes=[xn.bs[c]])


def stage_ffn(P, nc, x_in, x_out, w_up_d, w_dn_d, cst, co, lname, psb):
    P.sched_on = SCHED_F
    with ExitStack() as st:
        E = Env(nc, P, st)
        wup = E.sb("wup", (128, 8, 2 * DFF), BF16, nb=8)
        wdn = E.sb("wdn", (128, NJ, D), BF16, nb=4)
        xt = E.sb("xt", (128, 8, 512), F32)
        xn = E.sb("xn", (128, 8, 512), BF16, nb=8)
        rstd = E.sb("rstd", (128, 512), F32)
        ones = E.sb("ones", (128, 128), BF16)
        ya = [E.sb("ya%d" % i, (128, 512), F32) for i in range(2)]
        yb = [E.sb("yb%d" % i, (128, 512), F32) for i in range(2)]
        sa = [E.sb("sa%d" % i, (128, 512), F32) for i in range(2)]
        g = E.sb("g", (128, NJ, 512), BF16, nb=NJ)
        xr = [E.sb("xr%d" % i, (128, 512), F32) for i in range(2)]
        xo = [E.sb("xo%d" % i, (128, 512), F32) for i in range(2)]
        psA = [psb[0], psb[1]]
        psB = [psb[2], psb[3]]
        ps_stat = psb[4]
        psD = [psb[5], psb[6], psb[7]]

        P.op("pool", "memset", dict(ap=ones.t[:, :], constant=1.0 / D), writes=[ones.b])
        gup = load_weight_cols(P, nc, wup, w_up_d, [(j * 256, min(NJ, j + 2) * 256) for j in range(0, NJ, 2)])
        gdn = load_weight(P, nc, wdn, w_dn_d, 4)

        goff = co[lname + "_norm"]
        cwo = co[lname + "_cw"]
        cbo = co[lname + "_cb"]
        mko = co["mask"]
        tiles = ffn_tiles(-34 if lname == "f0" else 0)
        xin_v = x_in.ap().rearrange("(c p) t -> p c t", p=128)

        def prep(i):
            s, wv = tiles[i]
            c0 = s + HALO - 2
            w = wv + 2
            emit_norm_prep(P, xt, xn, rstd, ps_stat, ones, cst, goff, co["eps"], w, xin_v[:, :, c0:c0 + w],
                           lnexp=True)

        dcount = 0
        prep(0)
        for i, (s, wv) in enumerate(tiles):
            c0 = s + HALO - 2
            w = wv + 2
            for j in range(NJ):
                pa, pb = psA[j % 2], psB[j % 2]
                for (pp, colbase) in ((pa, j * 256), (pb, j * 256 + 128)):
                    for c in range(8):
                        P.op("pe", "matmul", dict(out=pp.t[:, :w], lhsT=wup.t[:, c, colbase:colbase + 128],
                                                  rhs=xn.t[:, c, :w], start=(c == 0), stop=(c == 7)),
                             reads=[grp_buf(gup, colbase), xn.bs[c]], writes=[pp.b])
                yy = (ya[j % 2], yb[j % 2])
                for h, (pp, y) in enumerate(((pa, yy[0]), (pb, yy[1]))):
                    jj = j + h * NJ
                    P.op("act", "activation", dict(out=y.t[:, :wv], in_=pp.t[:, 2:w], func=AF.Identity,
                                                   bias=cst.t[:, cbo + jj:cbo + jj + 1],
                                                   scale=cst.t[:, cwo + jj * 3 + 2:cwo + jj * 3 + 3]),
                         reads=[pp.b, cst.b], writes=[y.b])
                    for k in (1, 0):
                        P.op("dve", "scalar_tensor_tensor", dict(
                            out=y.t[:, :wv], in0=pp.t[:, k:k + wv],
                            scalar=cst.t[:, cwo + jj * 3 + k:cwo + jj * 3 + k + 1],
                            in1=y.t[:, :wv], op0=ALU.mult, op1=ALU.add),
                            reads=[pp.b, y.b, cst.b], writes=[y.b])
                s_ = sa[j % 2]
                P.op("act", "activation", dict(out=s_.t[:, :wv], in_=yy[0].t[:, :wv], func=AF.Silu),
                     reads=[yy[0].b], writes=[s_.b])
                P.op("pool", "tensor_tensor", dict(out=g.t[:, j, :wv], in0=s_.t[:, :wv], in1=yy[1].t[:, :wv],
                                                   op=ALU.mult),
                     reads=[s_.b, yy[1].b], writes=[g.bs[j]])
            if i + 1 < len(tiles):
                prep(i + 1)
            nneg = max(0, min(wv, -s))
            for o in range(8):
                slot = dcount % 2
                pd = psD[dcount % 3]
                dcount += 1
                xr_, xo_ = xr[slot], xo[slot]
                P.dma("sp", dict(out=xr_.t[:, :wv], in_=x_in[o * 128:(o + 1) * 128, c0 + 2:c0 + 2 + wv]),
                      writes=[xr_.b], stream="xr%d" % slot)
                for k in range(NJ):
                    P.op("pe", "matmul", dict(out=pd.t[:, :wv], lhsT=wdn.t[:, k, o * 128:(o + 1) * 128],
                                              rhs=g.t[:, k, :wv], start=(k == 0), stop=(k == NJ - 1)),
                         reads=[grp_buf(gdn, k), g.bs[k]], writes=[pd.b])
                if nneg > 0:
                    P.op("dve", "scalar_tensor_tensor", dict(
                        out=xo_.t[:, :nneg], in0=pd.t[:, :nneg], scalar=cst.t[:, mko:mko + 1],
                        in1=xr_.t[:, :nneg], op0=ALU.mult, op1=ALU.add),
                        reads=[pd.b, xr_.b, cst.b], writes=[xo_.b])
                if nneg < wv:
                    P.op("dve", "tensor_tensor", dict(out=xo_.t[:, nneg:wv], in0=pd.t[:, nneg:wv],
                                                      in1=xr_.t[:, nneg:wv], op=ALU.add),
                         reads=[pd.b, xr_.b], writes=[xo_.b])
                P.dma("sp", dict(out=x_out[o * 128:(o + 1) * 128, c0 + 2:c0 + 2 + wv], in_=xo_.t[:, :wv]),
                      reads=[xo_.b], stream="xo%d" % slot)
        P.barrier()
        P.replay()


def std_tiles(halo=HV):
    tiles = [(-halo, halo)]
    s = 0
    while s < T:
        tiles.append((s, 512))
        s += 512
    return tiles


def stage_conf(P, nc, x_in, x_out, w_in_d, w_out_d, cst, co, psb):
    P.sched_on = SCHED_C
    with ExitStack() as st:
        E = Env(nc, P, st)
        win = E.sb("win", (128, 8, 2 * D), BF16, nb=8)
        wout = E.sb("wout", (128, 8, D), BF16, nb=4)
        dg = E.sb("dg", (128, 8, CK, 128), BF16, nb=8)
        xt = E.sb("xt", (128, 8, 512), F32)
        xn = E.sb("xn", (128, 8, 512), BF16, nb=8)
        rstd = E.sb("rstd", (128, 512), F32)
        ones = E.sb("ones", (128, 128), BF16)
        ub = E.sb("ub", (128, 8, 30 + 512), BF16, nb=8)
        sg = [E.sb("sg%d" % i, (128, 512), F32) for i in range(2)]
        y = E.sb("y", (128, 8, 512), F32, nb=8)
        ybf = E.sb("ybf", (128, 8, 512), BF16, nb=8)
        ysq = E.sb("ysq", (128, 8, 512), BF16, nb=8)
        zs = ybf
        mu = E.sb("mu", (128, 512), F32)
        var = E.sb("var", (128, 512), F32)
        tmp = [E.sb("tmp%d" % i, (128, 512), F32) for i in range(2)]
        xr = [E.sb("xr%d" % i, (128, 512), F32) for i in range(2)]
        xo = [E.sb("xo%d" % i, (128, 512), F32) for i in range(2)]
        psA = [psb[0], psb[1]]
        psB = [psb[2], psb[3]]
        psC = [psb[4], psb[7]]
        ps_stat, ps_mu, ps_m2 = psb[4], psb[5], psb[6]

        P.op("pool", "memset", dict(ap=ones.t[:, :], constant=1.0 / D), writes=[ones.b])
        P.op("pool", "memset", dict(ap=ub.t[:, :, 0:30], constant=0.0), writes=ub.bs)
        gin = load_weight_cols(P, nc, win, w_in_d, [(j * 512, (j + 1) * 512) for j in range(4)])
        gout = load_weight(P, nc, wout, w_out_d, 4)
        ido = co["ident"]
        dwo = co["c_dw"]
        goff = co["c_norm"]
        bio, dbo, lgo, lbo, boo, mko = co["c_bin"], co["c_db"], co["c_lg"], co["c_lb"], co["c_bout"], co["mask"]
        xin_v = x_in.ap().rearrange("(c p) t -> p c t", p=128)
        tiles = std_tiles(32)
        cnt = {"o": 0}

        def load_x(i):
            s, w = tiles[i]
            c0 = s + HALO
            P.dma("sp", dict(out=xt.t[:, :, :w], in_=xin_v[:, :, c0:c0 + w]), writes=[xt.b], stream="xt")

        def prep(i):
            s, w = tiles[i]
            emit_norm_prep(P, xt, xn, rstd, ps_stat, ones, cst, goff, co["eps"], w, None, lnexp=True)

        def phase_a(i):
            s, w = tiles[i]
            for j in range(8):
                pa, pb = psA[j % 2], psB[j % 2]
                for (pp, colbase) in ((pa, j * 256), (pb, j * 256 + 128)):
                    for c in range(8):
                        P.op("pe", "matmul", dict(out=pp.t[:, :w], lhsT=win.t[:, c, colbase:colbase + 128],
                                                  rhs=xn.t[:, c, :w], start=(c == 0), stop=(c == 7)),
                             reads=[grp_buf(gin, colbase), xn.bs[c]], writes=[pp.b])
                sg_ = sg[j % 2]
                P.op("act", "activation", dict(out=sg_.t[:, :w], in_=pb.t[:, :w], func=AF.Sigmoid,
                                               bias=cst.t[:, bio + 8 + j:bio + 9 + j], scale=1.0),
                     reads=[pb.b, cst.b], writes=[sg_.b])
                P.op("dve", "scalar_tensor_tensor", dict(out=ub.t[:, j, 30:30 + w], in0=pa.t[:, :w],
                                                         scalar=cst.t[:, bio + j:bio + j + 1], in1=sg_.t[:, :w],
                                                         op0=ALU.add, op1=ALU.mult),
                     reads=[pa.b, sg_.b, cst.b], writes=[ub.bs[j]])
                if s < 0:
                    P.op("act", "activation", dict(out=ub.t[:, j, 30:30 + w], in_=ub.t[:, j, 30:30 + w],
                                                   func=AF.Identity, scale=cst.t[:, mko:mko + 1]),
                         reads=[ub.bs[j], cst.b], writes=[ub.bs[j]])

        def phase_b(i):
            s, w = tiles[i]
            for c in range(8):
                pc = psC[c % 2]
                for k in range(KD, CK):
                    P.op("pe", "matmul", dict(out=pc.t[:, :w], lhsT=dg.t[:, c, k, :], rhs=ub.t[:, c, k:k + w],
                                              start=(k == KD), stop=(k == CK - 1)),
                         reads=[dg.bs[c], ub.bs[c]], writes=[pc.b])
                P.op("act", "activation", dict(out=y.t[:, c, :w], in_=pc.t[:, :w], func=AF.Identity,
                                               bias=cst.t[:, dbo + c:dbo + c + 1], scale=1.0),
                     reads=[pc.b, cst.b], writes=[y.bs[c]])
                for k in range(KD):
                    P.op("dve", "scalar_tensor_tensor", dict(
                        out=y.t[:, c, :w], in0=ub.t[:, c, k:k + w],
                        scalar=cst.t[:, dwo + c * CK + k:dwo + c * CK + k + 1], in1=y.t[:, c, :w],
                        op0=ALU.mult, op1=ALU.add),
                        reads=[ub.bs[c], y.bs[c], cst.b], writes=[y.bs[c]])
                P.op("act", "activation", dict(out=ysq.t[:, c, :w], in_=y.t[:, c, :w], func=AF.Square),
                     reads=[y.bs[c]], writes=[ysq.bs[c]])
                P.op("dve", "tensor_copy", dict(out=ybf.t[:, c, :w], in_=y.t[:, c, :w]),
                     reads=[y.bs[c]], writes=[ybf.bs[c]])
            P.op("pool", "tensor_copy", dict(out=ub.t[:, :, 0:30], in_=ub.t[:, :, w:w + 30]),
                 reads=ub.bs, writes=ub.bs)
            for c in range(8):
                P.op("pe", "matmul", dict(out=ps_mu.t[:, :w], lhsT=ones.t[:, :], rhs=ybf.t[:, c, :w],
                                          start=(c == 0), stop=(c == 7)),
                     reads=[ybf.bs[c], ones.b], writes=[ps_mu.b])
            for c in range(8):
                P.op("pe", "matmul", dict(out=ps_m2.t[:, :w], lhsT=ones.t[:, :], rhs=ysq.t[:, c, :w],
                                          start=(c == 0), stop=(c == 7)),
                     reads=[ysq.bs[c], ones.b], writes=[ps_m2.b])

        def phase_c_elem(i):
            s, w = tiles[i]
            P.op("act", "activation", dict(out=mu.t[:, :w], in_=ps_mu.t[:, :w], func=AF.Identity),
                 reads=[ps_mu.b], writes=[mu.b])
            P.op("dve", "scalar_tensor_tensor", dict(out=var.t[:, :w], in0=mu.t[:, :w], scalar=-1.0,
                                                     in1=mu.t[:, :w], op0=ALU.mult, op1=ALU.mult),
                 reads=[mu.b], writes=[var.b])
            P.op("dve", "tensor_tensor", dict(out=var.t[:, :w], in0=ps_m2.t[:, :w], in1=var.t[:, :w], op=ALU.add),
                 reads=[ps_m2.b, var.b], writes=[var.b])
            P.op("act", "activation", dict(out=var.t[:, :w], in_=var.t[:, :w], func=AF.Ln,
                                           bias=cst.t[:, co["eps"]:co["eps"] + 1], scale=1.0),
                 reads=[var.b, cst.b], writes=[var.b])
            P.op("act", "activation", dict(out=var.t[:, :w], in_=var.t[:, :w], func=AF.Exp, scale=-0.5),
                 reads=[var.b], writes=[var.b])
            for c in range(8):
                t_ = tmp[c % 2]
                P.op("dve", "tensor_tensor", dict(out=t_.t[:, :w], in0=y.t[:, c, :w], in1=mu.t[:, :w],
                                                  op=ALU.subtract),
                     reads=[y.bs[c], mu.b], writes=[t_.b])
                P.op("dve", "scalar_tensor_tensor", dict(out=t_.t[:, :w], in0=t_.t[:, :w],
                                                         scalar=cst.t[:, lgo + c:lgo + c + 1], in1=var.t[:, :w],
                                                         op0=ALU.mult, op1=ALU.mult),
                     reads=[t_.b, var.b, cst.b], writes=[t_.b])
                P.op("act", "activation", dict(out=zs.t[:, c, :w], in_=t_.t[:, :w], func=AF.Silu,
                                               bias=cst.t[:, lbo + c:lbo + c + 1], scale=1.0),
                     reads=[t_.b, cst.b], writes=[zs.bs[c]])

        def phase_c_pe(i):
            s, w = tiles[i]
            c0 = s + HALO
            for o in range(8):
                po = psB[o % 2]
                slot = cnt["o"] % 2
                cnt["o"] += 1
                xo_, xr_ = xo[slot], xr[slot]
                P.dma("sp", dict(out=xr_.t[:, :w], in_=x_in[o * 128:(o + 1) * 128, c0:c0 + w]),
                      writes=[xr_.b], stream="xr%d" % slot)
                for c in range(8):
                    P.op("pe", "matmul", dict(out=po.t[:, :w], lhsT=wout.t[:, c, o * 128:(o + 1) * 128],
                                              rhs=zs.t[:, c, :w], start=(c == 0), stop=(c == 7)),
                         reads=[grp_buf(gout, c), zs.bs[c]], writes=[po.b])
                if s < 0:
                    P.op("dve", "tensor_scalar", dict(out=xo_.t[:, :w], in0=po.t[:, :w],
                                                      scalar1=cst.t[:, boo + o:boo + o + 1],
                                                      scalar2=cst.t[:, mko:mko + 1], op0=ALU.add, op1=ALU.mult),
                         reads=[po.b, cst.b], writes=[xo_.b])
                    P.op("dve", "tensor_tensor", dict(out=xo_.t[:, :w], in0=xo_.t[:, :w], in1=xr_.t[:, :w],
                                                      op=ALU.add),
                         reads=[xo_.b, xr_.b], writes=[xo_.b])
                else:
                    P.op("dve", "scalar_tensor_tensor", dict(out=xo_.t[:, :w], in0=po.t[:, :w],
                                                             scalar=cst.t[:, boo + o:boo + o + 1],
                                                             in1=xr_.t[:, :w], op0=ALU.add, op1=ALU.add),
                         reads=[po.b, xr_.b, cst.b], writes=[xo_.b])
                P.dma("sp", dict(out=x_out[o * 128:(o + 1) * 128, c0:c0 + w], in_=xo_.t[:, :w]),
                      reads=[xo_.b], stream="xo%d" % slot)

        n = len(tiles)
        load_x(0)
        prep(0)
        if n > 1:
            load_x(1)
        ident_b = cst.t[:, ido:ido + 128].rearrange("p (a b) -> p a b", a=1).broadcast_to([128, CK, 128])
        for c in range(8):
            dwc = cst.t[:, dwo + c * CK:dwo + (c + 1) * CK].rearrange("p (a b) -> p a b", b=1).broadcast_to(
                [128, CK, 128])
            P.op("dve", "tensor_tensor", dict(out=dg.t[:, c, :, :], in0=ident_b, in1=dwc, op=ALU.mult),
                 reads=[cst.b], writes=[dg.bs[c]])
        phase_a(0)
        for i in range(n):
            phase_b(i)
            if i + 1 < n:
                prep(i + 1)
                if i + 2 < n:
                    load_x(i + 2)
            phase_c_elem(i)
            if i + 1 < n:
                phase_a(i + 1)
            phase_c_pe(i)
        P.barrier()
        P.replay()


SCHED_M, SCHED_C, SCHED_F = True, True, False
NPRE = T - HV
MW = 3080


def stage_mlstm(P, nc, x_in, xp, x_out, w_in_d, w_out_d, cst, co, psb):
    P.sched_on = SCHED_M
    with ExitStack() as st:
        E = Env(nc, P, st)
        win = E.sb("win", (128, 8, MW), BF16, nb=8)
        wout = E.sb("wout", (128, 8, D), BF16, nb=8)
        xt = E.sb("xt", (128, 8, 512), F32)
        xn2 = [E.sb("xn%d" % i, (128, 8, 512), BF16, nb=8) for i in range(2)]
        rstd = E.sb("rstd", (128, 512), F32)
        ones = E.sb("ones", (128, 128), BF16)
        onesf = E.sb("onesf", (128, 128), F32)
        identb = E.sb("identb", (128, 128), BF16)
        pq = E.sb("pq", (128, 8, 3 + 512), F32, nb=8)
        cv = [E.sb("cv%d" % i, (128, 512), F32) for i in range(2)]
        qk2 = [E.sb("qk%d" % i, (128, 8, 512), BF16, nb=8) for i in range(2)]
        vh2 = [E.sb("vh%d" % i, (128, 4, 4, 260), BF16, nb=4) for i in range(2)]
        sgo2 = [E.sb("sgo%d" % i, (128, 4, D), BF16, nb=4) for i in range(2)]
        gsb = E.sb("gsb", (128, 4, 8), F32)
        sp_ = E.sb("sp", (128, 4, 4), F32)
        tmpg = E.sb("tmpg", (128, 4, 4), F32)
        ek3 = [E.sb("ek%d" % i, (128, 16), F32) for i in range(3)]
        ebq_2 = [E.sb("ebq%d" % i, (128, 16), F32) for i in range(3)]
        ebq2_2 = [E.sb("ebqq%d" % i, (128, 16), F32) for i in range(3)]
        ebl = [E.sb("ebl%d" % i, (128, 16), F32) for i in range(3)]
        elast = E.sb("elast", (128, 4), F32)
        stb = [E.sb("stb0", (128, 4, 128), BF16)] * 2
        ktok = [E.sb("ktok0", (128, 4, 128), BF16)] * 2
        G = E.sb("G", (128, 4, 260), F32)
        Cb = E.sb("Cb", (128, 4, 260), BF16)
        ssq = E.sb("ssq", (128, 4), F32)
        junk = E.sb("junk", (128, 256), BF16)
        sm = [E.sb("sm%d" % i, (128, 4), F32) for i in range(4)]
        hn = [E.sb("hn0", (128, D), BF16)] * 2
        numraw = [E.sb("numraw%d" % i, (128, D), F32) for i in range(2)]
        denraw = [E.sb("denraw%d" % i, (128, 4), F32) for i in range(2)]
        hnT = E.sb("hnT", (128, 8, 512), BF16, nb=4)
        xr = [E.sb("xr%d" % i, (128, 512), F32) for i in range(2)]
        xo = [E.sb("xo%d" % i, (128, 512), F32) for i in range(2)]
        psA = [psb[0], psb[1], psb[2]]
        ps_stat = psb[7]
        misc = psb[3]
        ps_s = psb[4]
        ps_num = [psb[5], psb[6]]
        ps_U = [psb[7], psb[7]]
        g_ps = misc.t[:, 0:32]
        bbn_ps = misc.t[:, 32:48]
        tot_ps = misc.t[:, 48:64]
        den_ps = misc.t[:, 64:68]
        nu_ps = misc.t[:, 72:76]
        psT4 = psb[4].t.bitcast(BF16)

        ido, mko, epo = co["ident"], co["mask"], co["eps"]
        tro = co["maskT"]
        P.op("pool", "memset", dict(ap=ones.t[:, :], constant=1.0 / D), writes=[ones.b])
        P.op("pool", "memset", dict(ap=onesf.t[:, :], constant=1.0), writes=[onesf.b])
        P.op("pool", "memset", dict(ap=pq.t[:, :, 0:3], constant=0.0), writes=pq.bs)
        P.op("pool", "memset", dict(ap=G.t[:, :, :], constant=0.0), writes=[G.b])
        P.op("pool", "memset", dict(ap=elast.t[:, :], constant=1.0), writes=[elast.b])
        for i in range(2):
            P.op("pool", "memset", dict(ap=vh2[i].t[:, :, :, :], constant=0.0), writes=vh2[i].bs)
        P.op("dve", "tensor_copy", dict(out=identb.t[:, :], in_=cst.t[:, ido:ido + 128]),
             reads=[cst.b], writes=[identb.b])
        gin = load_weight_cols(P, nc, win, w_in_d, [(0, 512), (512, 1024), (3072, 3080), (1024, 2048), (2048, 3072)])
        gout = load_weight(P, nc, wout, w_out_d, 8)
        hno = co["m_hn"]
        for j in range(8):
            P.op("act", "activation", dict(out=wout.t[:, j, :], in_=wout.t[:, j, :], func=AF.Identity,
                                           scale=cst.t[:, hno + j:hno + j + 1]),
                 reads=[grp_buf(gout, j), cst.b], writes=[grp_buf(gout, j)])

        goff, cwo, cbo, bgo, lqo = co["m_norm"], co["m_cw"], co["m_cb"], co["m_bg"], co["lnqs"]
        xin_v = x_in.ap().rearrange("(c p) t -> p c t", p=128)
        xp_v = xp.ap().rearrange("(c p) t -> p c t", p=128)

        tiles = []
        s = 0
        while s < NPRE:
            w = min(512, NPRE - s)
            tiles.append(("pre", xp_v[:, :, s:s + w], w, s - T))
            s += w
        for (s, w) in std_tiles():
            tiles.append(("main", xin_v[:, :, s + HALO:s + HALO + w], w, s))
        cnt = {"t": 0, "o": 0, "k": 0, "e": 0}

        def gen_g(ti):
            kind, xsrc, w, s0 = tiles[ti]
            main = kind == "main"
            nch = w // 128
            xn = xn2[ti % 2]
            ek, ebq, ebq2, ebl_c = ek3[ti % 3], ebq_2[ti % 3], ebq2_2[ti % 3], ebl[ti % 3]
            if ti == 0:
                P.dma("sp", dict(out=xt.t[:, :, :w], in_=xsrc), writes=[xt.b], stream="xt")
            emit_norm_prep(P, xt, xn, rstd, ps_stat, ones, cst, goff, epo, w, None, lnexp=True)
            if ti + 1 < len(tiles):
                w2 = tiles[ti + 1][2]
                P.dma("sp", dict(out=xt.t[:, :, :w2], in_=tiles[ti + 1][1]), writes=[xt.b], stream="xt")
            yield
            for ch in range(nch):
                for c in range(8):
                    P.op("pe", "matmul", dict(out=g_ps[:, ch * 8:(ch + 1) * 8],
                                              lhsT=xn.t[:, c, ch * 128:(ch + 1) * 128], rhs=win.t[:, c, 3072:3080],
                                              start=(c == 0), stop=(c == 7)),
                         reads=[xn.bs[c], grp_buf(gin, 3072)], writes=[misc.b])
            gs3 = gsb.t[:, 0:nch, :]
            P.op("dve", "tensor_tensor", dict(out=gsb.t[:, 0:nch, :].rearrange("p a b -> p (a b)"),
                                              in0=g_ps[:, 0:nch * 8], in1=cst.t[:, bgo:bgo + nch * 8], op=ALU.add),
                 reads=[misc.b, cst.b], writes=[gsb.b])
            P.op("act", "activation", dict(out=sp_.t[:, 0:nch, :], in_=gs3[:, :, 4:8], func=AF.Exp, scale=-1.0),
                 reads=[gsb.b], writes=[sp_.b])
            P.op("act", "activation", dict(out=sp_.t[:, 0:nch, :], in_=sp_.t[:, 0:nch, :], func=AF.Ln,
                                           bias=1.0, scale=1.0),
                 reads=[sp_.b], writes=[sp_.b])
            spf = sp_.t[:, 0:nch, :].rearrange("p a b -> p (a b)")
            P.op("pe", "matmul", dict(out=bbn_ps[:, 0:nch * 4], lhsT=cst.t[:, tro:tro + 128], rhs=spf,
                                      start=True, stop=True),
                 reads=[sp_.b, cst.b], writes=[misc.b])
            P.op("pe", "matmul", dict(out=tot_ps[:, 0:nch * 4], lhsT=onesf.t[:, :], rhs=spf, start=True, stop=True),
                 reads=[sp_.b, onesf.b], writes=[misc.b])
            P.op("dve", "tensor_tensor", dict(out=tmpg.t[:, 0:nch, :], in0=gs3[:, :, 0:4],
                                              in1=bbn_ps[:, 0:nch * 4].rearrange("p (a b) -> p a b", b=4),
                                              op=ALU.add),
                 reads=[gsb.b, misc.b], writes=[tmpg.b])
            P.op("act", "activation", dict(out=ek.t[:, 0:nch * 4],
                                           in_=tmpg.t[:, 0:nch, :].rearrange("p a b -> p (a b)"), func=AF.Exp),
                 reads=[tmpg.b], writes=[ek.b])
            if s0 < 0:
                P.op("dve", "tensor_scalar", dict(out=ek.t[:, 0:nch * 4], in0=ek.t[:, 0:nch * 4],
                                                  scalar1=cst.t[:, mko:mko + 1], scalar2=None, op0=ALU.mult),
                     reads=[ek.b, cst.b], writes=[ek.b])
            P.op("act", "activation", dict(out=ebl_c.t[:, 0:nch * 4], in_=tot_ps[:, 0:nch * 4], func=AF.Exp,
                                           scale=-1.0),
                 reads=[misc.b], writes=[ebl_c.b])
            if main:
                P.op("act", "activation", dict(out=ebq.t[:, 0:nch * 4], in_=bbn_ps[:, 0:nch * 4], func=AF.Exp,
                                               scale=-1.0, bias=cst.t[:, lqo:lqo + 1]),
                     reads=[misc.b, cst.b], writes=[ebq.b])
                P.op("dve", "tensor_tensor", dict(out=ebq2.t[:, 0:nch * 4], in0=ebq.t[:, 0:nch * 4],
                                                  in1=ebq.t[:, 0:nch * 4], op=ALU.mult),
                     reads=[ebq.b], writes=[ebq2.b])
            yield

        def gen_p(ti):
            kind, xsrc, w, s0 = tiles[ti]
            main = kind == "main"
            last_pre = (not main) and tiles[ti + 1][0] == "main"
            nch = w // 128
            par = ti % 2
            xn = xn2[ti % 2]
            qk, vh, sgo = qk2[par], vh2[par], sgo2[par]
            ek = ek3[ti % 3]
            jlist = list(range(8)) if (main or last_pre) else [4, 5, 6, 7]
            for j in jlist:
                pa = psA[cnt["t"] % 3]
                cv_ = cv[cnt["t"] % 2]
                cnt["t"] += 1
                for c in range(8):
                    P.op("pe", "matmul", dict(out=pa.t[:, :w], lhsT=win.t[:, c, j * 128:(j + 1) * 128],
                                              rhs=xn.t[:, c, :w], start=(c == 0), stop=(c == 7)),
                         reads=[grp_buf(gin, j * 128), xn.bs[c]], writes=[pa.b])
                P.op("act", "activation", dict(out=pq.t[:, j, 3:3 + w], in_=pa.t[:, :w], func=AF.Identity),
                     reads=[pa.b], writes=[pq.bs[j]])
                P.op("act", "activation", dict(out=cv_.t[:, :w], in_=pa.t[:, :w], func=AF.Identity,
                                               bias=cst.t[:, cbo + j:cbo + j + 1],
                                               scale=cst.t[:, cwo + j * 4 + 3:cwo + j * 4 + 4]),
                     reads=[pa.b, cst.b], writes=[cv_.b])
                for k in (2, 1, 0):
                    P.op("dve", "scalar_tensor_tensor", dict(
                        out=cv_.t[:, :w], in0=pq.t[:, j, k:k + w],
                        scalar=cst.t[:, cwo + j * 4 + k:cwo + j * 4 + k + 1], in1=cv_.t[:, :w],
                        op0=ALU.mult, op1=ALU.add),
                        reads=[pq.bs[j], cv_.b, cst.b], writes=[cv_.b])
                P.op("act", "activation", dict(out=qk.t[:, j, :w], in_=cv_.t[:, :w], func=AF.Silu),
                     reads=[cv_.b], writes=[qk.bs[j]])
                yield
            P.op("pool", "tensor_copy", dict(out=pq.t[:, :, 0:3], in_=pq.t[:, :, w:w + 3]),
                 reads=pq.bs, writes=pq.bs)
            for ch in range(nch):
                tok = slice(ch * 128, (ch + 1) * 128)
                for half in range(2):
                    pa = psA[cnt["t"] % 3]
                    cnt["t"] += 1
                    for c in range(8):
                        P.op("pe", "matmul", dict(out=pa.t[:, :], lhsT=xn.t[:, c, tok],
                                                  rhs=win.t[:, c, 1024 + half * 512:1024 + (half + 1) * 512],
                                                  start=(c == 0), stop=(c == 7)),
                             reads=[xn.bs[c], grp_buf(gin, 1024)], writes=[pa.b])
                    for hh in range(2):
                        h = half * 2 + hh
                        if hh == 0:
                            P.op("act", "activation", dict(out=vh.t[:, ch, h, 0:256],
                                                           in_=pa.t[:, hh * 256:(hh + 1) * 256], func=AF.Identity,
                                                           scale=ek.t[:, ch * 4 + h:ch * 4 + h + 1]),
                                 reads=[pa.b, ek.b], writes=[vh.bs[ch]])
                        else:
                            P.op("dve", "tensor_scalar", dict(out=vh.t[:, ch, h, 0:256],
                                                              in0=pa.t[:, hh * 256:(hh + 1) * 256],
                                                              scalar1=ek.t[:, ch * 4 + h:ch * 4 + h + 1],
                                                              scalar2=None, op0=ALU.mult),
                                 reads=[pa.b, ek.b], writes=[vh.bs[ch]])
                    yield
                P.op("dve", "tensor_copy", dict(out=vh.t[:, ch, :, 256:257],
                                                in_=ek.t[:, ch * 4:ch * 4 + 4].rearrange("p (a b) -> p a b", b=1)),
                     reads=[ek.b], writes=[vh.bs[ch]])
                if main:
                    for half in range(2):
                        pa = psA[cnt["t"] % 3]
                        cnt["t"] += 1
                        for c in range(8):
                            P.op("pe", "matmul", dict(out=pa.t[:, :], lhsT=xn.t[:, c, tok],
                                                      rhs=win.t[:, c, 2048 + half * 512:2048 + (half + 1) * 512],
                                                      start=(c == 0), stop=(c == 7)),
                                 reads=[xn.bs[c], grp_buf(gin, 2048)], writes=[pa.b])
                        P.op("act", "activation", dict(out=sgo.t[:, ch, half * 512:(half + 1) * 512], in_=pa.t[:, :],
                                                       func=AF.Sigmoid),
                             reads=[pa.b], writes=[sgo.bs[ch]])
                        yield

        def gen_chunks(ti):
            kind, xsrc, w, s0 = tiles[ti]
            main = kind == "main"
            nch = w // 128
            par = ti % 2
            qk, vh, sgo = qk2[par], vh2[par], sgo2[par]
            ebq, ebq2, ebl_c = ebq_2[ti % 3], ebq2_2[ti % 3], ebl[ti % 3]

            def state_part(ch):
                tok = slice(ch * 128, (ch + 1) * 128)
                kt = ktok[cnt["k"] % 2]
                sb_ = stb[cnt["k"] % 2]
                nr = numraw[cnt["k"] % 2]
                dr = denraw[cnt["k"] % 2]
                cnt["k"] += 1
                if ch > 0:
                    eprev = ebl_c.t[:, (ch - 1) * 4:(ch - 1) * 4 + 4]
                    eprev_b = ebl_c.b
                else:
                    eprev = elast.t[:, 0:4]
                    eprev_b = elast.b
                for h in range(4):
                    P.op("pe", "transpose", dict(out=psT4[:, h * 128:(h + 1) * 128], in_=qk.t[:, 4 + h, tok],
                                                 identity=identb.t[:, :]),
                         reads=[qk.bs[4 + h], identb.b], writes=[ps_s.b])
                P.op("dve", "tensor_copy", dict(out=kt.t[:, :, :].rearrange("p a b -> p (a b)"), in_=psT4[:, 0:512]),
                     reads=[ps_s.b], writes=[kt.b])
                yield
                if main:
                    for h in range(4):
                        P.op("act", "activation", dict(out=Cb.t[:, h, 0:257], in_=G.t[:, h, 0:257], func=AF.Identity,
                                                       scale=eprev[:, h:h + 1]),
                             reads=[G.b, eprev_b], writes=[Cb.b])
                    for h in range(4):
                        P.op("pe", "matmul", dict(out=ps_s.t[:, h * 128:(h + 1) * 128], lhsT=qk.t[:, 4 + h, tok],
                                                  rhs=qk.t[:, h, tok], start=True, stop=True),
                             reads=[qk.bs[4 + h], qk.bs[h]], writes=[ps_s.b])
                    P.op("dve", "tensor_tensor", dict(
                        out=sb_.t[:, :, :], in0=ps_s.t[:, :].rearrange("p (a b) -> p a b", b=128),
                        in1=cst.t[:, tro:tro + 128].rearrange("p (a b) -> p a b", a=1).broadcast_to([128, 4, 128]),
                        op=ALU.mult),
                        reads=[ps_s.b, cst.b], writes=[sb_.b])
                    yield
                def u_pair(hp):
                    for h in (2 * hp, 2 * hp + 1):
                        pu = ps_U[h // 2]
                        P.op("pe", "matmul", dict(out=pu.t[:, (h % 2) * 256:(h % 2 + 1) * 256], lhsT=kt.t[:, h, :],
                                                  rhs=vh.t[:, ch, h, 0:256], start=True, stop=True),
                             reads=[kt.b, vh.bs[ch]], writes=[pu.b])
                        P.op("pe", "matmul", dict(out=nu_ps[:, h:h + 1], lhsT=kt.t[:, h, :],
                                                  rhs=vh.t[:, ch, h, 256:257], start=True, stop=True),
                             reads=[kt.b, vh.bs[ch]], writes=[misc.b])

                def g_pair(hp):
                    for h in (2 * hp, 2 * hp + 1):
                        pu = ps_U[h // 2]
                        P.op("dve", "scalar_tensor_tensor", dict(out=G.t[:, h, 0:256], in0=G.t[:, h, 0:256],
                                                                 scalar=eprev[:, h:h + 1],
                                                                 in1=pu.t[:, (h % 2) * 256:(h % 2 + 1) * 256],
                                                                 op0=ALU.mult, op1=ALU.add),
                             reads=[G.b, eprev_b, pu.b], writes=[G.b])
                u_pair(0)
                if main:
                    for h in range(4):
                        pn = ps_num[h // 2]
                        P.op("pe", "matmul", dict(out=pn.t[:, (h % 2) * 256:(h % 2 + 1) * 256], lhsT=sb_.t[:, h, :],
                                                  rhs=vh.t[:, ch, h, 0:256], start=True, stop=False),
                             reads=[sb_.b, vh.bs[ch]], writes=[pn.b])
                        P.op("pe", "matmul", dict(out=pn.t[:, (h % 2) * 256:(h % 2 + 1) * 256], lhsT=qk.t[:, h, tok],
                                                  rhs=Cb.t[:, h, 0:256], start=False, stop=True),
                             reads=[qk.bs[h], Cb.b], writes=[pn.b])
                        P.op("pe", "matmul", dict(out=den_ps[:, h:h + 1], lhsT=sb_.t[:, h, :],
                                                  rhs=vh.t[:, ch, h, 256:257], start=True, stop=False),
                             reads=[sb_.b, vh.bs[ch]], writes=[misc.b])
                        P.op("pe", "matmul", dict(out=den_ps[:, h:h + 1], lhsT=qk.t[:, h, tok],
                                                  rhs=Cb.t[:, h, 256:257], start=False, stop=True),
                             reads=[qk.bs[h], Cb.b], writes=[misc.b])
                g_pair(0)
                u_pair(1)
                g_pair(1)
                gn = G.t[:, :, 256:257].rearrange("p a b -> p (a b)")
                P.op("dve", "tensor_tensor", dict(out=gn, in0=gn, in1=eprev, op=ALU.mult),
                     reads=[G.b, eprev_b], writes=[G.b])
                P.op("dve", "tensor_tensor", dict(out=gn, in0=gn, in1=nu_ps[:, 0:4], op=ALU.add),
                     reads=[G.b, misc.b], writes=[G.b])
                if main:
                    for q_ in range(2):
                        P.op("dve", "tensor_copy", dict(out=nr.t[:, q_ * 512:(q_ + 1) * 512], in_=ps_num[q_].t[:, :]),
                             reads=[ps_num[q_].b], writes=[nr.b])
                    P.op("dve", "tensor_copy", dict(out=dr.t[:, :], in_=den_ps[:, 0:4]),
                         reads=[misc.b], writes=[dr.b])
                yield
                return (nr, dr)

            def epilogue(ch, nr, dr):
                tok = slice(ch * 128, (ch + 1) * 128)
                hn_ = hn[cnt["e"] % 2]
                cnt["e"] += 1
                P.op("pool", "memset", dict(ap=ssq.t[:, :], constant=0.0), writes=[ssq.b])
                for h in range(4):
                    P.op("act", "activation", dict(out=junk.t[:, :], in_=nr.t[:, h * 256:(h + 1) * 256],
                                                   func=AF.Square, accum_out=ssq.t[:, h:h + 1]),
                         reads=[nr.b, ssq.b], writes=[junk.b, ssq.b])
                e_ = ebq.t[:, ch * 4:ch * 4 + 4]
                e2_ = ebq2.t[:, ch * 4:ch * 4 + 4]
                a0, a1, a2, a3 = sm
                P.op("dve", "tensor_tensor", dict(out=a0.t[:, :], in0=dr.t[:, :], in1=e_, op=ALU.mult),
                     reads=[dr.b, ebq.b], writes=[a0.b])
                P.op("dve", "tensor_tensor", dict(out=a1.t[:, :], in0=a0.t[:, :], in1=a0.t[:, :], op=ALU.mult),
                     reads=[a0.b], writes=[a1.b])
                P.op("dve", "tensor_single_scalar", dict(out=a1.t[:, :], in_=a1.t[:, :], scalar=1.0, op=ALU.max),
                     reads=[a1.b], writes=[a1.b])
                P.op("dve", "scalar_tensor_tensor", dict(out=a2.t[:, :], in0=ssq.t[:, :], scalar=1.0 / 256,
                                                         in1=e2_, op0=ALU.mult, op1=ALU.mult),
                     reads=[ssq.b, ebq2.b], writes=[a2.b])
                P.op("dve", "scalar_tensor_tensor", dict(out=a2.t[:, :], in0=a1.t[:, :], scalar=EPS,
                                                         in1=a2.t[:, :], op0=ALU.mult, op1=ALU.add),
                     reads=[a1.b, a2.b], writes=[a2.b])
                P.op("act", "activation", dict(out=a2.t[:, :], in_=a2.t[:, :], func=AF.Ln),
                     reads=[a2.b], writes=[a2.b])
                P.op("act", "activation", dict(out=a2.t[:, :], in_=a2.t[:, :], func=AF.Exp, scale=-0.5),
                     reads=[a2.b], writes=[a2.b])
                P.op("dve", "tensor_tensor", dict(out=a3.t[:, :], in0=a2.t[:, :], in1=e_, op=ALU.mult),
                     reads=[a2.b, ebq.b], writes=[a3.b])
                for h in range(4):
                    P.op("dve", "scalar_tensor_tensor", dict(out=hn_.t[:, h * 256:(h + 1) * 256],
                                                             in0=nr.t[:, h * 256:(h + 1) * 256],
                                                             scalar=a3.t[:, h:h + 1],
                                                             in1=sgo.t[:, ch, h * 256:(h + 1) * 256],
                                                             op0=ALU.mult, op1=ALU.mult),
                         reads=[nr.b, a3.b, sgo.bs[ch]], writes=[hn_.b])
                yield
                return (ch, hn_)

            def htrans(ch, hn_):
                tok = slice(ch * 128, (ch + 1) * 128)
                for j in range(8):
                    P.op("pe", "transpose", dict(out=psT4[:, j * 128:(j + 1) * 128],
                                                 in_=hn_.t[:, j * 128:(j + 1) * 128], identity=identb.t[:, :]),
                         reads=[hn_.b, identb.b], writes=[ps_s.b])
                P.op("act", "activation", dict(out=hnT.t[:, :, tok],
                                               in_=psT4[:, :].rearrange("p (a b) -> p a b", b=128),
                                               func=AF.Identity),
                     reads=[ps_s.b], writes=[hnT.bs[ch]])
                yield

            pend = None
            pend_h = None
            for ch in range(nch):
                raw = yield from state_part(ch)
                if main:
                    if pend_h is not None:
                        yield from htrans(*pend_h)
                        pend_h = None
                    if pend is not None:
                        pend_h = yield from epilogue(*pend)
                    pend = (ch, raw[0], raw[1])
            if pend_h is not None:
                yield from htrans(*pend_h)
            if pend is not None:
                pend_h = yield from epilogue(*pend)
                yield
                yield from htrans(*pend_h)
            P.op("dve", "tensor_copy", dict(out=elast.t[:, :], in_=ebl_c.t[:, (nch - 1) * 4:nch * 4]),
                 reads=[ebl_c.b], writes=[elast.b])
            if main:
                c0 = s0 + HALO
                for o in range(8):
                    po = psA[cnt["t"] % 3]
                    cnt["t"] += 1
                    slot = cnt["o"] % 2
                    cnt["o"] += 1
                    xo_, xr_ = xo[slot], xr[slot]
                    P.dma("sp", dict(out=xr_.t[:, :w], in_=x_in[o * 128:(o + 1) * 128, c0:c0 + w]),
                          writes=[xr_.b], stream="xr%d" % slot)
                    for j in range(8):
                        P.op("pe", "matmul", dict(out=po.t[:, :w], lhsT=wout.t[:, j, o * 128:(o + 1) * 128],
                                                  rhs=hnT.t[:, j, :w], start=(j == 0), stop=(j == 7)),
                             reads=[grp_buf(gout, j)] + hnT.bs[:nch], writes=[po.b])
                    if s0 < 0:
                        P.op("dve", "scalar_tensor_tensor", dict(out=xo_.t[:, :w], in0=po.t[:, :w],
                                                                 scalar=cst.t[:, mko:mko + 1], in1=xr_.t[:, :w],
                                                                 op0=ALU.mult, op1=ALU.add),
                             reads=[po.b, xr_.b, cst.b], writes=[xo_.b])
                    else:
                        P.op("dve", "tensor_tensor", dict(out=xo_.t[:, :w], in0=po.t[:, :w], in1=xr_.t[:, :w],
                                                          op=ALU.add),
                             reads=[po.b, xr_.b], writes=[xo_.b])
                    P.dma("sp", dict(out=x_out[o * 128:(o + 1) * 128, c0:c0 + w], in_=xo_.t[:, :w]),
                          reads=[xo_.b], stream="xo%d" % slot)
                    yield

        n = len(tiles)

        def run_all(g):
            for _ in g:
                pass

        def interleave(gens):
            alive = [True] * len(gens)
            while any(alive):
                for i, g in enumerate(gens):
                    if alive[i]:
                        try:
                            next(g)
                        except StopIteration:
                            alive[i] = False

        run_all(gen_g(0))
        if n > 1:
            run_all(gen_g(1))
        run_all(gen_p(0))
        for ti in range(n):
            gens = [gen_chunks(ti)]
            if ti + 1 < n:
                gens.append(gen_p(ti + 1))
            if ti + 2 < n:
                gens.append(gen_g(ti + 2))
            interleave(gens)
        P.barrier()
        P.replay()


def stage_final(P, nc, x_in, y_out, cst, co, psb):
    P.sched_on = False
    with ExitStack() as st:
        E = Env(nc, P, st)
        xts = [E.sb("xt%d" % i, (128, 8, 512), F32) for i in range(2)]
        xn = E.sb("xn", (128, 8, 512), BF16, nb=8)
        rstd = E.sb("rstd", (128, 512), F32)
        ones = E.sb("ones", (128, 128), BF16)
        yo = [E.sb("yo%d" % i, (128, 8, 512), F32) for i in range(2)]
        ps_stat = [psb[0], psb[1]]
        P.op("pool", "memset", dict(ap=ones.t[:, :], constant=1.0 / D), writes=[ones.b])
        goff, epso = co["fin_norm"], co["eps"]
        xin_v = x_in.ap().rearrange("(c p) t -> p c t", p=128)
        yout_v = y_out.ap().rearrange("(c p) t -> p c t", p=128)
        NT_ = T // 512
        w = 512

        def fload(i):
            c0 = HALO + i * 512
            P.dma("sp", dict(out=xts[i % 2].t[:, :, :w], in_=xin_v[:, :, c0:c0 + w]), writes=[xts[i % 2].b],
                  stream="fx%d" % (i % 2))

        fload(0)
        fload(1)
        for i in range(NT_):
            xt = xts[i % 2]
            y_ = yo[i % 2]
            pst = ps_stat[i % 2]
            P.op("act", "activation", dict(out=xn.t[:, :, :w], in_=xt.t[:, :, :w], func=AF.Square),
                 reads=[xt.b], writes=xn.bs)
            for c in range(8):
                P.op("pe", "matmul", dict(out=pst.t[:, :w], lhsT=ones.t[:, :], rhs=xn.t[:, c, :w],
                                          start=(c == 0), stop=(c == 7)),
                     reads=[xn.bs[c], ones.b], writes=[pst.b])
            P.op("act", "activation", dict(out=rstd.t[:, :w], in_=pst.t[:, :w], func=AF.Ln,
                                           bias=cst.t[:, epso:epso + 1], scale=1.0),
                 reads=[pst.b, cst.b], writes=[rstd.b])
            P.op("act", "activation", dict(out=rstd.t[:, :w], in_=rstd.t[:, :w], func=AF.Exp, scale=-0.5),
                 reads=[rstd.b], writes=[rstd.b])
            for c in range(8):
                P.op("dve", "scalar_tensor_tensor", dict(out=y_.t[:, c, :w], in0=xt.t[:, c, :w],
                                                         scalar=cst.t[:, goff + c:goff + c + 1],
                                                         in1=rstd.t[:, :w], op0=ALU.mult, op1=ALU.mult),
                     reads=[xt.b, rstd.b, cst.b], writes=[y_.b])
            if i + 2 < NT_:
                fload(i + 2)
            P.dma("sp", dict(out=yout_v[:, :, i * 512:i * 512 + w], in_=y_.t[:, :, :w]), reads=[y_.b],
                  stream="fy%d" % (i % 2))
        P.barrier()
        P.replay()


def build_consts(inputs, core):
    cp = ConstPack()
    odd = core % 2
    cp.add("mask", np.full((128, 1), float(odd), np.float32))
    cp.add("eps", np.full((128, 1), EPS, np.float32))
    cp.add("ident", np.eye(128, dtype=np.float32))
    cp.add("maskT", np.triu(np.ones((128, 128), np.float32)))
    cp.add("lnqs", np.full((128, 1), np.log(128.0 ** -0.5), np.float32))
    cp.add("m_norm", chan_cols(inputs["norm_mix"][0]))
    mcw = np.asarray(inputs["m_conv_w"][0], np.float32)
    cp.add("m_cw", mcw.reshape(4, 8, 128).transpose(2, 1, 0).reshape(128, 32))
    cp.add("m_cb", chan_cols(inputs["m_conv_b"][0]))
    cp.add("m_bg", np.tile(np.asarray(inputs["m_b_gates"][0], np.float32)[None, :], (128, 4)))
    cp.add("m_hn", chan_cols(inputs["m_head_norm"][0]))
    cp.add("fin_norm", chan_cols(inputs["norm_final"]))
    cp.add("c_norm", chan_cols(inputs["norm_mix"][1]))
    cp.add("c_bin", chan_cols(inputs["c_b_in"][0]))
    dw = np.asarray(inputs["c_dw_w"][0], np.float32)
    cp.add("c_dw", dw.reshape(CK, 8, 128).transpose(2, 1, 0).reshape(128, 8 * CK))
    cp.add("c_db", chan_cols(inputs["c_dw_b"][0]))
    cp.add("c_lg", chan_cols(inputs["c_ln_g"][0]))
    cp.add("c_lb", chan_cols(inputs["c_ln_b"][0]))
    cp.add("c_bout", chan_cols(inputs["c_b_out"][0]))
    for l in range(2):
        cp.add("f%d_norm" % l, chan_cols(inputs["norm_ffn"][l]))
        cw = np.asarray(inputs["f_conv_w"][l], np.float32)
        cwl = cw.reshape(3, 2 * NJ, 128).transpose(2, 1, 0).reshape(128, 2 * NJ * 3)
        cp.add("f%d_cw" % l, cwl)
        cp.add("f%d_cb" % l, chan_cols(inputs["f_conv_b"][l]))
    return cp


def build_stage_prog(stage, co, ncst):
    nc = bass.Bass("TRN2", target_bir_lowering=False)
    x_in = nc.dram_tensor("x_in", [D, HALO + T], F32, kind="ExternalInput")
    x_out = nc.dram_tensor("x_out", [D, HALO + T], F32, kind="ExternalOutput")
    cst_d = nc.dram_tensor("consts", [128, ncst], F32, kind="ExternalInput")
    with ExitStack() as st:
        P = Prog(nc, st)
        E = Env(nc, P, st)
        cst = E.sb("cst", (128, ncst), F32)
        psb = [E.ps("ps%d" % i) for i in range(8)]
        P.dma("sp", dict(out=cst.t[:, :], in_=cst_d[:, :]), writes=[cst.b])
        if stage in ("f0", "f1"):
            w_up_d = nc.dram_tensor("w_up", [128, 8, 2 * DFF], F32, kind="ExternalInput")
            w_dn_d = nc.dram_tensor("w_dn", [128, NJ, D], F32, kind="ExternalInput")
            stage_ffn(P, nc, x_in, x_out, w_up_d, w_dn_d, cst, co, stage, psb)
        elif stage == "m":
            xp = nc.dram_tensor("xp", [D, NPRE], F32, kind="ExternalInput")
            w_in_d = nc.dram_tensor("w_in", [128, 8, MW], F32, kind="ExternalInput")
            w_out_d = nc.dram_tensor("w_out", [128, 8, D], F32, kind="ExternalInput")
            stage_mlstm(P, nc, x_in, xp, x_out, w_in_d, w_out_d, cst, co, psb)
        elif stage == "fin":
            stage_final(P, nc, x_in, x_out, cst, co, psb)
        elif stage == "c":
            w_in_d = nc.dram_tensor("w_in", [128, 8, 2 * D], F32, kind="ExternalInput")
            w_out_d = nc.dram_tensor("w_out", [128, 8, D], F32, kind="ExternalInput")
            stage_conf(P, nc, x_in, x_out, w_in_d, w_out_d, cst, co, psb)
        P.finish()
    return nc


def stage_meta(stage, inputs):
    cp = build_consts(inputs, 0)
    return cp.off, cp.n


def stage_inputs(stage, inputs, core):
    cp = build_consts(inputs, core)
    m = {"consts": cp.build()}
    if stage in ("f0", "f1"):
        l = int(stage[1])
        m["w_up"] = wlayout(glu_interleave(np.asarray(inputs["f_w_up"][l], np.float32), DFF))
        m["w_dn"] = wlayout(np.asarray(inputs["f_w_down"][l], np.float32))
    elif stage == "m":
        m["w_in"] = wlayout(np.asarray(inputs["m_w_in"][0], np.float32))
        m["w_out"] = wlayout(np.asarray(inputs["m_w_out"][0], np.float32))
    elif stage == "c":
        m["w_in"] = wlayout(glu_interleave(np.asarray(inputs["c_w_in"][0], np.float32), D))
        m["w_out"] = wlayout(np.asarray(inputs["c_w_out"][0], np.float32))
    return m


def core_inputs(inputs, core):
    x = np.asarray(inputs["x"], np.float32)
    b, half = core // 2, core % 2
    t0 = half * T
    x_in = np.zeros((D, HALO + T), np.float32)
    x_in[:, HALO:] = x[b, t0:t0 + T].T
    xp = np.zeros((D, NPRE), np.float32)
    if half:
        x_in[:, :HALO] = x[b, t0 - HALO:t0].T
        xp[:] = x[b, 0:NPRE].T
    return x_in, xp


WEIGHTS = (("m_w_in", "m_w_in", 0, (128, 8, MW)), ("m_w_out", "m_w_out", 0, (128, 8, D)),
           ("f0_w_up", "f_w_up", 0, (128, 8, 2 * DFF)), ("f0_w_dn", "f_w_down", 0, (128, NJ, D)),
           ("c_w_in", "c_w_in", 0, (128, 8, 2 * D)), ("c_w_out", "c_w_out", 0, (128, 8, D)),
           ("f1_w_up", "f_w_up", 1, (128, 8, 2 * DFF)), ("f1_w_dn", "f_w_down", 1, (128, NJ, D)))


def build_fused(co, ncst):
    nc = bass.Bass("TRN2", target_bir_lowering=False)
    x_in = nc.dram_tensor("x_in", [D, HALO + T], F32, kind="ExternalInput")
    xp = nc.dram_tensor("xp", [D, NPRE], F32, kind="ExternalInput")
    cst_d = nc.dram_tensor("consts", [128, ncst], F32, kind="ExternalInput")
    wd = {n: nc.dram_tensor(n, list(shp), F32, kind="ExternalInput") for (n, _, _, shp) in WEIGHTS}
    sa = nc.dram_tensor("scr_a", [D, HALO + T], F32)
    sb = nc.dram_tensor("scr_b", [D, HALO + T], F32)
    y = nc.dram_tensor("y", [D, T], F32, kind="ExternalOutput")
    with ExitStack() as st:
        P = Prog(nc, st)
        E = Env(nc, P, st)
        cst = E.sb("cst", (128, ncst), F32)
        psb = [E.ps("ps%d" % i) for i in range(8)]
        P.dma("sp", dict(out=cst.t[:, :], in_=cst_d[:, :]), writes=[cst.b])
        P.dma("sp", dict(out=sa[:, 0:4], in_=x_in[:, 0:4]))
        stage_mlstm(P, nc, x_in, xp, sa, wd["m_w_in"], wd["m_w_out"], cst, co, psb)
        stage_ffn(P, nc, sa, sb, wd["f0_w_up"], wd["f0_w_dn"], cst, co, "f0", psb)
        stage_conf(P, nc, sb, sa, wd["c_w_in"], wd["c_w_out"], cst, co, psb)
        stage_ffn(P, nc, sa, sb, wd["f1_w_up"], wd["f1_w_dn"], cst, co, "f1", psb)
        stage_final(P, nc, sb, y, cst, co, psb)
        P.finish()
    return nc


FUSED = True


def kernel(**inputs):
    inputs = {k: np.asarray(v) for k, v in inputs.items()}
    x = inputs["x"]
    out = np.empty((BATCH, SEQ, D), np.float32)
    cps = [build_consts(inputs, c) for c in range(NCORES)]
    co, ncst = cps[0].off, cps[0].n
    consts = [cp.build() for cp in cps]
    cores = list(range(NCORES))
    xin_xp = [core_inputs(inputs, c) for c in cores]
    if FUSED:
        wl = {}
        for (n, src, l, _) in WEIGHTS:
            wsrc = np.asarray(inputs[src][l], np.float32)
            if src == "f_w_up":
                wsrc = glu_interleave(wsrc, DFF)
            elif src == "c_w_in":
                wsrc = glu_interleave(wsrc, D)
            wl[n] = wlayout(wsrc)
        nc = build_fused(co, ncst)
        in_maps = []
        for c in cores:
            m = {"x_in": xin_xp[c][0], "xp": xin_xp[c][1], "consts": consts[c]}
            m.update(wl)
            in_maps.append(m)
        res = run_bass_kernel_spmd(nc, in_maps, core_ids=cores)
        ys = [res.results[c]["y"] for c in cores]
    else:
        cur = [xin_xp[c][0] for c in cores]
        for stage in ("m", "f0", "c", "f1", "fin"):
            nc = build_stage_prog(stage, co, ncst)
            in_maps = []
            for c in cores:
                m = stage_inputs(stage, inputs, c)
                m["consts"] = consts[c]
                m["x_in"] = cur[c]
                if stage == "m":
                    m["xp"] = xin_xp[c][1]
                in_maps.append(m)
            res = run_bass_kernel_spmd(nc, in_maps, core_ids=cores)
            cur = [np.asarray(res.results[c]["x_out"]) for c in cores]
        ys = [cur[c][:, :T] for c in cores]
    for c in cores:
        b, half = c // 2, c % 2
        out[b, half * T:(half + 1) * T, :] = np.asarray(ys[c]).T
    return out
```
